# Optimizing a Trainium2 kernel written in Bass

```python
import jax
import jax.numpy as jnp
from jax import lax
import numpy as np

D_MODEL = 2048
BATCH = 4
SEQ = 4096
DEPTH = 2

N_HEADS = 16
HEAD_DIM = D_MODEL // N_HEADS
D_ATT = N_HEADS * HEAD_DIM
D_CONV = D_MODEL
D_MIX = D_ATT + D_CONV
D_IN = 4 * D_ATT + 3 * D_CONV
CONV_WIDTH = 31
DILATED_PATTERNS = ((128, 1), (512, 4), (2048, 16))
Q_BLOCK = 128
ROPE_THETA = 10000.0
EPS = 1e-6
SPLIT_POINTS = (D_ATT, 2 * D_ATT, 3 * D_ATT, 4 * D_ATT, 4 * D_ATT + D_CONV, 4 * D_ATT + 2 * D_CONV)

kernel_name = "hybrid_dilated_attn_conformer_conv"


def rms_norm(x, gain):
    xf = x.astype(jnp.float32)
    y = xf * lax.rsqrt(jnp.mean(xf * xf, axis=-1, keepdims=True) + EPS)
    return (y * gain.astype(jnp.float32)).astype(x.dtype)


def layer_norm(x, gain, bias):
    xf = x.astype(jnp.float32)
    xc = xf - jnp.mean(xf, axis=-1, keepdims=True)
    y = xc * lax.rsqrt(jnp.mean(xc * xc, axis=-1, keepdims=True) + EPS)
    return (y * gain.astype(jnp.float32) + bias.astype(jnp.float32)).astype(x.dtype)


def rope_tables(seq):
    inv_freq = 1.0 / (ROPE_THETA ** (jnp.arange(0, HEAD_DIM, 2, dtype=jnp.float32) / HEAD_DIM))
    ang = jnp.arange(seq, dtype=jnp.float32)[:, None] * inv_freq[None, :]
    return jnp.cos(ang), jnp.sin(ang)


def apply_rope(x, cos, sin):
    x1, x2 = jnp.split(x.astype(jnp.float32), 2, axis=-1)
    c = cos[None, :, None, :]
    s = sin[None, :, None, :]
    return jnp.concatenate([x1 * c - x2 * s, x2 * c + x1 * s], axis=-1).astype(x.dtype)


def dilated_branch(q, k, v, window, dilation):
    B, S, H, Dh = q.shape
    L = S // dilation
    n_back = window // dilation
    assert n_back <= Q_BLOCK
    Lp = -(-L // Q_BLOCK) * Q_BLOCK
    nb = Lp // Q_BLOCK

    def strided(t):
        t = t.reshape(B, L, dilation, H, Dh).transpose(0, 2, 1, 3, 4)
        t = jnp.pad(t, ((0, 0), (0, 0), (0, Lp - L), (0, 0), (0, 0)))
        return t.reshape(B, dilation, nb, Q_BLOCK, H, Dh)

    def with_prev(t):
        prev = jnp.pad(t[:, :, :-1], ((0, 0), (0, 0), (1, 0), (0, 0), (0, 0), (0, 0)))
        return jnp.concatenate([prev, t], axis=3)

    qb = strided(q)
    kb = with_prev(strided(k))
    vb = with_prev(strided(v))
    s = jnp.einsum("brnqhd,brnkhd->brnhqk", qb, kb, preferred_element_type=jnp.float32) * (Dh ** -0.5)
    blk = jnp.arange(nb)[:, None, None]
    qi = blk * Q_BLOCK + jnp.arange(Q_BLOCK)[None, :, None]
    ki = (blk - 1) * Q_BLOCK + jnp.arange(2 * Q_BLOCK)[None, None, :]
    dist = qi - ki
    mask = (dist >= 0) & (dist <= n_back) & (ki >= 0)
    s = jnp.where(mask[None, None, :, None], s, -jnp.inf)
    m = jnp.max(s, axis=-1)
    p = jnp.exp(s - m[..., None])
    l = jnp.sum(p, axis=-1)
    o = jnp.einsum("brnhqk,brnkhd->brnqhd", p.astype(v.dtype), vb, preferred_element_type=jnp.float32)

    def unstride(t):
        tail = t.shape[5:]
        t = t.reshape((B, dilation, Lp, H) + tail)[:, :, :L]
        t = jnp.moveaxis(t, 1, 2)
        return t.reshape((B, S, H) + tail)

    return unstride(o), unstride(jnp.moveaxis(m, 3, 4)), unstride(jnp.moveaxis(l, 3, 4))


def dilated_mixture(q, k, v):
    branches = [dilated_branch(q, k, v, w, d) for (w, d) in DILATED_PATTERNS]
    ms = jnp.stack([b[1] for b in branches])
    ls = jnp.stack([b[2] for b in branches])
    wts = jnp.exp(ms - jnp.max(ms, axis=0))
    num = wts[0][..., None] * branches[0][0]
    for i in range(1, len(branches)):
        num = num + wts[i][..., None] * branches[i][0]
    den = jnp.sum(wts * ls, axis=0)
    return (num / den[..., None]).astype(q.dtype)


def conformer_conv(glu_a, glu_b, dw_kernel, dw_bias, ln_g, ln_b, w_pw):
    u = glu_a * jax.nn.sigmoid(glu_b)
    u = jnp.pad(u, ((0, 0), (CONV_WIDTH - 1, 0), (0, 0)))
    y = lax.conv_general_dilated(u, dw_kernel.astype(u.dtype)[:, None, :], window_strides=(1,),
                                 padding="VALID", dimension_numbers=("NWC", "WIO", "NWC"),
                                 feature_group_count=D_CONV) + dw_bias
    y = jax.nn.silu(layer_norm(y, ln_g, ln_b))
    return jnp.einsum("bsc,ce->bse", y, w_pw)


def setup_inputs(seed: int = 0) -> dict:
    key = jax.random.key(seed)
    ks = jax.random.split(key, 13)
    nrm = jax.random.normal
    f32 = jnp.float32
    return {
        "x": nrm(ks[0], (BATCH, SEQ, D_MODEL), f32),
        "norm_g": 1.0 + 0.02 * nrm(ks[1], (DEPTH, D_MODEL), f32),
        "w_in": nrm(ks[2], (DEPTH, D_MODEL, D_IN), f32) * D_MODEL ** -0.5,
        "q_norm_g": 1.0 + 0.02 * nrm(ks[3], (DEPTH, HEAD_DIM), f32),
        "k_norm_g": 1.0 + 0.02 * nrm(ks[4], (DEPTH, HEAD_DIM), f32),
        "dw_kernel": nrm(ks[5], (DEPTH, CONV_WIDTH, D_CONV), f32) * CONV_WIDTH ** -0.5,
        "dw_bias": 0.02 * nrm(ks[6], (DEPTH, D_CONV), f32),
        "conv_ln_g": 1.0 + 0.02 * nrm(ks[7], (DEPTH, D_CONV), f32),
        "conv_ln_b": 0.02 * nrm(ks[8], (DEPTH, D_CONV), f32),
        "w_pw": nrm(ks[9], (DEPTH, D_CONV, D_CONV), f32) * D_CONV ** -0.5,
        "att_out_g": 1.0 + 0.02 * nrm(ks[10], (DEPTH, D_ATT), f32),
        "conv_out_g": 1.0 + 0.02 * nrm(ks[11], (DEPTH, D_CONV), f32),
        "w_out": nrm(ks[12], (DEPTH, D_MIX, D_MODEL), f32) * D_MIX ** -0.5,
    }


def reference(x, norm_g, w_in, q_norm_g, k_norm_g, dw_kernel, dw_bias, conv_ln_g, conv_ln_b, w_pw,
              att_out_g, conv_out_g, w_out):
    B, S, _ = x.shape
    cos, sin = rope_tables(S)
    for layer in range(DEPTH):
        h = rms_norm(x, norm_g[layer])
        proj = jnp.einsum("bsd,de->bse", h, w_in[layer])
        q, k, v, g_att, glu_a, glu_b, g_conv = jnp.split(proj, SPLIT_POINTS, axis=-1)
        q = apply_rope(rms_norm(q.reshape(B, S, N_HEADS, HEAD_DIM), q_norm_g[layer]), cos, sin)
        k = apply_rope(rms_norm(k.reshape(B, S, N_HEADS, HEAD_DIM), k_norm_g[layer]), cos, sin)
        v = v.reshape(B, S, N_HEADS, HEAD_DIM)
        att = dilated_mixture(q, k, v).reshape(B, S, D_ATT)
        att_y = rms_norm(att, att_out_g[layer]) * jax.nn.silu(g_att)
        conv = conformer_conv(glu_a, glu_b, dw_kernel[layer], dw_bias[layer], conv_ln_g[layer],
                              conv_ln_b[layer], w_pw[layer])
        conv_y = rms_norm(conv, conv_out_g[layer]) * jax.nn.silu(g_conv)
        y = jnp.einsum("bse,ed->bsd", jnp.concatenate([att_y, conv_y], axis=-1), w_out[layer])
        x = x + y
    return x
```

```python
import math
import numpy as np
import concourse.bass as bass
import concourse.mybir as mybir
from concourse.bass_utils import run_bass_kernel_spmd

F32 = mybir.dt.float32
BF16 = mybir.dt.bfloat16
I32 = mybir.dt.int32
AF = mybir.ActivationFunctionType
ALU = mybir.AluOpType

D = 2048
DIN = 14336
TS = 2048
NCH = 16
HD = 128
NH = 16
CW = 31
EPS = 1e-6
DEPTH = 2
NPV = 6 * 16 + 2 + 16 * CW
GRAN = 2048
PIPE = True
RGS = [[0, 1], [2, 3], [4, 5], [6, 7]]


class Sched:
    NDS = 12

    def __init__(self, nc):
        self.nc = nc
        self.ops = []
        self.esem = {e: nc.alloc_semaphore("es_" + e) for e in ("pe", "act", "dve", "pool")}
        self.dsem = {q: [nc.alloc_semaphore("ds_%s%d" % (q, i)) for i in range(self.NDS)]
                     for q in ("sp", "pool", "act")}
        self.ccsem = nc.alloc_semaphore("ccsem")
        self.cccount = 0

    def op(self, eng, fn, r=(), w=()):
        self.ops.append(dict(eng=eng, fn=fn, r=tuple(r), w=tuple(w), dma=False))

    def dma(self, q, fn, r=(), w=()):
        self.ops.append(dict(eng=q, fn=fn, r=tuple(r), w=tuple(w), dma=True))

    def cc(self, fn, r=(), w=()):
        self.ops.append(dict(eng="pool", fn=fn, r=tuple(r), w=tuple(w), dma=True, cc=True))

    def analyse(self):
        ops = self.ops
        last_w = {}
        readers = {}
        for i, o in enumerate(ops):
            pr = tuple(k for k in o["r"] if k[0] in ("PS", "PX"))
            if pr:
                o["w"] = tuple(o["w"]) + pr
                o["r"] = tuple(k for k in o["r"] if k[0] not in ("PS", "PX"))
            deps = set()
            for k in o["r"]:
                if k in last_w:
                    deps.add(last_w[k])
            for k in o["w"]:
                if k in last_w:
                    deps.add(last_w[k])
                deps.update(readers.get(k, ()))
            deps.discard(i)
            o["deps"] = deps
            for k in o["r"]:
                readers.setdefault(k, []).append(i)
            for k in o["w"]:
                last_w[k] = i
                readers[k] = []
        for o in ops:
            o["ms"] = False
        for o in ops:
            for d in o["deps"]:
                p = ops[d]
                if p["dma"]:
                    continue
                if p["eng"] == "pe" and o["eng"] == "pe" and not o["dma"]:
                    continue
                p["ms"] = True
        cnt = {e: 0 for e in self.esem}
        dstate = {q: [0] * self.NDS for q in self.dsem}
        drr = {q: 0 for q in self.dsem}
        for o in ops:
            if o.get("cc"):
                o["dsem"] = self.ccsem
                o["guard"] = self.cccount
                self.cccount += 1
                o["target"] = self.cccount
            elif o["dma"]:
                q = o["eng"]
                j = drr[q]
                drr[q] = (j + 1) % self.NDS
                o["dsem"] = self.dsem[q][j]
                o["guard"] = dstate[q][j]
                dstate[q][j] += 16
                o["target"] = dstate[q][j]
            elif o["ms"]:
                cnt[o["eng"]] += 1
                o["count"] = cnt[o["eng"]]
        self.final_dma = {q: list(dstate[q]) for q in dstate}
        waited = {}
        for o in ops:
            e = o["eng"]
            wl = {}
            if o["dma"] and o["guard"] > 0:
                wl[id(o["dsem"])] = (o["dsem"], o["guard"])
            for d in sorted(o["deps"]):
                p = ops[d]
                if p["dma"]:
                    s, v = p["dsem"], p["target"]
                else:
                    if p["eng"] == "pe" and e == "pe" and not o["dma"]:
                        continue
                    s, v = self.esem[p["eng"]], p["count"]
                if id(s) not in wl or wl[id(s)][1] < v:
                    wl[id(s)] = (s, v)
            out = []
            wd = waited.setdefault(e, {})
            for sid, (s, v) in wl.items():
                if wd.get(sid, 0) >= v:
                    continue
                wd[sid] = v
                out.append((s, v))
            o["waits"] = out

    def emit(self):
        self.analyse()
        nc = self.nc
        per = {e: [] for e in ("sp", "act", "pe", "dve", "pool")}
        for o in self.ops:
            per[o["eng"]].append(o)

        def make_body(ename):
            def body(e):
                for o in per[ename]:
                    for (s, v) in o["waits"]:
                        e.wait_ge(s, v)
                    inst = o["fn"](e)
                    if o.get("cc"):
                        inst.then_inc(o["dsem"], 1)
                    elif o["dma"]:
                        inst.then_inc(o["dsem"], 16)
                    elif o["ms"]:
                        inst.then_inc(self.esem[ename], 1)
                if ename == "sp":
                    for q in self.dsem:
                        for j, s in enumerate(self.dsem[q]):
                            if self.final_dma[q][j] > 0:
                                e.wait_ge(s, self.final_dma[q][j])
                    if self.cccount > 0:
                        e.wait_ge(self.ccsem, self.cccount)
            return body

        with nc.Block() as block:
            block.sync(make_body("sp"))
            block.scalar(make_body("act"))
            block.tensor(make_body("pe"))
            block.vector(make_body("dve"))
            block.gpsimd(make_body("pool"))


class Region:
    def __init__(self, nc, name, nbytes):
        self.t = nc.alloc_sbuf_tensor(name, [128, nbytes // 4], F32)
        self.name = name
        self.nbytes = nbytes

    def view(self, off, n, dt):
        sz = 2 if dt == BF16 else 4
        assert off % 4 == 0 and (n * sz) % 4 == 0 and off + n * sz <= self.nbytes, (off, n, self.nbytes)
        ap = self.t[:, off // 4:(off + n * sz) // 4]
        if dt != F32:
            ap = ap.bitcast(dt)
        keys = [(self.name, g) for g in range(off // GRAN, (off + n * sz - 1) // GRAN + 1)]
        return ap, keys


def build(nseg, debug=False, stop_after=None, pair=False):
    nc = bass.Bass("TRN2", target_bir_lowering=False)
    S = Sched(nc)
    NT = nseg * TS
    okind = "ExternalOutput" if debug else "Internal"

    x_in = nc.dram_tensor("x", [NT, D], F32, kind="ExternalInput").ap()
    w_in = nc.dram_tensor("w_in", [DEPTH, D, DIN], F32, kind="ExternalInput").ap()
    w_pw = nc.dram_tensor("w_pw", [DEPTH, D, D], F32, kind="ExternalInput").ap()
    w_out = nc.dram_tensor("w_out", [DEPTH, 2 * D, D], F32, kind="ExternalInput").ap()
    pv_in = nc.dram_tensor("pv", [DEPTH, 128, NPV], F32, kind="ExternalInput").ap()
    cst_in = nc.dram_tensor("cst", [128, 512], F32, kind="ExternalInput").ap()
    cinfo_in = nc.dram_tensor("cinfo", [128, 4], F32, kind="ExternalInput").ap()
    out = nc.dram_tensor("out", [NT, D], F32, kind="ExternalOutput").ap()

    X1 = nc.dram_tensor("X1", [NT, D], F32, kind=okind).ap()
    QT = nc.dram_tensor("QT", [NH, 128, TS], BF16, kind=okind).ap()
    if pair:
        assert nseg == 1
        XBK = [nc.dram_tensor("XBK%d" % i, [512, TS], BF16).ap() for i in range(4)]
        XGK = [nc.dram_tensor("XGK%d" % i, [1024, TS], BF16).ap() for i in range(4)]
        XBV = [nc.dram_tensor("XBV%d" % i, [512, TS], BF16).ap() for i in range(4)]
        XGV = [nc.dram_tensor("XGV%d" % i, [1024, TS], BF16).ap() for i in range(4)]
        XBH = nc.dram_tensor("XBH", [D, 32], BF16).ap()
        XGH = nc.dram_tensor("XGH", [2 * D, 32], BF16).ap()
        KT = V = None
    else:
        KT = nc.dram_tensor("KT", [NH, 128, NT], BF16, kind=okind).ap()
        V = nc.dram_tensor("V", [NH, 128, NT], BF16, kind=okind).ap()
    SGA = nc.dram_tensor("SGA", [NH, 128, TS], BF16, kind="Internal").ap()
    SGC = nc.dram_tensor("SGC", [NCH, 128, TS], BF16, kind="Internal").ap()
    Y = nc.dram_tensor("Y", [NCH, 128, TS], BF16, kind=okind).ap()
    ATT = nc.dram_tensor("ATT", [NH, 128, TS], BF16, kind=okind).ap()
    CV = nc.dram_tensor("CV", [NCH, 128, TS], BF16, kind="Internal").ap()
    AY = nc.dram_tensor("AY", [2 * NCH, 128, TS], BF16, kind=okind).ap()
    U = nc.dram_tensor("U", [NCH, 128, TS], BF16, kind="Internal").ap()
    UH = nc.dram_tensor("UH", [NCH, 128, 32], BF16, kind="Internal").ap()

    B0 = Region(nc, "B0", 65536)
    WR = Region(nc, "WR", 65536)
    RP = Region(nc, "RP", 16384)
    R4 = Region(nc, "R4", 24576)
    R8 = Region(nc, "R8", 16384)
    MS = Region(nc, "MS", 8192)
    CVR = Region(nc, "CVR", 3 * 4160)
    PS = nc.alloc_psum_tensor("PSA", [128, 2048], F32)
    PX = nc.alloc_psum_tensor("PSX", [128, 2048], F32)

    def psv(t, name, off, n, dt=F32):
        ap = t[:, off:off + n]
        if dt == BF16:
            ap = ap.bitcast(BF16)
        keys = [(name, b) for b in range(off // 512, (off + n - 1) // 512 + 1)]
        return ap, keys

    class Ring:
        def __init__(self, reg, size, count, base=0):
            self.reg, self.size, self.count, self.i, self.base = reg, size, count, 0, base

        def get(self, n, dt):
            off = self.base + self.i * self.size
            self.i = (self.i + 1) % self.count
            return self.reg.view(off, n, dt)

    r4 = Ring(R4, 4096, 6)
    r8 = Ring(R8, 8192, 2)
    r2k = Ring(R8, 2048, 8)

    ident, k_ident = MS.view(0, 128, BF16)
    ones, k_ones = MS.view(256, 128, BF16)
    swp, k_swp = MS.view(512, 128, BF16)
    mask2, k_mask = MS.view(768, 256, BF16)
    pvs = [MS.view(2048 + l * 2560, NPV, F32) for l in range(DEPTH)]
    sc_off = 2048 + 2 * 2560
    sgn2, k_sgn2 = MS.view(sc_off, 1, F32)
    invf, k_invf = MS.view(sc_off + 4, 1, F32)
    gq2 = [MS.view(sc_off + 8 + 8 * l, 2, F32) for l in range(DEPTH)]
    ssq_t = [MS.view(sc_off + 64 + 8 * i, 1, F32) for i in range(2)]
    rstd_t = [MS.view(sc_off + 128 + 8 * i, 1, F32) for i in range(2)]
    negpi, k_negpi = MS.view(sc_off + 192, 1, F32)
    cinfo, k_cinfo = MS.view(sc_off + 256, 4, F32)
    rc16, k_rc16 = MS.view(sc_off + 448, 16, F32)
    one_f, k_onef = MS.view(sc_off + 512, 1, F32)
    cosT, k_cos = RP.view(0, TS, F32)
    sinT, k_sin = RP.view(8192, TS, F32)

    cst_f, k_cstf = r4.get(512, F32)
    S.dma("sp", lambda e: e.dma_start(out=cst_f, in_=cst_in[:, :]), w=k_cstf)
    S.op("dve", lambda e: e.tensor_copy(out=ident, in_=cst_f[:, 0:128]), r=k_cstf, w=k_ident)
    S.op("dve", lambda e: e.tensor_copy(out=swp, in_=cst_f[:, 128:256]), r=k_cstf, w=k_swp)
    S.op("dve", lambda e: e.tensor_copy(out=mask2, in_=cst_f[:, 256:512]), r=k_cstf, w=k_mask)
    S.op("dve", lambda e: e.memset(ones, 1.0), w=k_ones)
    S.op("dve", lambda e: e.memset(one_f, 1.0), w=k_onef)
    S.dma("sp", lambda e: e.dma_start(out=cinfo, in_=cinfo_in[:, :]), w=k_cinfo)
    S.op("dve", lambda e: e.memset(negpi, EPS), w=k_negpi)
    for l in range(DEPTH):
        S.dma("sp", lambda e, l=l: e.dma_start(out=pvs[l][0], in_=pv_in[l]), w=pvs[l][1])
        S.op("dve", lambda e, l=l: e.tensor_scalar(out=gq2[l][0], in0=pvs[l][0][:, 96:98], scalar1=1.0,
                                                   scalar2=None, op0=ALU.mult), r=pvs[l][1], w=gq2[l][1])
    pidx_i, k_pi = r4.get(1, I32)
    pidx_f, k_pf = r4.get(1, F32)
    S.op("pool", lambda e: e.iota(pidx_i, pattern=[[0, 1]], base=0, channel_multiplier=1), w=k_pi)
    S.op("dve", lambda e: e.tensor_copy(out=pidx_f, in_=pidx_i), r=k_pi, w=k_pf)
    S.op("dve", lambda e: e.tensor_scalar(out=sgn2, in0=pidx_f, scalar1=64.0, scalar2=2.0, op0=ALU.is_ge, op1=ALU.mult),
         r=k_pf, w=k_sgn2)
    S.op("dve", lambda e: e.tensor_scalar(out=sgn2, in0=sgn2, scalar1=-1.0, scalar2=None, op0=ALU.add), r=k_sgn2, w=k_sgn2)
    pm, k_pm = r4.get(1, F32)
    S.op("dve", lambda e: e.tensor_scalar(out=pm, in0=pidx_f, scalar1=64.0, scalar2=-64.0, op0=ALU.is_ge, op1=ALU.mult),
         r=k_pf, w=k_pm)
    S.op("dve", lambda e: e.tensor_tensor(out=pm, in0=pm, in1=pidx_f, op=ALU.add), r=k_pf + k_pm, w=k_pm)
    S.op("act", lambda e: e.activation(out=invf, in_=pm, func=AF.Exp, scale=-2.0 * math.log(10000.0) / 128.0),
         r=k_pm, w=k_invf)

    def pvcol(l, grp, c):
        return pvs[l][0][:, grp * 16 + c: grp * 16 + c + 1]
    G_NORM, G_ATTO, G_CONVO, G_DWB, G_LNG, G_LNB = range(6)

    def rsqrt_tile(dst, kdst, src, ksrc, scale):
        S.op("act", lambda e: e.activation(out=dst, in_=src, func=AF.Ln, bias=negpi, scale=scale), r=ksrc + k_negpi, w=kdst)
        S.op("act", lambda e: e.activation(out=dst, in_=dst, func=AF.Exp, scale=-0.5), r=kdst, w=kdst)

    def dwk(l, c, j):
        o = 98 + c * CW + j
        return pvs[l][0][:, o:o + 1]

    def rope_tables(seg):
        C1 = 6.28125
        C2 = 2 * math.pi - C1
        pos_i, k1 = B0.view(0, TS, I32)
        S.op("pool", lambda e: e.iota(pos_i, pattern=[[1, TS]], base=seg * TS, channel_multiplier=0), w=k1)
        ang, k2 = B0.view(8192, TS, F32)
        S.op("dve", lambda e: e.tensor_copy(out=ang, in_=pos_i), r=k1, w=k2)
        if pair:
            S.op("dve", lambda e: e.tensor_scalar(out=ang, in0=ang, scalar1=cinfo[:, 0:1], scalar2=None, op0=ALU.add),
                 r=k2 + k_cinfo, w=k2)
        S.op("dve", lambda e: e.tensor_scalar(out=ang, in0=ang, scalar1=invf, scalar2=None, op0=ALU.mult),
             r=k2 + k_invf, w=k2)
        for which, phase in ((0, 0.0), (1, math.pi / 2)):
            dst, kd = (sinT, k_sin) if which == 0 else (cosT, k_cos)
            base = 16384 + which * 24576
            angp, ka = B0.view(base, TS, F32)
            ki, kki = B0.view(base + 8192, TS, I32)
            kf, kkf = B0.view(base + 16384, TS, F32)
            S.op("dve", lambda e, angp=angp, phase=phase: e.tensor_scalar(out=angp, in0=ang, scalar1=phase, scalar2=None,
                                                                          op0=ALU.add), r=k2, w=ka)
            S.op("dve", lambda e, angp=angp, kf=kf: e.tensor_scalar(out=kf, in0=angp, scalar1=1.0 / (2 * math.pi), scalar2=None,
                                                                    op0=ALU.mult), r=ka, w=kkf)
            S.op("dve", lambda e, ki=ki, kf=kf: e.tensor_copy(out=ki, in_=kf), r=kkf, w=kki)
            S.op("dve", lambda e, ki=ki, kf=kf: e.tensor_copy(out=kf, in_=ki), r=kki, w=kkf)
            S.op("dve", lambda e, angp=angp, kf=kf: e.scalar_tensor_tensor(out=angp, in0=kf, scalar=-C1, in1=angp, op0=ALU.mult,
                                                                           op1=ALU.add), r=ka + kkf, w=ka)
            S.op("dve", lambda e, angp=angp, kf=kf: e.scalar_tensor_tensor(out=angp, in0=kf, scalar=-C2, in1=angp, op0=ALU.mult,
                                                                           op1=ALU.add), r=ka + kkf, w=ka)
            S.op("dve", lambda e, angp=angp, kf=kf: e.tensor_scalar(out=kf, in0=angp, scalar1=math.pi, scalar2=-2 * math.pi,
                                                                    op0=ALU.is_gt, op1=ALU.mult), r=ka, w=kkf)
            S.op("dve", lambda e, angp=angp, kf=kf: e.tensor_tensor(out=angp, in0=angp, in1=kf, op=ALU.add), r=ka + kkf, w=ka)
            S.op("dve", lambda e, angp=angp, kf=kf: e.tensor_scalar(out=kf, in0=angp, scalar1=-math.pi, scalar2=2 * math.pi,
                                                                    op0=ALU.is_lt, op1=ALU.mult), r=ka, w=kkf)
            S.op("dve", lambda e, angp=angp, kf=kf: e.tensor_tensor(out=angp, in0=angp, in1=kf, op=ALU.add), r=ka + kkf, w=ka)
            S.op("act", lambda e, angp=angp, dst=dst: e.activation(out=dst, in_=angp, func=AF.Sin), r=ka, w=kd)
        S.op("dve", lambda e: e.tensor_scalar(out=sinT, in0=sinT, scalar1=sgn2, scalar2=None, op0=ALU.mult),
             r=k_sin + k_sgn2, w=k_sin)

    def hT(c, t0, n):
        ap, keys = B0.view(c * 4096 + t0 * 2, n, BF16)
        return ap, keys

    def build_gb(l):
        gb, kgb = CVR.view(0, 16 * 128, BF16)
        for c in range(16):
            S.op("pool", lambda e, c=c: e.tensor_scalar(out=gb[:, c * 128:(c + 1) * 128], in0=ones, scalar1=pvcol(l, G_NORM, c),
                                                        scalar2=None, op0=ALU.mult), r=k_ones + pvs[l][1], w=kgb)

    def phase1(l, xsrc, tok0):
        ssq16, kss = MS.view(sc_off + 320, 16, F32)
        rstd16, krs = MS.view(sc_off + 384, 16, F32)
        S.op("dve", lambda e: e.memset(ssq16, 0.0), w=kss)
        gb, kgb = CVR.view(0, 16 * 128, BF16)
        hall, _ = B0.view(0, 16 * TS, BF16)
        h3 = hall.rearrange("p (c t) -> p c t", c=16)

        def load(tt):
            xt, kx = r8.get(D, F32)
            S.dma("sp", lambda e: e.dma_start(out=xt, in_=xsrc[tok0 + tt * 128: tok0 + (tt + 1) * 128, :]), w=kx)
            return xt, kx
        nxt = load(0)
        for tt in range(16):
            xt, kx = nxt
            junk, kj = r4.get(D, BF16)
            S.op("act", lambda e, xt=xt, junk=junk, tt=tt: e.activation(out=junk, in_=xt, func=AF.Square, accum_out=ssq16[:, tt:tt + 1]),
                 r=kx + kss, w=kj + [("ssq", tt)])
            S.op("act", lambda e, tt=tt: e.activation(out=rstd16[:, tt:tt + 1], in_=ssq16[:, tt:tt + 1], func=AF.Ln, bias=negpi, scale=1.0 / D),
                 r=[("ssq", tt)] + k_negpi, w=[("rstd", tt)])
            S.op("act", lambda e, tt=tt: e.activation(out=rstd16[:, tt:tt + 1], in_=rstd16[:, tt:tt + 1], func=AF.Exp, scale=-0.5),
                 r=[("rstd", tt)], w=[("rstd", tt)])
            xn, kxn = r4.get(D, BF16)
            S.op("act", lambda e, xt=xt, xn=xn, tt=tt: e.activation(out=xn, in_=xt, func=AF.Copy, scale=rstd16[:, tt:tt + 1]),
                 r=kx + [("rstd", tt)], w=kxn)
            if tt + 1 < 16:
                nxt = load(tt + 1)
            for g in range(4):
                pt, kpt = psv(PX, "PX", ((tt * 4 + g) % 4) * 512, 256, BF16)

                def tr(e, xn=xn, pt=pt, g=g):
                    for k in range(4):
                        c = g * 4 + k
                        ins = e.transpose(out=pt[:, k * 128:(k + 1) * 128], in_=xn[:, c * 128:(c + 1) * 128], identity=ident)
                    return ins
                S.op("pe", tr, r=kxn + k_ident, w=kpt)
                kh = []
                for k in range(4):
                    kh += hT(g * 4 + k, tt * 128, 128)[1]
                S.op("dve", lambda e, pt=pt, g=g, tt=tt: e.tensor_tensor(
                    out=h3[:, g * 4:(g + 1) * 4, tt * 128:(tt + 1) * 128], in0=pt.rearrange("p (c t) -> p c t", c=4),
                    in1=gb[:, g * 512:(g + 1) * 512].rearrange("p (c t) -> p c t", c=4), op=ALU.mult), r=kpt + kgb, w=kh)

    wslot = [0, 0]

    def load_w(src, col0, nchunks=16, ncols=512, slots=1):
        if slots == 1:
            i = wslot[0]
            wslot[0] = (i + 1) % 3
        else:
            i = wslot[1]
            wslot[1] = (i + 2) % 4
        wb, kw = WR.view(i * 16384, nchunks * ncols, BF16)
        wb3 = wb.rearrange("p (c n) -> p c n", c=nchunks)
        sv = src.rearrange("(c p) n -> p c n", p=128)
        for c0 in range(0, nchunks, 4):
            S.dma("pool", lambda e, c0=c0: e.dma_start(out=wb3[:, c0:c0 + 4, :], in_=sv[:, c0:c0 + 4, col0:col0 + ncols]),
                  w=kw)
        return wb3, kw

    acc_i = [0]
    acc6 = [0]

    def ws_unit(wb3, kw, j, th, rhs_fn, split=False):
        a = acc_i[0]
        acc_i[0] ^= 1
        pa, kpa = psv(PS, "PS", a * 1024, 1024)
        rk = []
        for c in range(16):
            rk += rhs_fn(c, th * 1024, 1024)[1]

        def mm(e):
            for c in range(16):
                for tb in range(2):
                    rv = rhs_fn(c, th * 1024 + tb * 512, 512)[0]
                    ins = e.matmul(pa[:, tb * 512:(tb + 1) * 512], lhsT=wb3[:, c, j * 128:(j + 1) * 128], rhs=rv,
                                   start=(c == 0), stop=(c == 15))
            return ins
        if not split:
            S.op("pe", mm, r=kw + rk, w=kpa)
        else:
            for c in range(16):
                def mmc(e, c=c):
                    for tb in range(2):
                        rv = rhs_fn(c, th * 1024 + tb * 512, 512)[0]
                        ins = e.matmul(pa[:, tb * 512:(tb + 1) * 512], lhsT=wb3[:, c, j * 128:(j + 1) * 128], rhs=rv,
                                       start=(c == 0), stop=(c == 15))
                    return ins
                S.op("pe", mmc, r=kw + rhs_fn(c, th * 1024, 1024)[1], w=kpa)
        return pa, kpa

    def qk_part1(pa, kpa, l, which):
        sqb, ksq = r2k.get(1024, BF16)
        S.op("act", lambda e: e.activation(out=sqb, in_=pa, func=AF.Square), r=kpa, w=ksq)
        qg, kqg = r2k.get(1024, BF16)
        S.op("act", lambda e: e.activation(out=qg, in_=pa, func=AF.Copy, scale=gq2[l][0][:, which:which + 1]),
             r=kpa + gq2[l][1], w=kqg)
        return sqb, ksq, qg, kqg

    def qk_epilogue(l, sqb, ksq, qg, kqg, which, h, th, tokg0):
        px1, kpx1 = psv(PX, "PX", 0, 1024)
        px2, kpx2 = psv(PX, "PX", 1024, 1024)

        def mm1(e):
            for tb in range(2):
                ins = e.matmul(px1[:, tb * 512:(tb + 1) * 512], lhsT=ones, rhs=sqb[:, tb * 512:(tb + 1) * 512], start=True, stop=True)
            return ins
        S.op("pe", mm1, r=ksq + k_ones, w=kpx1)

        def mm2(e):
            for tb in range(2):
                ins = e.matmul(px2[:, tb * 512:(tb + 1) * 512], lhsT=swp, rhs=qg[:, tb * 512:(tb + 1) * 512], start=True, stop=True)
            return ins
        S.op("pe", mm2, r=kqg + k_swp, w=kpx2)
        rs, krs = r4.get(1024, F32)
        rsqrt_tile(rs, krs, px1, kpx1, 1.0 / HD)
        t1, kt1 = r4.get(1024, F32)
        S.op("dve", lambda e: e.tensor_tensor(out=t1, in0=qg, in1=cosT[:, th * 1024:(th + 1) * 1024], op=ALU.mult),
             r=kqg + k_cos, w=kt1)
        t2, kt2 = r4.get(1024, F32)
        S.op("dve", lambda e: e.tensor_tensor(out=t2, in0=px2, in1=sinT[:, th * 1024:(th + 1) * 1024], op=ALU.mult),
             r=kpx2 + k_sin, w=kt2)
        S.op("dve", lambda e: e.tensor_tensor(out=t1, in0=t1, in1=t2, op=ALU.add), r=kt1 + kt2, w=kt1)
        qo, kqo = r4.get(1024, BF16)
        S.op("dve", lambda e: e.tensor_tensor(out=qo, in0=t1, in1=rs, op=ALU.mult), r=kt1 + krs, w=kqo)
        if which == 0:
            S.dma("sp", lambda e: e.dma_start(out=QT[h][:, th * 1024:(th + 1) * 1024], in_=qo), r=kqo, w=[("QT", h)])
        else:
            if pair:
                kdst = XBK[h // 4][(h % 4) * 128:(h % 4 + 1) * 128, th * 1024:(th + 1) * 1024]
            else:
                kdst = KT[h][:, tokg0 + th * 1024: tokg0 + (th + 1) * 1024]
            S.dma("sp", lambda e: e.dma_start(out=kdst, in_=qo), r=kqo, w=[("KT", h)])

    def gate_epilogue(pa, kpa, dst, idx, th):
        sg, ksg = r4.get(1024, BF16)
        S.op("act", lambda e: e.activation(out=sg, in_=pa, func=AF.Silu), r=kpa, w=ksg)
        S.dma("sp", lambda e: e.dma_start(out=dst[idx][:, th * 1024:(th + 1) * 1024], in_=sg), r=ksg,
              w=[(dst.tensor.name, idx)])

    def phase2(l, seg):
        tokg0 = seg * TS
        wl = w_in[l]

        def rhs_h(c, t0, n):
            return hT(c, t0, n)

        blocks = []
        for i in range(4):
            blocks += [("q", i), ("k", i), ("v", i)]
        blocks += [("ga", i) for i in range(4)]
        for i in range(4):
            blocks += [("gb", i), ("ua", i)]
        blocks += [("gc", i) for i in range(4)]
        colbase = {"q": 0, "k": 2048, "v": 4096, "ga": 6144, "ua": 8192, "gb": 10240, "gc": 12288}
        loaded = {}

        def prefetch(bi):
            if bi < len(blocks) and bi not in loaded:
                kind, i = blocks[bi]
                loaded[bi] = load_w(wl, colbase[kind] + i * 512)

        units = []
        for bi, (kind, i) in enumerate(blocks):
            first = [True]

            def pre(bi=bi, first=first):
                if first[0]:
                    first[0] = False
                    prefetch(bi)
                    prefetch(bi + 1)
                    prefetch(bi + 2)
                return loaded[bi]
            for j in range(4):
                for th in range(2):
                    def mmw(pre=pre, j=j, th=th, kind=kind):
                        wb3, kw = pre()
                        pa, kpa = ws_unit(wb3, kw, j, th, rhs_h)
                        if kind in ("q", "k"):
                            return qk_part1(pa, kpa, l, 0 if kind == "q" else 1)
                        return pa, kpa
                    if kind in ("q", "k"):
                        def ep(st, kind=kind, h=i * 4 + j, th=th, i=i, j=j):
                            qk_epilogue(l, st[0], st[1], st[2], st[3], 0 if kind == "q" else 1, h, th, tokg0)
                            if pair and kind == "k" and j == 3 and th == 1:
                                S.cc(lambda e: e.collective_compute("AllGather", ALU.bypass, replica_groups=RGS, ins=[XBK[i][:, :]],
                                                                    outs=[XGK[i][:, :]]),
                                     r=[("KT", hh) for hh in range(4 * i, 4 * i + 4)], w=[("XGK", i)])
                    elif kind == "v":
                        def ep(st, h=i * 4 + j, th=th, i=i, j=j):
                            vt, kvt = r4.get(1024, BF16)
                            S.op("act", lambda e: e.activation(out=vt, in_=st[0], func=AF.Copy), r=st[1], w=kvt)
                            if pair:
                                vdst = XBV[i][j * 128:(j + 1) * 128, th * 1024:(th + 1) * 1024]
                            else:
                                vdst = V[h][:, tokg0 + th * 1024: tokg0 + (th + 1) * 1024]
                            S.dma("sp", lambda e: e.dma_start(out=vdst, in_=vt), r=kvt, w=[("V", h)])
                            if pair and j == 3 and th == 1:
                                S.cc(lambda e: e.collective_compute("AllGather", ALU.bypass, replica_groups=RGS, ins=[XBV[i][:, :]],
                                                                    outs=[XGV[i][:, :]]),
                                     r=[("V", hh) for hh in range(4 * i, 4 * i + 4)], w=[("XGV", i)])
                    elif kind in ("ga", "gc"):
                        def ep(st, kind=kind, idx=i * 4 + j, th=th):
                            gate_epilogue(st[0], st[1], SGA if kind == "ga" else SGC, idx, th)
                    elif kind == "gb":
                        def ep(st, j=j, th=th):
                            sb, ksb = WR.view(49152 + (j * 2 + th) * 2048, 1024, BF16)
                            S.op("act", lambda e: e.activation(out=sb, in_=st[0], func=AF.Sigmoid), r=st[1], w=ksb)
                    else:
                        def ep(st, c=i * 4 + j, j=j, th=th):
                            sb, ksb = WR.view(49152 + (j * 2 + th) * 2048, 1024, BF16)
                            ut, kut = r4.get(1024, BF16)
                            S.op("dve", lambda e: e.tensor_tensor(out=ut, in0=st[0], in1=sb, op=ALU.mult), r=st[1] + ksb, w=kut)
                            S.dma("sp", lambda e: e.dma_start(out=U[c][:, th * 1024:(th + 1) * 1024], in_=ut), r=kut, w=[("U", c)])
                            if pair and th == 1:
                                S.dma("sp", lambda e: e.dma_start(out=XBH[c * 128:(c + 1) * 128, :], in_=ut[:, 992:1024]),
                                      r=kut, w=[("XBH", c)])
                    units.append((mmw, ep))
        prev = None
        for (mmf, epf) in units:
            st = mmf()
            if not PIPE:
                epf(st)
                continue
            if prev is not None:
                prev[0](prev[1])
            prev = (epf, st)
        if PIPE:
            prev[0](prev[1])

    def conv_prep(l, seg, c):
        ue, kue = CVR.view((c % 3) * 4160, 2080, BF16)
        if pair:
            S.dma("sp", lambda e: e.dma_start(out=ue[:, 0:32], in_=XGH[c * 128:(c + 1) * 128, :]), r=[("XGH",)], w=kue)
            S.op("pool", lambda e: e.tensor_scalar(out=ue[:, 0:32], in0=ue[:, 0:32], scalar1=cinfo[:, 2:3], scalar2=None,
                                                   op0=ALU.mult), r=kue + k_cinfo, w=kue)
        elif seg == 0:
            S.op("pool", lambda e: e.memset(ue[:, 0:32], 0.0), w=kue)
        else:
            S.dma("sp", lambda e: e.dma_start(out=ue[:, 0:32], in_=UH[c]), r=[("UH", c)], w=kue)
        S.dma("sp", lambda e: e.dma_start(out=ue[:, 32:32 + TS], in_=U[c]), r=[("U", c)], w=kue)
        if seg + 1 < nseg:
            S.dma("sp", lambda e: e.dma_start(out=UH[c], in_=ue[:, TS:TS + 32]), r=kue, w=[("UH", c)])
        dg, kdg = WR.view((c % 2) * 8192, CW * 128, BF16)
        dg3 = dg.rearrange("p (j n) -> p j n", j=CW)
        wv = pvs[l][0][:, 98 + c * CW: 98 + (c + 1) * CW]
        S.op("pool", lambda e: e.tensor_tensor(
            out=dg3, in0=ident.unsqueeze(1).broadcast_to([128, CW, 128]), in1=wv.unsqueeze(2).broadcast_to([128, CW, 128]),
            op=ALU.mult), r=k_ident + pvs[l][1], w=kdg)
        return ue, kue, dg3, kdg

    def conv_tile(l, seg, c, prep):
        ue, kue, dg3, kdg = prep
        for th in range(2):
            px, kpx = psv(PX, "PX", ((c * 2 + th) % 2) * 1024, 1024)

            def mm(e, px=px, th=th):
                for j in range(CW):
                    for tb in range(2):
                        o = 2 + j + th * 1024 + tb * 512
                        ins = e.matmul(px[:, tb * 512:(tb + 1) * 512], lhsT=dg3[:, j, :], rhs=ue[:, o:o + 512],
                                       start=(j == 0), stop=(j == CW - 1))
                return ins
            S.op("pe", mm, r=kdg + kue, w=kpx)
            yv, kyv = B0.view(c * 4096 + th * 2048, 1024, BF16)
            S.op("act", lambda e, yv=yv, px=px: e.activation(out=yv, in_=px, func=AF.Identity, bias=pvcol(l, G_DWB, c),
                                                             scale=1.0), r=kpx + pvs[l][1], w=kyv)

    def phase3(l, seg):
        koff = TS if (seg > 0 or pair) else 0
        nk = koff + TS
        hb = 1 if (seg > 0 or pair) else 0
        scale = HD ** -0.5
        small = Ring(R4, 2048, 12)
        vbase = {}
        nslot = 0
        for d in (1, 4, 16):
            nb = TS // (128 * d)
            for r in range(d):
                vbase[(d, r)] = nslot
                nslot += nb + hb
        assert nslot * 256 <= 24576 and 49152 + 2 * 8192 <= 65536
        def head_bufs(h):
            s = h % 2
            qT, kq = B0.view(s * 28672, TS, BF16)
            kT, kk = B0.view(s * 28672 + 4096, nk, BF16)
            acc, ka = B0.view(s * 28672 + 12288, 2 * TS, F32)
            vb, kvb = WR.view(s * 24576, nslot * 128, BF16)
            return qT, kq, kT, kk, acc, ka, vb, kvb

        def load_head(h):
            qT, kq, kT, kk, acc, ka, vb, kvb = head_bufs(h)
            vb3 = vb.rearrange("p (m f) -> p m f", f=128)
            S.dma("sp", lambda e: e.dma_start(out=qT, in_=QT[h]), r=[("QT", h)], w=kq)
            hs = slice((h % 4) * 128, (h % 4 + 1) * 128)
            if pair:
                S.dma("sp", lambda e: e.dma_start(out=kT[:, 0:TS], in_=XGK[h // 4][hs, :]), r=[("XGK", h // 4)], w=kk)
                S.dma("sp", lambda e: e.dma_start(out=kT[:, TS:2 * TS], in_=XBK[h // 4][hs, :]), r=[("KT", h)], w=kk)
            else:
                S.dma("sp", lambda e: e.dma_start(out=kT, in_=KT[h][:, seg * TS - koff: seg * TS + TS]), r=[("KT", h)], w=kk)
            vT, kvt = WR.view(49152 + (h % 2) * 8192, nk, BF16)
            if pair:
                S.dma("sp", lambda e: e.dma_start(out=vT[:, 0:TS], in_=XGV[h // 4][hs, :]), r=[("XGV", h // 4)], w=kvt)
                S.dma("sp", lambda e: e.dma_start(out=vT[:, TS:2 * TS], in_=XBV[h // 4][hs, :]), r=[("V", h)], w=kvt)
            else:
                S.dma("sp", lambda e: e.dma_start(out=vT, in_=V[h][:, seg * TS - koff: seg * TS + TS]), r=[("V", h)], w=kvt)

        slot_src = []
        for d in (1, 4, 16):
            for r in range(d):
                for m in range(TS // (128 * d) + hb):
                    slot_src.append((koff - hb * 128 * d + r + m * 128 * d, d))
        assert len(slot_src) == nslot
        NG = (nslot + 7) // 8
        tbank = [0]

        def prep_v(h, g):
            qT, kq, kT, kk, acc, ka, vb, kvb = head_bufs(h)
            vT, kvt = WR.view(49152 + (h % 2) * 8192, nk, BF16)
            s0, s1 = 8 * g, min(nslot, 8 * g + 8)
            bk = 2 + tbank[0] % 2
            tbank[0] += 1
            pt, kpt = psv(PS, "PS", bk * 512, 512, BF16)

            def tr(e):
                for k, sl in enumerate(range(s0, s1)):
                    c0, d = slot_src[sl]
                    ins = e.transpose(out=pt[:, k * 128:(k + 1) * 128], in_=vT[:, c0: c0 + 127 * d + 1: d], identity=ident)
                return ins
            S.op("pe", tr, r=kvt + k_ident, w=kpt)
            n = (s1 - s0) * 128
            dstv = vb[:, s0 * 128: s0 * 128 + n]
            kd = [(WR.name, gg) for gg in range(((h % 2) * 24576 + s0 * 256) // GRAN, ((h % 2) * 24576 + s1 * 256 - 1) // GRAN + 1)]
            if g % 2 == 0:
                S.op("act", lambda e: e.activation(out=dstv, in_=pt[:, 0:n], func=AF.Copy), r=kpt, w=kd)
            else:
                S.op("dve", lambda e: e.tensor_copy(out=dstv, in_=pt[:, 0:n]), r=kpt, w=kd)

        pending_norm = []
        load_head(0)
        for g in range(NG):
            prep_v(0, g)
        for h in range(NH):
            if h + 1 < NH:
                load_head(h + 1)
            qT, kq, kT, kk, acc, ka, vb, kvb = head_bufs(h)
            acc3 = acc.rearrange("p (a n) -> p a n", a=2)
            vb3 = vb.rearrange("p (m f) -> p m f", f=128)
            ulist = []
            for d in (1, 4, 16):
                nbo = TS // (128 * d)
                for r in range(d):
                    for n in range(nbo):
                        ulist.append((d, r, n))
            pslot = [0, 0]
            state = {}

            def stage1(u, h=h, qT=qT, kT=kT, kq=kq, kk=kk):
                d, r, n = u
                bq = n * 128 * d + r
                has_prev = (koff + bq - 128 * d) >= 0
                qv = qT[:, bq: bq + 127 * d + 1: d]
                kc = kT[:, koff + bq: koff + bq + 127 * d + 1: d]
                ps, kps = psv(PX, "PX", (pslot[0] % 3) * 512, 256)
                pslot[0] += 1
                c0 = 0 if has_prev else 128

                def mm1(e):
                    if has_prev:
                        kp = kT[:, koff + bq - 128 * d: koff + bq - d + 1: d]
                        e.matmul(ps[:, 0:128], lhsT=kp, rhs=qv, start=True, stop=True)
                    return e.matmul(ps[:, 128:256], lhsT=kc, rhs=qv, start=True, stop=True)
                S.op("pe", mm1, r=kq + kk, w=kps)
                pt, kpt = small.get(256, BF16)
                if pair and n == 0:
                    S.op("act", lambda e: e.activation(out=pt[:, 0:128], in_=ps[:, 0:128], func=AF.Exp, bias=cinfo[:, 1:2], scale=scale),
                         r=kps + k_cinfo, w=kpt)
                    S.op("act", lambda e: e.activation(out=pt[:, 128:256], in_=ps[:, 128:256], func=AF.Exp, scale=scale), r=kps, w=kpt)
                else:
                    S.op("act", lambda e: e.activation(out=pt[:, c0:256], in_=ps[:, c0:256], func=AF.Exp, scale=scale), r=kps, w=kpt)
                S.op("pool", lambda e: e.tensor_tensor(out=pt[:, c0:256], in0=pt[:, c0:256], in1=mask2[:, c0:256], op=ALU.mult),
                     r=kpt + k_mask, w=kpt)
                state[u] = (pt, kpt, has_prev, bq)

            def stage2(u, vb3=vb3, kvb=kvb, acc3=acc3, ka=ka):
                d, r, n = u
                pt, kpt, has_prev, bq = state.pop(u)
                pi_ = pslot[1] % 3
                po, kpo = psv(PX, "PX", 1536, 256) if pi_ == 0 else psv(PS, "PS", (pi_ - 1) * 512, 256)
                pslot[1] += 1
                sl = vbase[(d, r)] + n + hb

                def mm2(e):
                    if has_prev:
                        e.matmul(po[:, 0:128], lhsT=vb3[:, sl - 1, :], rhs=pt[:, 0:128], start=True, stop=False)
                    e.matmul(po[:, 0:128], lhsT=vb3[:, sl, :], rhs=pt[:, 128:256], start=(not has_prev), stop=True)
                    if has_prev:
                        e.matmul(po[:, 128:256], lhsT=ones, rhs=pt[:, 0:128], start=True, stop=False)
                    return e.matmul(po[:, 128:256], lhsT=ones, rhs=pt[:, 128:256], start=(not has_prev), stop=True)
                S.op("pe", mm2, r=kpt + kvb + k_ones, w=kpo)
                po3 = po.rearrange("p (a n) -> p a n", a=2)
                av = acc3[:, :, bq: bq + 127 * d + 1: d]
                if d == 1:
                    S.op("dve", lambda e: e.tensor_copy(out=av, in_=po3), r=kpo, w=ka)
                else:
                    S.op("dve", lambda e: e.tensor_tensor(out=av, in0=po3, in1=av, op=ALU.add), r=kpo + ka, w=ka)
            LA = 3
            gnext = 0
            for idx in range(len(ulist) + LA):
                if idx < len(ulist):
                    stage1(ulist[idx])
                if idx >= LA:
                    stage2(ulist[idx - LA])
                if idx == 5 and pending_norm:
                    pending_norm.pop(0)()
                if h + 1 < NH and idx >= 6 and idx % 4 == 0 and gnext < NG:
                    prep_v(h + 1, gnext)
                    gnext += 1
            while h + 1 < NH and gnext < NG:
                prep_v(h + 1, gnext)
                gnext += 1
            def normalise(acc=acc, ka=ka, h=h):
                S.op("act", lambda e: e.activation(out=acc[:, TS:2 * TS], in_=acc[:, TS:2 * TS], func=AF.Ln), r=ka, w=ka)
                S.op("act", lambda e: e.activation(out=acc[:, TS:2 * TS], in_=acc[:, TS:2 * TS], func=AF.Exp, scale=-1.0), r=ka, w=ka)
                at, kat = r8.get(TS, BF16)
                S.op("dve", lambda e: e.tensor_tensor(out=at, in0=acc[:, 0:TS], in1=acc[:, TS:2 * TS], op=ALU.mult), r=ka, w=kat)
                S.dma("sp", lambda e: e.dma_start(out=ATT[h], in_=at), r=kat, w=[("ATT", h)])
            pending_norm.append(normalise)
        while pending_norm:
            pending_norm.pop(0)()

    def colsum_sq(src_fn, nsrc, pt, ptname, pbase):
        for c in range(nsrc):
            sv, ksv = src_fn(c)
            sq, ksq = r4.get(TS, BF16)
            S.op("act", lambda e, sq=sq, sv=sv: e.activation(out=sq, in_=sv, func=AF.Square), r=ksv, w=ksq)
            pk = [(ptname, b) for b in range(pbase // 512, pbase // 512 + 4)]

            def mm(e, sq=sq, c=c):
                for tb in range(4):
                    ins = e.matmul(pt[:, pbase + tb * 512: pbase + (tb + 1) * 512], lhsT=ones, rhs=sq[:, tb * 512:(tb + 1) * 512],
                                   start=(c == 0), stop=(c == nsrc - 1))
                return ins
            S.op("pe", mm, r=ksq + k_ones, w=pk)

    def gate_finalize(l, src_fn, rstd, krstd, ggrp, sgsrc, aybase):
        for c in range(16):
            sv, ksv = src_fn(c)
            sg, ksg = r4.get(TS, BF16)
            S.dma("sp", lambda e, sg=sg, c=c: e.dma_start(out=sg, in_=sgsrc[c]), r=[(sgsrc.tensor.name, c)], w=ksg)
            t, kt = r8.get(TS, F32)
            S.op("dve", lambda e, t=t, sv=sv, c=c: e.scalar_tensor_tensor(out=t, in0=sv, scalar=pvcol(l, ggrp, c), in1=rstd,
                                                                         op0=ALU.mult, op1=ALU.mult),
                 r=ksv + krstd + pvs[l][1], w=kt)
            o, ko = r4.get(TS, BF16)
            S.op("pool", lambda e, o=o, t=t, sg=sg: e.tensor_tensor(out=o, in0=t, in1=sg, op=ALU.mult), r=kt + ksg, w=ko)
            S.dma("act", lambda e, o=o, c=c: e.dma_start(out=AY[aybase + c], in_=o), r=ko, w=[("AY", aybase + c)])

    def finalize_one(l, c, sv, ksv, rstd, krstd, ggrp, sgsrc, aybase):
        sg, ksg = r4.get(TS, BF16)
        S.dma("sp", lambda e: e.dma_start(out=sg, in_=sgsrc[c]), r=[(sgsrc.tensor.name, c)], w=ksg)
        t, kt = r8.get(TS, F32)
        S.op("dve", lambda e: e.scalar_tensor_tensor(out=t, in0=sv, scalar=pvcol(l, ggrp, c), in1=rstd, op0=ALU.mult, op1=ALU.mult),
             r=ksv + krstd + pvs[l][1], w=kt)
        o, ko = r4.get(TS, BF16)
        S.op("dve", lambda e: e.tensor_tensor(out=o, in0=t, in1=sg, op=ALU.mult), r=kt + ksg, w=ko)
        S.dma("act", lambda e: e.dma_start(out=AY[aybase + c], in_=o), r=ko, w=[("AY", aybase + c)])

    def phase4_conv(l, seg):
        for h in range(NH):
            sv, ksv = r4.get(TS, BF16)
            S.dma("sp", lambda e, sv=sv, h=h: e.dma_start(out=sv, in_=ATT[h]), r=[("ATT", h)], w=ksv)
            sq, ksq = r4.get(TS, BF16)
            S.op("act", lambda e, sq=sq, sv=sv: e.activation(out=sq, in_=sv, func=AF.Square), r=ksv, w=ksq)

            def mm(e, sq=sq, h=h):
                for tb in range(4):
                    ins = e.matmul(PX[:, tb * 512:(tb + 1) * 512], lhsT=ones, rhs=sq[:, tb * 512:(tb + 1) * 512],
                                   start=(h == 0), stop=(h == NH - 1))
                return ins
            S.op("pe", mm, r=ksq + k_ones, w=[("PX", b_) for b_ in range(4)])
        rstd, krstd = WR.view(49152, TS, F32)
        rsqrt_tile(rstd, krstd, PX[:, 0:TS], [("PX", b_) for b_ in range(4)], 1.0 / D)
        prep = conv_prep(l, seg, 0)
        for c in range(NCH):
            nprep = conv_prep(l, seg, c + 1) if c + 1 < NCH else None
            conv_tile(l, seg, c, prep)
            prep = nprep
            sv, ksv = r4.get(TS, BF16)
            S.dma("sp", lambda e, sv=sv, c=c: e.dma_start(out=sv, in_=ATT[c]), r=[("ATT", c)], w=ksv)
            finalize_one(l, c, sv, ksv, rstd, krstd, G_ATTO, SGA, 0)

    def phase5(l):
        def src(c):
            return B0.view(c * 4096, TS, BF16)
        for c in range(NCH):
            sv, ksv = src(c)

            def mm(e, sv=sv, c=c):
                for tb in range(4):
                    ins = e.matmul(PX[:, tb * 512:(tb + 1) * 512], lhsT=ones, rhs=sv[:, tb * 512:(tb + 1) * 512],
                                   start=(c == 0), stop=(c == NCH - 1))
                return ins
            S.op("pe", mm, r=ksv + k_ones, w=[("PX", b) for b in range(4)])
        colsum_sq(src, NCH, PS, "PS", 0)
        mean, kmean = WR.view(57344, TS, F32)
        rstd, krstd = WR.view(49152, TS, F32)
        kpx = [("PX", b) for b in range(4)]
        kps = [("PS", b) for b in range(4)]
        S.op("dve", lambda e: e.tensor_scalar(out=mean, in0=PX[:, 0:TS], scalar1=1.0 / D, scalar2=None, op0=ALU.mult),
             r=kpx, w=kmean)
        msq, kmsq = r8.get(TS, F32)
        S.op("pool", lambda e: e.tensor_tensor(out=msq, in0=mean, in1=mean, op=ALU.mult), r=kmean, w=kmsq)
        S.op("dve", lambda e: e.scalar_tensor_tensor(out=rstd, in0=PS[:, 0:TS], scalar=1.0 / D, in1=msq, op0=ALU.mult,
                                                     op1=ALU.subtract), r=kps + kmsq, w=krstd)
        rsqrt_tile(rstd, krstd, rstd, krstd, 1.0)
        S.op("dve", lambda e: e.tensor_tensor(out=mean, in0=mean, in1=rstd, op=ALU.mult), r=kmean + krstd, w=kmean)
        for c in range(NCH):
            for th in range(2):
                sv, ksv = B0.view(c * 4096 + th * 2048, 1024, BF16)
                t, kt = r4.get(1024, F32)
                S.op("dve", lambda e, t=t, sv=sv, th=th: e.tensor_tensor(out=t, in0=sv, in1=rstd[:, th * 1024:(th + 1) * 1024],
                                                                        op=ALU.mult), r=ksv + krstd, w=kt)
                S.op("dve", lambda e, t=t, th=th: e.tensor_tensor(out=t, in0=t, in1=mean[:, th * 1024:(th + 1) * 1024],
                                                                  op=ALU.subtract), r=kt + kmean, w=kt)
                S.op("act", lambda e, t=t, sv=sv, c=c: e.activation(out=sv, in_=t, func=AF.Silu, bias=pvcol(l, G_LNB, c),
                                                                    scale=pvcol(l, G_LNG, c)), r=kt + pvs[l][1], w=ksv)
        def rhs_y(c, t0, n):
            return B0.view(c * 4096 + t0 * 2, n, BF16)
        pw_loaded = {}

        def pw_pre(i):
            for ii in (i, i + 1):
                if ii < 4 and ii not in pw_loaded:
                    pw_loaded[ii] = load_w(w_pw[l], ii * 512)
            return pw_loaded[i]

        def pw_mm(i, j, th):
            wb3, kw = pw_pre(i)
            e_ = i * 4 + j
            sgt, ksgt = r2k.get(1024, BF16)
            S.dma("sp", lambda e: e.dma_start(out=sgt, in_=SGC[e_][:, th * 1024:(th + 1) * 1024]), r=[("SGC", e_)], w=ksgt)
            return ws_unit(wb3, kw, j, th, rhs_y, split=(i == 0 and j == 0)) + (sgt, ksgt)

        def pw_epi(st, e_, th):
            pa, kpa, sgt, ksgt = st
            o, ko = r4.get(1024, BF16)
            S.op("dve", lambda e: e.scalar_tensor_tensor(out=o, in0=pa, scalar=pvcol(l, G_CONVO, e_), in1=sgt, op0=ALU.mult,
                                                         op1=ALU.mult), r=kpa + ksgt + pvs[l][1], w=ko)
            S.dma("act", lambda e: e.dma_start(out=AY[16 + e_][:, th * 1024:(th + 1) * 1024], in_=o), r=ko, w=[("AY", 16 + e_)])
            sq, ksq = r4.get(1024, BF16)
            S.op("act", lambda e: e.activation(out=sq, in_=pa, func=AF.Square), r=kpa, w=ksq)

            def mms(e):
                for tb in range(2):
                    o_ = th * 1024 + tb * 512
                    ins = e.matmul(PX[:, o_:o_ + 512], lhsT=ones, rhs=sq[:, tb * 512:(tb + 1) * 512],
                                   start=(e_ == 0), stop=(e_ == NCH - 1))
                return ins
            S.op("pe", mms, r=ksq + k_ones, w=[("PX", th * 2), ("PX", th * 2 + 1)])
        prevu = None
        for i in range(4):
            for j in range(4):
                for th in range(2):
                    st = pw_mm(i, j, th)
                    if prevu is not None:
                        pw_epi(*prevu)
                    prevu = (st, i * 4 + j, th)
        pw_epi(*prevu)
        rsqrt_tile(rstd, krstd, PX[:, 0:TS], kpx, 1.0 / D)
        pr, kpr = psv(PS, "PS", 0, 16)

        def mmt(e):
            for tt in range(16):
                ins = e.matmul(pr[:, tt:tt + 1], lhsT=rstd[0:1, tt * 128:(tt + 1) * 128], rhs=one_f[0:1, 0:1], start=True, stop=True)
            return ins
        S.op("pe", mmt, r=krstd + k_onef, w=kpr)
        S.op("dve", lambda e: e.tensor_copy(out=rc16, in_=pr), r=kpr, w=k_rc16)

    def phase6(l, xsrc, xdst, tok0):
        for th in range(2):
            ayv, kay = B0.view(0, 32 * 1024, BF16)
            ay3 = ayv.rearrange("p (c n) -> p c n", c=32)
            for c in range(32):
                v1, k1 = B0.view(c * 2048, 1024, BF16)
                S.dma("sp", lambda e, v1=v1, c=c, th=th: e.dma_start(out=v1, in_=AY[c][:, th * 1024:(th + 1) * 1024]),
                      r=[("AY", c)], w=k1)
            nxt = load_w(w_out[l], 0, nchunks=32, slots=2)
            for cb in range(4):
                wb3, kw = nxt
                if cb + 1 < 4:
                    nxt = load_w(w_out[l], (cb + 1) * 512, nchunks=32, slots=2)
                for tt in range(8):
                    a = acc6[0]
                    acc6[0] = (a + 1) % 2
                    pa, kpa = psv(PS, "PS", a * 1024, 512)
                    pc, kpc = psv(PS, "PS", a * 1024 + 512, 512)
                    t0 = tok0 + th * 1024 + tt * 128
                    ttg = th * 8 + tt
                    xt, kxt = r4.get(512, F32)
                    S.dma("sp", lambda e, xt=xt, t0=t0, cb=cb: e.dma_start(out=xt, in_=xsrc[t0:t0 + 128, cb * 512:(cb + 1) * 512]),
                          w=kxt)

                    def mm(e, pa=pa, pc=pc, tt=tt, wb3=wb3):
                        for c in range(32):
                            ins = e.matmul(pa if c < 16 else pc, lhsT=ay3[:, c, tt * 128:(tt + 1) * 128], rhs=wb3[:, c, :],
                                           start=(c % 16 == 0), stop=(c % 16 == 15))
                        return ins
                    if cb == 0 and tt == 0:
                        for c in range(32):
                            S.op("pe", lambda e, pa=pa, pc=pc, tt=tt, wb3=wb3, c=c: e.matmul(
                                pa if c < 16 else pc, lhsT=ay3[:, c, tt * 128:(tt + 1) * 128], rhs=wb3[:, c, :],
                                start=(c % 16 == 0), stop=(c % 16 == 15)),
                                r=kw + B0.view(c * 2048, 1024, BF16)[1], w=(kpa if c < 16 else kpc))
                    else:
                        S.op("pe", mm, r=kw + kay, w=kpa + kpc)
                    ot, kot = r4.get(512, F32)
                    S.op("dve", lambda e, ot=ot, pc=pc, xt=xt, ttg=ttg: e.scalar_tensor_tensor(
                        out=ot, in0=pc, scalar=rc16[:, ttg:ttg + 1], in1=xt, op0=ALU.mult, op1=ALU.add), r=kpc + kxt + k_rc16, w=kot)
                    S.op("dve", lambda e, ot=ot, pa=pa: e.tensor_tensor(out=ot, in0=pa, in1=ot, op=ALU.add), r=kpa + kot, w=kot)
                    S.dma("act", lambda e, ot=ot, t0=t0, cb=cb: e.dma_start(out=xdst[t0:t0 + 128, cb * 512:(cb + 1) * 512], in_=ot),
                          r=kot, w=[("XD", l, t0 // TS)])

    done = False
    for l in range(DEPTH):
        xsrc = x_in if l == 0 else X1
        xdst = X1 if l == 0 else out
        for seg in range(nseg):
            if nseg > 1 or l == 0:
                rope_tables(seg)
            if l == 0 or nseg > 1:
                build_gb(l)
            phase1(l, xsrc, seg * TS)
            if stop_after == "p1":
                done = True
                break
            phase2(l, seg)
            if stop_after == "p2":
                done = True
                break
            if pair:
                S.cc(lambda e: e.collective_compute("AllGather", ALU.bypass, replica_groups=RGS, ins=[XBH[:, :]], outs=[XGH[:, :]]),
                     r=[("XBH", c) for c in range(NCH)], w=[("XGH",)])
            phase3(l, seg)
            if stop_after == "p3":
                done = True
                break
            phase4_conv(l, seg)
            if debug:
                for c in range(NCH):
                    S.dma("sp", lambda e, c=c: e.dma_start(out=Y[c], in_=B0.view(c * 4096, TS, BF16)[0]),
                          r=B0.view(c * 4096, TS, BF16)[1], w=[("Y", c)])
            phase5(l)
            if stop_after == "p5":
                done = True
                break
            if l + 1 < DEPTH and seg == nseg - 1:
                build_gb(l + 1)
            phase6(l, xsrc, xdst, seg * TS)
        if done or stop_after == "l0":
            break
    S.emit()
    return nc


def host_consts():
    c = np.zeros((128, 512), np.float32)
    idx = np.arange(128)
    c[idx, idx] = 1.0
    c[(idx + 64) % 128, 128 + idx] = 1.0
    kk = idx[:, None]
    qq = idx[None, :]
    c[:, 256:384] = (kk >= qq).astype(np.float32)
    c[:, 384:512] = (kk <= qq).astype(np.float32)
    return c


def pack_params(norm_g, att_out_g, conv_out_g, dw_bias, conv_ln_g, conv_ln_b, q_norm_g, k_norm_g, dw_kernel):
    pv = np.zeros((DEPTH, 128, NPV), np.float32)
    for l in range(DEPTH):
        for gi, a in enumerate((norm_g, att_out_g, conv_out_g, dw_bias, conv_ln_g, conv_ln_b)):
            pv[l, :, gi * 16:(gi + 1) * 16] = a[l].reshape(16, 128).T
        pv[l, :, 96] = q_norm_g[l]
        pv[l, :, 97] = k_norm_g[l]
        pv[l, :, 98:] = dw_kernel[l].reshape(CW, 16, 128).transpose(2, 1, 0).reshape(128, 16 * CW)
    return pv


def kernel(x, norm_g, w_in, q_norm_g, k_norm_g, dw_kernel, dw_bias, conv_ln_g, conv_ln_b, w_pw, att_out_g, conv_out_g, w_out):
    x = np.asarray(x, np.float32)
    B, SEQ = x.shape[0], x.shape[1]
    assert SEQ == 2 * TS and B == 4
    pv = pack_params(*[np.asarray(a, np.float32) for a in (norm_g, att_out_g, conv_out_g, dw_bias, conv_ln_g, conv_ln_b,
                                                             q_norm_g, k_norm_g, dw_kernel)])
    cst = host_consts()
    w_in = np.ascontiguousarray(w_in, np.float32)
    w_pw = np.ascontiguousarray(w_pw, np.float32)
    w_out = np.ascontiguousarray(w_out, np.float32)
    nc = build(1, pair=True)
    in_maps = []
    for core in range(8):
        b, half = core // 2, core % 2
        ci = np.zeros((128, 4), np.float32)
        ci[:, 0] = half * TS
        ci[:, 1] = 0.0 if half == 1 else -30000.0
        ci[:, 2] = float(half)
        in_maps.append({"x": np.ascontiguousarray(x[b, half * TS:(half + 1) * TS]), "w_in": w_in, "w_pw": w_pw, "w_out": w_out,
                        "pv": pv, "cst": cst, "cinfo": ci})
    res = run_bass_kernel_spmd(nc, in_maps, core_ids=list(range(8)))
    out = np.zeros((B, SEQ, D), np.float32)
    for core in range(8):
        b, half = core // 2, core % 2
        out[b, half * TS:(half + 1) * TS] = np.asarray(res.results[core]["out"]).reshape(TS, D)
    return out
```

```python
import math
import numpy as np
import concourse.bass as bass
import concourse.mybir as mybir
from concourse.bass_utils import run_bass_kernel_spmd

F32 = mybir.dt.float32
BF16 = mybir.dt.bfloat16
I32 = mybir.dt.int32
AF = mybir.ActivationFunctionType
ALU = mybir.AluOpType

D = 2048
DIN = 14336
TS = 2048
NCH = 16
HD = 128
NH = 16
CW = 31
EPS = 1e-6
DEPTH = 2
NPV = 6 * 16 + 2 + 16 * CW
GRAN = 2048
PIPE = True
RGS = [[0, 1], [2, 3], [4, 5], [6, 7]]


class Sched:
    NDS = 12

    def __init__(self, nc):
        self.nc = nc
        self.ops = []
        self.esem = {e: nc.alloc_semaphore("es_" + e) for e in ("pe", "act", "dve", "pool")}
        self.dsem = {q: [nc.alloc_semaphore("ds_%s%d" % (q, i)) for i in range(self.NDS)]
                     for q in ("sp", "pool", "act")}
        self.ccsem = nc.alloc_semaphore("ccsem")
        self.cccount = 0

    def op(self, eng, fn, r=(), w=()):
        self.ops.append(dict(eng=eng, fn=fn, r=tuple(r), w=tuple(w), dma=False))

    def dma(self, q, fn, r=(), w=()):
        self.ops.append(dict(eng=q, fn=fn, r=tuple(r), w=tuple(w), dma=True))

    def cc(self, fn, r=(), w=()):
        self.ops.append(dict(eng="pool", fn=fn, r=tuple(r), w=tuple(w), dma=True, cc=True))

    def analyse(self):
        ops = self.ops
        last_w = {}
        readers = {}
        for i, o in enumerate(ops):
            pr = tuple(k for k in o["r"] if k[0] in ("PS", "PX"))
            if pr:
                o["w"] = tuple(o["w"]) + pr
                o["r"] = tuple(k for k in o["r"] if k[0] not in ("PS", "PX"))
            deps = set()
            for k in o["r"]:
                if k in last_w:
                    deps.add(last_w[k])
            for k in o["w"]:
                if k in last_w:
                    deps.add(last_w[k])
                deps.update(readers.get(k, ()))
            deps.discard(i)
            o["deps"] = deps
            for k in o["r"]:
                readers.setdefault(k, []).append(i)
            for k in o["w"]:
                last_w[k] = i
                readers[k] = []
        for o in ops:
            o["ms"] = False
        for o in ops:
            for d in o["deps"]:
                p = ops[d]
                if p["dma"]:
                    continue
                if p["eng"] == "pe" and o["eng"] == "pe" and not o["dma"]:
                    continue
                p["ms"] = True
        cnt = {e: 0 for e in self.esem}
        dstate = {q: [0] * self.NDS for q in self.dsem}
        drr = {q: 0 for q in self.dsem}
        for o in ops:
            if o.get("cc"):
                o["dsem"] = self.ccsem
                o["guard"] = self.cccount
                self.cccount += 1
                o["target"] = self.cccount
            elif o["dma"]:
                q = o["eng"]
                j = drr[q]
                drr[q] = (j + 1) % self.NDS
                o["dsem"] = self.dsem[q][j]
                o["guard"] = dstate[q][j]
                dstate[q][j] += 16
                o["target"] = dstate[q][j]
            elif o["ms"]:
                cnt[o["eng"]] += 1
                o["count"] = cnt[o["eng"]]
        self.final_dma = {q: list(dstate[q]) for q in dstate}
        waited = {}
        for o in ops:
            e = o["eng"]
            wl = {}
            if o["dma"] and o["guard"] > 0:
                wl[id(o["dsem"])] = (o["dsem"], o["guard"])
            for d in sorted(o["deps"]):
                p = ops[d]
                if p["dma"]:
                    s, v = p["dsem"], p["target"]
                else:
                    if p["eng"] == "pe" and e == "pe" and not o["dma"]:
                        continue
                    s, v = self.esem[p["eng"]], p["count"]
                if id(s) not in wl or wl[id(s)][1] < v:
                    wl[id(s)] = (s, v)
            out = []
            wd = waited.setdefault(e, {})
            for sid, (s, v) in wl.items():
                if wd.get(sid, 0) >= v:
                    continue
                wd[sid] = v
                out.append((s, v))
            o["waits"] = out

    def emit(self):
        self.analyse()
        nc = self.nc
        per = {e: [] for e in ("sp", "act", "pe", "dve", "pool")}
        for o in self.ops:
            per[o["eng"]].append(o)

        def make_body(ename):
            def body(e):
                for o in per[ename]:
                    for (s, v) in o["waits"]:
                        e.wait_ge(s, v)
                    inst = o["fn"](e)
                    if o.get("cc"):
                        inst.then_inc(o["dsem"], 1)
                    elif o["dma"]:
                        inst.then_inc(o["dsem"], 16)
                    elif o["ms"]:
                        inst.then_inc(self.esem[ename], 1)
                if ename == "sp":
                    for q in self.dsem:
                        for j, s in enumerate(self.dsem[q]):
                            if self.final_dma[q][j] > 0:
                                e.wait_ge(s, self.final_dma[q][j])
                    if self.cccount > 0:
                        e.wait_ge(self.ccsem, self.cccount)
            return body

        with nc.Block() as block:
            block.sync(make_body("sp"))
            block.scalar(make_body("act"))
            block.tensor(make_body("pe"))
            block.vector(make_body("dve"))
            block.gpsimd(make_body("pool"))


class Region:
    def __init__(self, nc, name, nbytes):
        self.t = nc.alloc_sbuf_tensor(name, [128, nbytes // 4], F32)
        self.name = name
        self.nbytes = nbytes

    def view(self, off, n, dt):
        sz = 2 if dt == BF16 else 4
        assert off % 4 == 0 and (n * sz) % 4 == 0 and off + n * sz <= self.nbytes, (off, n, self.nbytes)
        ap = self.t[:, off // 4:(off + n * sz) // 4]
        if dt != F32:
            ap = ap.bitcast(dt)
        keys = [(self.name, g) for g in range(off // GRAN, (off + n * sz - 1) // GRAN + 1)]
        return ap, keys


def build(nseg, debug=False, stop_after=None, pair=False):
    nc = bass.Bass("TRN2", target_bir_lowering=False)
    S = Sched(nc)
    NT = nseg * TS
    okind = "ExternalOutput" if debug else "Internal"

    x_in = nc.dram_tensor("x", [NT, D], F32, kind="ExternalInput").ap()
    w_in = nc.dram_tensor("w_in", [DEPTH, D, DIN], F32, kind="ExternalInput").ap()
    w_pw = nc.dram_tensor("w_pw", [DEPTH, D, D], F32, kind="ExternalInput").ap()
    w_out = nc.dram_tensor("w_out", [DEPTH, 2 * D, D], F32, kind="ExternalInput").ap()
    pv_in = nc.dram_tensor("pv", [DEPTH, 128, NPV], F32, kind="ExternalInput").ap()
    cst_in = nc.dram_tensor("cst", [128, 512], F32, kind="ExternalInput").ap()
    cinfo_in = nc.dram_tensor("cinfo", [128, 4], F32, kind="ExternalInput").ap()
    out = nc.dram_tensor("out", [NT, D], F32, kind="ExternalOutput").ap()

    X1 = nc.dram_tensor("X1", [NT, D], F32, kind=okind).ap()
    QT = nc.dram_tensor("QT", [NH, 128, TS], BF16, kind=okind).ap()
    if pair:
        assert nseg == 1
        XBK = [nc.dram_tensor("XBK%d" % i, [512, TS], BF16).ap() for i in range(4)]
        XGK = [nc.dram_tensor("XGK%d" % i, [1024, TS], BF16).ap() for i in range(4)]
        XBV = [nc.dram_tensor("XBV%d" % i, [512, TS], BF16).ap() for i in range(4)]
        XGV = [nc.dram_tensor("XGV%d" % i, [1024, TS], BF16).ap() for i in range(4)]
        XBH = nc.dram_tensor("XBH", [D, 32], BF16).ap()
        XGH = nc.dram_tensor("XGH", [2 * D, 32], BF16).ap()
        KT = V = None
    else:
        KT = nc.dram_tensor("KT", [NH, 128, NT], BF16, kind=okind).ap()
        V = nc.dram_tensor("V", [NH, 128, NT], BF16, kind=okind).ap()
    SGA = nc.dram_tensor("SGA", [NH, 128, TS], BF16, kind="Internal").ap()
    SGC = nc.dram_tensor("SGC", [NCH, 128, TS], BF16, kind="Internal").ap()
    Y = nc.dram_tensor("Y", [NCH, 128, TS], BF16, kind=okind).ap()
    ATT = nc.dram_tensor("ATT", [NH, 128, TS], BF16, kind=okind).ap()
    CV = nc.dram_tensor("CV", [NCH, 128, TS], BF16, kind="Internal").ap()
    AY = nc.dram_tensor("AY", [2 * NCH, 128, TS], BF16, kind=okind).ap()
    U = nc.dram_tensor("U", [NCH, 128, TS], BF16, kind="Internal").ap()
    UH = nc.dram_tensor("UH", [NCH, 128, 32], BF16, kind="Internal").ap()

    B0 = Region(nc, "B0", 65536)
    WR = Region(nc, "WR", 65536)
    RP = Region(nc, "RP", 16384)
    R4 = Region(nc, "R4", 24576)
    R8 = Region(nc, "R8", 16384)
    MS = Region(nc, "MS", 8192)
    CVR = Region(nc, "CVR", 3 * 4160)
    PS = nc.alloc_psum_tensor("PSA", [128, 2048], F32)
    PX = nc.alloc_psum_tensor("PSX", [128, 2048], F32)

    def psv(t, name, off, n, dt=F32):
        ap = t[:, off:off + n]
        if dt == BF16:
            ap = ap.bitcast(BF16)
        keys = [(name, b) for b in range(off // 512, (off + n - 1) // 512 + 1)]
        return ap, keys

    class Ring:
        def __init__(self, reg, size, count, base=0):
            self.reg, self.size, self.count, self.i, self.base = reg, size, count, 0, base

        def get(self, n, dt):
            off = self.base + self.i * self.size
            self.i = (self.i + 1) % self.count
            return self.reg.view(off, n, dt)

    r4 = Ring(R4, 4096, 6)
    r8 = Ring(R8, 8192, 2)
    r2k = Ring(R8, 2048, 8)

    ident, k_ident = MS.view(0, 128, BF16)
    ones, k_ones = MS.view(256, 128, BF16)
    swp, k_swp = MS.view(512, 128, BF16)
    mask2, k_mask = MS.view(768, 256, BF16)
    pvs = [MS.view(2048 + l * 2560, NPV, F32) for l in range(DEPTH)]
    sc_off = 2048 + 2 * 2560
    sgn2, k_sgn2 = MS.view(sc_off, 1, F32)
    invf, k_invf = MS.view(sc_off + 4, 1, F32)
    gq2 = [MS.view(sc_off + 8 + 8 * l, 2, F32) for l in range(DEPTH)]
    ssq_t = [MS.view(sc_off + 64 + 8 * i, 1, F32) for i in range(2)]
    rstd_t = [MS.view(sc_off + 128 + 8 * i, 1, F32) for i in range(2)]
    negpi, k_negpi = MS.view(sc_off + 192, 1, F32)
    cinfo, k_cinfo = MS.view(sc_off + 256, 4, F32)
    rc16, k_rc16 = MS.view(sc_off + 448, 16, F32)
    one_f, k_onef = MS.view(sc_off + 512, 1, F32)
    cosT, k_cos = RP.view(0, TS, F32)
    sinT, k_sin = RP.view(8192, TS, F32)

    cst_f, k_cstf = r4.get(512, F32)
    S.dma("sp", lambda e: e.dma_start(out=cst_f, in_=cst_in[:, :]), w=k_cstf)
    S.op("dve", lambda e: e.tensor_copy(out=ident, in_=cst_f[:, 0:128]), r=k_cstf, w=k_ident)
    S.op("dve", lambda e: e.tensor_copy(out=swp, in_=cst_f[:, 128:256]), r=k_cstf, w=k_swp)
    S.op("dve", lambda e: e.tensor_copy(out=mask2, in_=cst_f[:, 256:512]), r=k_cstf, w=k_mask)
    S.op("dve", lambda e: e.memset(ones, 1.0), w=k_ones)
    S.op("dve", lambda e: e.memset(one_f, 1.0), w=k_onef)
    S.dma("sp", lambda e: e.dma_start(out=cinfo, in_=cinfo_in[:, :]), w=k_cinfo)
    S.op("dve", lambda e: e.memset(negpi, EPS), w=k_negpi)
    for l in range(DEPTH):
        S.dma("sp", lambda e, l=l: e.dma_start(out=pvs[l][0], in_=pv_in[l]), w=pvs[l][1])
        S.op("dve", lambda e, l=l: e.tensor_scalar(out=gq2[l][0], in0=pvs[l][0][:, 96:98], scalar1=1.0,
                                                   scalar2=None, op0=ALU.mult), r=pvs[l][1], w=gq2[l][1])
    pidx_i, k_pi = r4.get(1, I32)
    pidx_f, k_pf = r4.get(1, F32)
    S.op("pool", lambda e: e.iota(pidx_i, pattern=[[0, 1]], base=0, channel_multiplier=1), w=k_pi)
    S.op("dve", lambda e: e.tensor_copy(out=pidx_f, in_=pidx_i), r=k_pi, w=k_pf)
    S.op("dve", lambda e: e.tensor_scalar(out=sgn2, in0=pidx_f, scalar1=64.0, scalar2=2.0, op0=ALU.is_ge, op1=ALU.mult),
         r=k_pf, w=k_sgn2)
    S.op("dve", lambda e: e.tensor_scalar(out=sgn2, in0=sgn2, scalar1=-1.0, scalar2=None, op0=ALU.add), r=k_sgn2, w=k_sgn2)
    pm, k_pm = r4.get(1, F32)
    S.op("dve", lambda e: e.tensor_scalar(out=pm, in0=pidx_f, scalar1=64.0, scalar2=-64.0, op0=ALU.is_ge, op1=ALU.mult),
         r=k_pf, w=k_pm)
    S.op("dve", lambda e: e.tensor_tensor(out=pm, in0=pm, in1=pidx_f, op=ALU.add), r=k_pf + k_pm, w=k_pm)
    S.op("act", lambda e: e.activation(out=invf, in_=pm, func=AF.Exp, scale=-2.0 * math.log(10000.0) / 128.0),
         r=k_pm, w=k_invf)

    def pvcol(l, grp, c):
        return pvs[l][0][:, grp * 16 + c: grp * 16 + c + 1]
    G_NORM, G_ATTO, G_CONVO, G_DWB, G_LNG, G_LNB = range(6)

    def rsqrt_tile(dst, kdst, src, ksrc, scale):
        S.op("act", lambda e: e.activation(out=dst, in_=src, func=AF.Ln, bias=negpi, scale=scale), r=ksrc + k_negpi, w=kdst)
        S.op("act", lambda e: e.activation(out=dst, in_=dst, func=AF.Exp, scale=-0.5), r=kdst, w=kdst)

    def dwk(l, c, j):
        o = 98 + c * CW + j
        return pvs[l][0][:, o:o + 1]

    def rope_tables(seg):
        C1 = 6.28125
        C2 = 2 * math.pi - C1
        pos_i, k1 = B0.view(0, TS, I32)
        S.op("pool", lambda e: e.iota(pos_i, pattern=[[1, TS]], base=seg * TS, channel_multiplier=0), w=k1)
        ang, k2 = B0.view(8192, TS, F32)
        S.op("dve", lambda e: e.tensor_copy(out=ang, in_=pos_i), r=k1, w=k2)
        if pair:
            S.op("dve", lambda e: e.tensor_scalar(out=ang, in0=ang, scalar1=cinfo[:, 0:1], scalar2=None, op0=ALU.add),
                 r=k2 + k_cinfo, w=k2)
        S.op("dve", lambda e: e.tensor_scalar(out=ang, in0=ang, scalar1=invf, scalar2=None, op0=ALU.mult),
             r=k2 + k_invf, w=k2)
        for which, phase in ((0, 0.0), (1, math.pi / 2)):
            dst, kd = (sinT, k_sin) if which == 0 else (cosT, k_cos)
            base = 16384 + which * 24576
            angp, ka = B0.view(base, TS, F32)
            ki, kki = B0.view(base + 8192, TS, I32)
            kf, kkf = B0.view(base + 16384, TS, F32)
            S.op("dve", lambda e, angp=angp, phase=phase: e.tensor_scalar(out=angp, in0=ang, scalar1=phase, scalar2=None,
                                                                          op0=ALU.add), r=k2, w=ka)
            S.op("dve", lambda e, angp=angp, kf=kf: e.tensor_scalar(out=kf, in0=angp, scalar1=1.0 / (2 * math.pi), scalar2=None,
                                                                    op0=ALU.mult), r=ka, w=kkf)
            S.op("dve", lambda e, ki=ki, kf=kf: e.tensor_copy(out=ki, in_=kf), r=kkf, w=kki)
            S.op("dve", lambda e, ki=ki, kf=kf: e.tensor_copy(out=kf, in_=ki), r=kki, w=kkf)
            S.op("dve", lambda e, angp=angp, kf=kf: e.scalar_tensor_tensor(out=angp, in0=kf, scalar=-C1, in1=angp, op0=ALU.mult,
                                                                           op1=ALU.add), r=ka + kkf, w=ka)
            S.op("dve", lambda e, angp=angp, kf=kf: e.scalar_tensor_tensor(out=angp, in0=kf, scalar=-C2, in1=angp, op0=ALU.mult,
                                                                           op1=ALU.add), r=ka + kkf, w=ka)
            S.op("dve", lambda e, angp=angp, kf=kf: e.tensor_scalar(out=kf, in0=angp, scalar1=math.pi, scalar2=-2 * math.pi,
                                                                    op0=ALU.is_gt, op1=ALU.mult), r=ka, w=kkf)
            S.op("dve", lambda e, angp=angp, kf=kf: e.tensor_tensor(out=angp, in0=angp, in1=kf, op=ALU.add), r=ka + kkf, w=ka)
            S.op("dve", lambda e, angp=angp, kf=kf: e.tensor_scalar(out=kf, in0=angp, scalar1=-math.pi, scalar2=2 * math.pi,
                                                                    op0=ALU.is_lt, op1=ALU.mult), r=ka, w=kkf)
            S.op("dve", lambda e, angp=angp, kf=kf: e.tensor_tensor(out=angp, in0=angp, in1=kf, op=ALU.add), r=ka + kkf, w=ka)
            S.op("act", lambda e, angp=angp, dst=dst: e.activation(out=dst, in_=angp, func=AF.Sin), r=ka, w=kd)
        S.op("dve", lambda e: e.tensor_scalar(out=sinT, in0=sinT, scalar1=sgn2, scalar2=None, op0=ALU.mult),
             r=k_sin + k_sgn2, w=k_sin)

    def hT(c, t0, n):
        ap, keys = B0.view(c * 4096 + t0 * 2, n, BF16)
        return ap, keys

    def build_gb(l):
        gb, kgb = CVR.view(0, 16 * 128, BF16)
        for c in range(16):
            S.op("pool", lambda e, c=c: e.tensor_scalar(out=gb[:, c * 128:(c + 1) * 128], in0=ones, scalar1=pvcol(l, G_NORM, c),
                                                        scalar2=None, op0=ALU.mult), r=k_ones + pvs[l][1], w=kgb)

    def phase1(l, xsrc, tok0):
        ssq16, kss = MS.view(sc_off + 320, 16, F32)
        rstd16, krs = MS.view(sc_off + 384, 16, F32)
        S.op("dve", lambda e: e.memset(ssq16, 0.0), w=kss)
        gb, kgb = CVR.view(0, 16 * 128, BF16)
        hall, _ = B0.view(0, 16 * TS, BF16)
        h3 = hall.rearrange("p (c t) -> p c t", c=16)

        def load(tt):
            xt, kx = r8.get(D, F32)
            S.dma("sp", lambda e: e.dma_start(out=xt, in_=xsrc[tok0 + tt * 128: tok0 + (tt + 1) * 128, :]), w=kx)
            return xt, kx
        nxt = load(0)
        for tt in range(16):
            xt, kx = nxt
            junk, kj = r4.get(D, BF16)
            S.op("act", lambda e, xt=xt, junk=junk, tt=tt: e.activation(out=junk, in_=xt, func=AF.Square, accum_out=ssq16[:, tt:tt + 1]),
                 r=kx + kss, w=kj + [("ssq", tt)])
            S.op("act", lambda e, tt=tt: e.activation(out=rstd16[:, tt:tt + 1], in_=ssq16[:, tt:tt + 1], func=AF.Ln, bias=negpi, scale=1.0 / D),
                 r=[("ssq", tt)] + k_negpi, w=[("rstd", tt)])
            S.op("act", lambda e, tt=tt: e.activation(out=rstd16[:, tt:tt + 1], in_=rstd16[:, tt:tt + 1], func=AF.Exp, scale=-0.5),
                 r=[("rstd", tt)], w=[("rstd", tt)])
            xn, kxn = r4.get(D, BF16)
            S.op("act", lambda e, xt=xt, xn=xn, tt=tt: e.activation(out=xn, in_=xt, func=AF.Copy, scale=rstd16[:, tt:tt + 1]),
                 r=kx + [("rstd", tt)], w=kxn)
            if tt + 1 < 16:
                nxt = load(tt + 1)
            for g in range(4):
                pt, kpt = psv(PX, "PX", ((tt * 4 + g) % 4) * 512, 256, BF16)

                def tr(e, xn=xn, pt=pt, g=g):
                    for k in range(4):
                        c = g * 4 + k
                        ins = e.transpose(out=pt[:, k * 128:(k + 1) * 128], in_=xn[:, c * 128:(c + 1) * 128], identity=ident)
                    return ins
                S.op("pe", tr, r=kxn + k_ident, w=kpt)
                kh = []
                for k in range(4):
                    kh += hT(g * 4 + k, tt * 128, 128)[1]
                S.op("dve", lambda e, pt=pt, g=g, tt=tt: e.tensor_tensor(
                    out=h3[:, g * 4:(g + 1) * 4, tt * 128:(tt + 1) * 128], in0=pt.rearrange("p (c t) -> p c t", c=4),
                    in1=gb[:, g * 512:(g + 1) * 512].rearrange("p (c t) -> p c t", c=4), op=ALU.mult), r=kpt + kgb, w=kh)

    wslot = [0, 0]

    def load_w(src, col0, nchunks=16, ncols=512, slots=1):
        if slots == 1:
            i = wslot[0]
            wslot[0] = (i + 1) % 3
        else:
            i = wslot[1]
            wslot[1] = (i + 2) % 4
        wb, kw = WR.view(i * 16384, nchunks * ncols, BF16)
        wb3 = wb.rearrange("p (c n) -> p c n", c=nchunks)
        sv = src.rearrange("(c p) n -> p c n", p=128)
        for c0 in range(0, nchunks, 4):
            S.dma("pool", lambda e, c0=c0: e.dma_start(out=wb3[:, c0:c0 + 4, :], in_=sv[:, c0:c0 + 4, col0:col0 + ncols]),
                  w=kw)
        return wb3, kw

    acc_i = [0]
    acc6 = [0]

    def ws_unit(wb3, kw, j, th, rhs_fn, split=False):
        a = acc_i[0]
        acc_i[0] ^= 1
        pa, kpa = psv(PS, "PS", a * 1024, 1024)
        rk = []
        for c in range(16):
            rk += rhs_fn(c, th * 1024, 1024)[1]

        def mm(e):
            for c in range(16):
                for tb in range(2):
                    rv = rhs_fn(c, th * 1024 + tb * 512, 512)[0]
                    ins = e.matmul(pa[:, tb * 512:(tb + 1) * 512], lhsT=wb3[:, c, j * 128:(j + 1) * 128], rhs=rv,
                                   start=(c == 0), stop=(c == 15))
            return ins
        if not split:
            S.op("pe", mm, r=kw + rk, w=kpa)
        else:
            for c in range(16):
                def mmc(e, c=c):
                    for tb in range(2):
                        rv = rhs_fn(c, th * 1024 + tb * 512, 512)[0]
                        ins = e.matmul(pa[:, tb * 512:(tb + 1) * 512], lhsT=wb3[:, c, j * 128:(j + 1) * 128], rhs=rv,
                                       start=(c == 0), stop=(c == 15))
                    return ins
                S.op("pe", mmc, r=kw + rhs_fn(c, th * 1024, 1024)[1], w=kpa)
        return pa, kpa

    def qk_part1(pa, kpa, l, which):
        sqb, ksq = r2k.get(1024, BF16)
        S.op("act", lambda e: e.activation(out=sqb, in_=pa, func=AF.Square), r=kpa, w=ksq)
        qg, kqg = r2k.get(1024, BF16)
        S.op("act", lambda e: e.activation(out=qg, in_=pa, func=AF.Copy, scale=gq2[l][0][:, which:which + 1]),
             r=kpa + gq2[l][1], w=kqg)
        return sqb, ksq, qg, kqg

    def qk_epilogue(l, sqb, ksq, qg, kqg, which, h, th, tokg0):
        px1, kpx1 = psv(PX, "PX", 0, 1024)
        px2, kpx2 = psv(PX, "PX", 1024, 1024)

        def mm1(e):
            for tb in range(2):
                ins = e.matmul(px1[:, tb * 512:(tb + 1) * 512], lhsT=ones, rhs=sqb[:, tb * 512:(tb + 1) * 512], start=True, stop=True)
            return ins
        S.op("pe", mm1, r=ksq + k_ones, w=kpx1)

        def mm2(e):
            for tb in range(2):
                ins = e.matmul(px2[:, tb * 512:(tb + 1) * 512], lhsT=swp, rhs=qg[:, tb * 512:(tb + 1) * 512], start=True, stop=True)
            return ins
        S.op("pe", mm2, r=kqg + k_swp, w=kpx2)
        rs, krs = r4.get(1024, F32)
        rsqrt_tile(rs, krs, px1, kpx1, 1.0 / HD)
        t1, kt1 = r4.get(1024, F32)
        S.op("dve", lambda e: e.tensor_tensor(out=t1, in0=qg, in1=cosT[:, th * 1024:(th + 1) * 1024], op=ALU.mult),
             r=kqg + k_cos, w=kt1)
        t2, kt2 = r4.get(1024, F32)
        S.op("dve", lambda e: e.tensor_tensor(out=t2, in0=px2, in1=sinT[:, th * 1024:(th + 1) * 1024], op=ALU.mult),
             r=kpx2 + k_sin, w=kt2)
        S.op("dve", lambda e: e.tensor_tensor(out=t1, in0=t1, in1=t2, op=ALU.add), r=kt1 + kt2, w=kt1)
        qo, kqo = r4.get(1024, BF16)
        S.op("dve", lambda e: e.tensor_tensor(out=qo, in0=t1, in1=rs, op=ALU.mult), r=kt1 + krs, w=kqo)
        if which == 0:
            S.dma("sp", lambda e: e.dma_start(out=QT[h][:, th * 1024:(th + 1) * 1024], in_=qo), r=kqo, w=[("QT", h)])
        else:
            if pair:
                kdst = XBK[h // 4][(h % 4) * 128:(h % 4 + 1) * 128, th * 1024:(th + 1) * 1024]
            else:
                kdst = KT[h][:, tokg0 + th * 1024: tokg0 + (th + 1) * 1024]
            S.dma("sp", lambda e: e.dma_start(out=kdst, in_=qo), r=kqo, w=[("KT", h)])

    def gate_epilogue(pa, kpa, dst, idx, th):
        sg, ksg = r4.get(1024, BF16)
        S.op("act", lambda e: e.activation(out=sg, in_=pa, func=AF.Silu), r=kpa, w=ksg)
        S.dma("sp", lambda e: e.dma_start(out=dst[idx][:, th * 1024:(th + 1) * 1024], in_=sg), r=ksg,
              w=[(dst.tensor.name, idx)])

    def phase2(l, seg):
        tokg0 = seg * TS
        wl = w_in[l]

        def rhs_h(c, t0, n):
            return hT(c, t0, n)

        blocks = []
        for i in range(4):
            blocks += [("q", i), ("k", i), ("v", i)]
        blocks += [("ga", i) for i in range(4)]
        for i in range(4):
            blocks += [("gb", i), ("ua", i)]
        blocks += [("gc", i) for i in range(4)]
        colbase = {"q": 0, "k": 2048, "v": 4096, "ga": 6144, "ua": 8192, "gb": 10240, "gc": 12288}
        loaded = {}

        def prefetch(bi):
            if bi < len(blocks) and bi not in loaded:
                kind, i = blocks[bi]
                loaded[bi] = load_w(wl, colbase[kind] + i * 512)

        units = []
        for bi, (kind, i) in enumerate(blocks):
            first = [True]

            def pre(bi=bi, first=first):
                if first[0]:
                    first[0] = False
                    prefetch(bi)
                    prefetch(bi + 1)
                    prefetch(bi + 2)
                return loaded[bi]
            for j in range(4):
                for th in range(2):
                    def mmw(pre=pre, j=j, th=th, kind=kind):
                        wb3, kw = pre()
                        pa, kpa = ws_unit(wb3, kw, j, th, rhs_h)
                        if kind in ("q", "k"):
                            return qk_part1(pa, kpa, l, 0 if kind == "q" else 1)
                        return pa, kpa
                    if kind in ("q", "k"):
                        def ep(st, kind=kind, h=i * 4 + j, th=th, i=i, j=j):
                            qk_epilogue(l, st[0], st[1], st[2], st[3], 0 if kind == "q" else 1, h, th, tokg0)
                            if pair and kind == "k" and j == 3 and th == 1:
                                S.cc(lambda e: e.collective_compute("AllGather", ALU.bypass, replica_groups=RGS, ins=[XBK[i][:, :]],
                                                                    outs=[XGK[i][:, :]]),
                                     r=[("KT", hh) for hh in range(4 * i, 4 * i + 4)], w=[("XGK", i)])
                    elif kind == "v":
                        def ep(st, h=i * 4 + j, th=th, i=i, j=j):
                            vt, kvt = r4.get(1024, BF16)
                            S.op("act", lambda e: e.activation(out=vt, in_=st[0], func=AF.Copy), r=st[1], w=kvt)
                            if pair:
                                vdst = XBV[i][j * 128:(j + 1) * 128, th * 1024:(th + 1) * 1024]
                            else:
                                vdst = V[h][:, tokg0 + th * 1024: tokg0 + (th + 1) * 1024]
                            S.dma("sp", lambda e: e.dma_start(out=vdst, in_=vt), r=kvt, w=[("V", h)])
                            if pair and j == 3 and th == 1:
                                S.cc(lambda e: e.collective_compute("AllGather", ALU.bypass, replica_groups=RGS, ins=[XBV[i][:, :]],
                                                                    outs=[XGV[i][:, :]]),
                                     r=[("V", hh) for hh in range(4 * i, 4 * i + 4)], w=[("XGV", i)])
                    elif kind in ("ga", "gc"):
                        def ep(st, kind=kind, idx=i * 4 + j, th=th):
                            gate_epilogue(st[0], st[1], SGA if kind == "ga" else SGC, idx, th)
                    elif kind == "gb":
                        def ep(st, j=j, th=th):
                            sb, ksb = WR.view(49152 + (j * 2 + th) * 2048, 1024, BF16)
                            S.op("act", lambda e: e.activation(out=sb, in_=st[0], func=AF.Sigmoid), r=st[1], w=ksb)
                    else:
                        def ep(st, c=i * 4 + j, j=j, th=th):
                            sb, ksb = WR.view(49152 + (j * 2 + th) * 2048, 1024, BF16)
                            ut, kut = r4.get(1024, BF16)
                            S.op("dve", lambda e: e.tensor_tensor(out=ut, in0=st[0], in1=sb, op=ALU.mult), r=st[1] + ksb, w=kut)
                            S.dma("sp", lambda e: e.dma_start(out=U[c][:, th * 1024:(th + 1) * 1024], in_=ut), r=kut, w=[("U", c)])
                            if pair and th == 1:
                                S.dma("sp", lambda e: e.dma_start(out=XBH[c * 128:(c + 1) * 128, :], in_=ut[:, 992:1024]),
                                      r=kut, w=[("XBH", c)])
                    units.append((mmw, ep))
        prev = None
        for (mmf, epf) in units:
            st = mmf()
            if not PIPE:
                epf(st)
                continue
            if prev is not None:
                prev[0](prev[1])
            prev = (epf, st)
        if PIPE:
            prev[0](prev[1])

    def conv_prep(l, seg, c):
        ue, kue = CVR.view((c % 3) * 4160, 2080, BF16)
        if pair:
            S.dma("sp", lambda e: e.dma_start(out=ue[:, 0:32], in_=XGH[c * 128:(c + 1) * 128, :]), r=[("XGH",)], w=kue)
            S.op("pool", lambda e: e.tensor_scalar(out=ue[:, 0:32], in0=ue[:, 0:32], scalar1=cinfo[:, 2:3], scalar2=None,
                                                   op0=ALU.mult), r=kue + k_cinfo, w=kue)
        elif seg == 0:
            S.op("pool", lambda e: e.memset(ue[:, 0:32], 0.0), w=kue)
        else:
            S.dma("sp", lambda e: e.dma_start(out=ue[:, 0:32], in_=UH[c]), r=[("UH", c)], w=kue)
        S.dma("sp", lambda e: e.dma_start(out=ue[:, 32:32 + TS], in_=U[c]), r=[("U", c)], w=kue)
        if seg + 1 < nseg:
            S.dma("sp", lambda e: e.dma_start(out=UH[c], in_=ue[:, TS:TS + 32]), r=kue, w=[("UH", c)])
        dg, kdg = WR.view((c % 2) * 8192, CW * 128, BF16)
        dg3 = dg.rearrange("p (j n) -> p j n", j=CW)
        wv = pvs[l][0][:, 98 + c * CW: 98 + (c + 1) * CW]
        S.op("pool", lambda e: e.tensor_tensor(
            out=dg3, in0=ident.unsqueeze(1).broadcast_to([128, CW, 128]), in1=wv.unsqueeze(2).broadcast_to([128, CW, 128]),
            op=ALU.mult), r=k_ident + pvs[l][1], w=kdg)
        return ue, kue, dg3, kdg

    def conv_tile(l, seg, c, prep):
        ue, kue, dg3, kdg = prep
        for th in range(2):
            px, kpx = psv(PX, "PX", ((c * 2 + th) % 2) * 1024, 1024)

            def mm(e, px=px, th=th):
                for j in range(CW):
                    for tb in range(2):
                        o = 2 + j + th * 1024 + tb * 512
                        ins = e.matmul(px[:, tb * 512:(tb + 1) * 512], lhsT=dg3[:, j, :], rhs=ue[:, o:o + 512],
                                       start=(j == 0), stop=(j == CW - 1))
                return ins
            S.op("pe", mm, r=kdg + kue, w=kpx)
            yv, kyv = B0.view(c * 4096 + th * 2048, 1024, BF16)
            S.op("act", lambda e, yv=yv, px=px: e.activation(out=yv, in_=px, func=AF.Identity, bias=pvcol(l, G_DWB, c),
                                                             scale=1.0), r=kpx + pvs[l][1], w=kyv)

    def phase3(l, seg):
        koff = TS if (seg > 0 or pair) else 0
        nk = koff + TS
        hb = 1 if (seg > 0 or pair) else 0
        scale = HD ** -0.5
        small = Ring(R4, 2048, 12)
        vbase = {}
        nslot = 0
        for d in (1, 4, 16):
            nb = TS // (128 * d)
            for r in range(d):
                vbase[(d, r)] = nslot
                nslot += nb + hb
        assert nslot * 256 <= 24576 and 49152 + 2 * 8192 <= 65536
        def head_bufs(h):
            s = h % 2
            qT, kq = B0.view(s * 28672, TS, BF16)
            kT, kk = B0.view(s * 28672 + 4096, nk, BF16)
            acc, ka = B0.view(s * 28672 + 12288, 2 * TS, F32)
            vb, kvb = WR.view(s * 24576, nslot * 128, BF16)
            return qT, kq, kT, kk, acc, ka, vb, kvb

        def load_head(h):
            qT, kq, kT, kk, acc, ka, vb, kvb = head_bufs(h)
            vb3 = vb.rearrange("p (m f) -> p m f", f=128)
            S.dma("sp", lambda e: e.dma_start(out=qT, in_=QT[h]), r=[("QT", h)], w=kq)
            hs = slice((h % 4) * 128, (h % 4 + 1) * 128)
            if pair:
                S.dma("sp", lambda e: e.dma_start(out=kT[:, 0:TS], in_=XGK[h // 4][hs, :]), r=[("XGK", h // 4)], w=kk)
                S.dma("sp", lambda e: e.dma_start(out=kT[:, TS:2 * TS], in_=XBK[h // 4][hs, :]), r=[("KT", h)], w=kk)
            else:
                S.dma("sp", lambda e: e.dma_start(out=kT, in_=KT[h][:, seg * TS - koff: seg * TS + TS]), r=[("KT", h)], w=kk)
            vT, kvt = WR.view(49152 + (h % 2) * 8192, nk, BF16)
            if pair:
                S.dma("sp", lambda e: e.dma_start(out=vT[:, 0:TS], in_=XGV[h // 4][hs, :]), r=[("XGV", h // 4)], w=kvt)
                S.dma("sp", lambda e: e.dma_start(out=vT[:, TS:2 * TS], in_=XBV[h // 4][hs, :]), r=[("V", h)], w=kvt)
            else:
                S.dma("sp", lambda e: e.dma_start(out=vT, in_=V[h][:, seg * TS - koff: seg * TS + TS]), r=[("V", h)], w=kvt)

        slot_src = []
        for d in (1, 4, 16):
            for r in range(d):
                for m in range(TS // (128 * d) + hb):
                    slot_src.append((koff - hb * 128 * d + r + m * 128 * d, d))
        assert len(slot_src) == nslot
        NG = (nslot + 7) // 8
        tbank = [0]

        def prep_v(h, g):
            qT, kq, kT, kk, acc, ka, vb, kvb = head_bufs(h)
            vT, kvt = WR.view(49152 + (h % 2) * 8192, nk, BF16)
            s0, s1 = 8 * g, min(nslot, 8 * g + 8)
            bk = 2 + tbank[0] % 2
            tbank[0] += 1
            pt, kpt = psv(PS, "PS", bk * 512, 512, BF16)

            def tr(e):
                for k, sl in enumerate(range(s0, s1)):
                    c0, d = slot_src[sl]
                    ins = e.transpose(out=pt[:, k * 128:(k + 1) * 128], in_=vT[:, c0: c0 + 127 * d + 1: d], identity=ident)
                return ins
            S.op("pe", tr, r=kvt + k_ident, w=kpt)
            n = (s1 - s0) * 128
            dstv = vb[:, s0 * 128: s0 * 128 + n]
            kd = [(WR.name, gg) for gg in range(((h % 2) * 24576 + s0 * 256) // GRAN, ((h % 2) * 24576 + s1 * 256 - 1) // GRAN + 1)]
            if g % 2 == 0:
                S.op("act", lambda e: e.activation(out=dstv, in_=pt[:, 0:n], func=AF.Copy), r=kpt, w=kd)
            else:
                S.op("dve", lambda e: e.tensor_copy(out=dstv, in_=pt[:, 0:n]), r=kpt, w=kd)

        pending_norm = []
        load_head(0)
        for g in range(NG):
            prep_v(0, g)
        for h in range(NH):
            if h + 1 < NH:
                load_head(h + 1)
            qT, kq, kT, kk, acc, ka, vb, kvb = head_bufs(h)
            acc3 = acc.rearrange("p (a n) -> p a n", a=2)
            vb3 = vb.rearrange("p (m f) -> p m f", f=128)
            ulist = []
            for d in (1, 4, 16):
                nbo = TS // (128 * d)
                for r in range(d):
                    for n in range(nbo):
                        ulist.append((d, r, n))
            pslot = [0, 0]
            state = {}

            def stage1(u, h=h, qT=qT, kT=kT, kq=kq, kk=kk):
                d, r, n = u
                bq = n * 128 * d + r
                has_prev = (koff + bq - 128 * d) >= 0
                qv = qT[:, bq: bq + 127 * d + 1: d]
                kc = kT[:, koff + bq: koff + bq + 127 * d + 1: d]
                ps, kps = psv(PX, "PX", (pslot[0] % 3) * 512, 256)
                pslot[0] += 1
                c0 = 0 if has_prev else 128

                def mm1(e):
                    if has_prev:
                        kp = kT[:, koff + bq - 128 * d: koff + bq - d + 1: d]
                        e.matmul(ps[:, 0:128], lhsT=kp, rhs=qv, start=True, stop=True)
                    return e.matmul(ps[:, 128:256], lhsT=kc, rhs=qv, start=True, stop=True)
                S.op("pe", mm1, r=kq + kk, w=kps)
                pt, kpt = small.get(256, BF16)
                if pair and n == 0:
                    S.op("act", lambda e: e.activation(out=pt[:, 0:128], in_=ps[:, 0:128], func=AF.Exp, bias=cinfo[:, 1:2], scale=scale),
                         r=kps + k_cinfo, w=kpt)
                    S.op("act", lambda e: e.activation(out=pt[:, 128:256], in_=ps[:, 128:256], func=AF.Exp, scale=scale), r=kps, w=kpt)
                else:
                    S.op("act", lambda e: e.activation(out=pt[:, c0:256], in_=ps[:, c0:256], func=AF.Exp, scale=scale), r=kps, w=kpt)
                S.op("pool", lambda e: e.tensor_tensor(out=pt[:, c0:256], in0=pt[:, c0:256], in1=mask2[:, c0:256], op=ALU.mult),
                     r=kpt + k_mask, w=kpt)
                state[u] = (pt, kpt, has_prev, bq)

            def stage2(u, vb3=vb3, kvb=kvb, acc3=acc3, ka=ka):
                d, r, n = u
                pt, kpt, has_prev, bq = state.pop(u)
                pi_ = pslot[1] % 3
                po, kpo = psv(PX, "PX", 1536, 256) if pi_ == 0 else psv(PS, "PS", (pi_ - 1) * 512, 256)
                pslot[1] += 1
                sl = vbase[(d, r)] + n + hb

                def mm2(e):
                    if has_prev:
                        e.matmul(po[:, 0:128], lhsT=vb3[:, sl - 1, :], rhs=pt[:, 0:128], start=True, stop=False)
                    e.matmul(po[:, 0:128], lhsT=vb3[:, sl, :], rhs=pt[:, 128:256], start=(not has_prev), stop=True)
                    if has_prev:
                        e.matmul(po[:, 128:256], lhsT=ones, rhs=pt[:, 0:128], start=True, stop=False)
                    return e.matmul(po[:, 128:256], lhsT=ones, rhs=pt[:, 128:256], start=(not has_prev), stop=True)
                S.op("pe", mm2, r=kpt + kvb + k_ones, w=kpo)
                po3 = po.rearrange("p (a n) -> p a n", a=2)
                av = acc3[:, :, bq: bq + 127 * d + 1: d]
                if d == 1:
                    S.op("dve", lambda e: e.tensor_copy(out=av, in_=po3), r=kpo, w=ka)
                else:
                    S.op("dve", lambda e: e.tensor_tensor(out=av, in0=po3, in1=av, op=ALU.add), r=kpo + ka, w=ka)
            LA = 3
            gnext = 0
            for idx in range(len(ulist) + LA):
                if idx < len(ulist):
                    stage1(ulist[idx])
                if idx >= LA:
                    stage2(ulist[idx - LA])
                if idx == 5 and pending_norm:
                    pending_norm.pop(0)()
                if h + 1 < NH and idx >= 6 and idx % 4 == 0 and gnext < NG:
                    prep_v(h + 1, gnext)
                    gnext += 1
            while h + 1 < NH and gnext < NG:
                prep_v(h + 1, gnext)
                gnext += 1
            def normalise(acc=acc, ka=ka, h=h):
                S.op("act", lambda e: e.activation(out=acc[:, TS:2 * TS], in_=acc[:, TS:2 * TS], func=AF.Ln), r=ka, w=ka)
                S.op("act", lambda e: e.activation(out=acc[:, TS:2 * TS], in_=acc[:, TS:2 * TS], func=AF.Exp, scale=-1.0), r=ka, w=ka)
                at, kat = r8.get(TS, BF16)
                S.op("dve", lambda e: e.tensor_tensor(out=at, in0=acc[:, 0:TS], in1=acc[:, TS:2 * TS], op=ALU.mult), r=ka, w=kat)
                S.dma("sp", lambda e: e.dma_start(out=ATT[h], in_=at), r=kat, w=[("ATT", h)])
            pending_norm.append(normalise)
        while pending_norm:
            pending_norm.pop(0)()

    def colsum_sq(src_fn, nsrc, pt, ptname, pbase):
        for c in range(nsrc):
            sv, ksv = src_fn(c)
            sq, ksq = r4.get(TS, BF16)
            S.op("act", lambda e, sq=sq, sv=sv: e.activation(out=sq, in_=sv, func=AF.Square), r=ksv, w=ksq)
            pk = [(ptname, b) for b in range(pbase // 512, pbase // 512 + 4)]

            def mm(e, sq=sq, c=c):
                for tb in range(4):
                    ins = e.matmul(pt[:, pbase + tb * 512: pbase + (tb + 1) * 512], lhsT=ones, rhs=sq[:, tb * 512:(tb + 1) * 512],
                                   start=(c == 0), stop=(c == nsrc - 1))
                return ins
            S.op("pe", mm, r=ksq + k_ones, w=pk)

    def gate_finalize(l, src_fn, rstd, krstd, ggrp, sgsrc, aybase):
        for c in range(16):
            sv, ksv = src_fn(c)
            sg, ksg = r4.get(TS, BF16)
            S.dma("sp", lambda e, sg=sg, c=c: e.dma_start(out=sg, in_=sgsrc[c]), r=[(sgsrc.tensor.name, c)], w=ksg)
            t, kt = r8.get(TS, F32)
            S.op("dve", lambda e, t=t, sv=sv, c=c: e.scalar_tensor_tensor(out=t, in0=sv, scalar=pvcol(l, ggrp, c), in1=rstd,
                                                                         op0=ALU.mult, op1=ALU.mult),
                 r=ksv + krstd + pvs[l][1], w=kt)
            o, ko = r4.get(TS, BF16)
            S.op("pool", lambda e, o=o, t=t, sg=sg: e.tensor_tensor(out=o, in0=t, in1=sg, op=ALU.mult), r=kt + ksg, w=ko)
            S.dma("act", lambda e, o=o, c=c: e.dma_start(out=AY[aybase + c], in_=o), r=ko, w=[("AY", aybase + c)])

    def finalize_one(l, c, sv, ksv, rstd, krstd, ggrp, sgsrc, aybase):
        sg, ksg = r4.get(TS, BF16)
        S.dma("sp", lambda e: e.dma_start(out=sg, in_=sgsrc[c]), r=[(sgsrc.tensor.name, c)], w=ksg)
        t, kt = r8.get(TS, F32)
        S.op("dve", lambda e: e.scalar_tensor_tensor(out=t, in0=sv, scalar=pvcol(l, ggrp, c), in1=rstd, op0=ALU.mult, op1=ALU.mult),
             r=ksv + krstd + pvs[l][1], w=kt)
        o, ko = r4.get(TS, BF16)
        S.op("dve", lambda e: e.tensor_tensor(out=o, in0=t, in1=sg, op=ALU.mult), r=kt + ksg, w=ko)
        S.dma("act", lambda e: e.dma_start(out=AY[aybase + c], in_=o), r=ko, w=[("AY", aybase + c)])

    def phase4_conv(l, seg):
        for h in range(NH):
            sv, ksv = r4.get(TS, BF16)
            S.dma("sp", lambda e, sv=sv, h=h: e.dma_start(out=sv, in_=ATT[h]), r=[("ATT", h)], w=ksv)
            sq, ksq = r4.get(TS, BF16)
            S.op("act", lambda e, sq=sq, sv=sv: e.activation(out=sq, in_=sv, func=AF.Square), r=ksv, w=ksq)

            def mm(e, sq=sq, h=h):
                for tb in range(4):
                    ins = e.matmul(PX[:, tb * 512:(tb + 1) * 512], lhsT=ones, rhs=sq[:, tb * 512:(tb + 1) * 512],
                                   start=(h == 0), stop=(h == NH - 1))
                return ins
            S.op("pe", mm, r=ksq + k_ones, w=[("PX", b_) for b_ in range(4)])
        rstd, krstd = WR.view(49152, TS, F32)
        rsqrt_tile(rstd, krstd, PX[:, 0:TS], [("PX", b_) for b_ in range(4)], 1.0 / D)
        prep = conv_prep(l, seg, 0)
        for c in range(NCH):
            nprep = conv_prep(l, seg, c + 1) if c + 1 < NCH else None
            conv_tile(l, seg, c, prep)
            prep = nprep
            sv, ksv = r4.get(TS, BF16)
            S.dma("sp", lambda e, sv=sv, c=c: e.dma_start(out=sv, in_=ATT[c]), r=[("ATT", c)], w=ksv)
            finalize_one(l, c, sv, ksv, rstd, krstd, G_ATTO, SGA, 0)

    def phase5(l):
        def src(c):
            return B0.view(c * 4096, TS, BF16)
        for c in range(NCH):
            sv, ksv = src(c)

            def mm(e, sv=sv, c=c):
                for tb in range(4):
                    ins = e.matmul(PX[:, tb * 512:(tb + 1) * 512], lhsT=ones, rhs=sv[:, tb * 512:(tb + 1) * 512],
                                   start=(c == 0), stop=(c == NCH - 1))
                return ins
            S.op("pe", mm, r=ksv + k_ones, w=[("PX", b) for b in range(4)])
        colsum_sq(src, NCH, PS, "PS", 0)
        mean, kmean = WR.view(57344, TS, F32)
        rstd, krstd = WR.view(49152, TS, F32)
        kpx = [("PX", b) for b in range(4)]
        kps = [("PS", b) for b in range(4)]
        S.op("dve", lambda e: e.tensor_scalar(out=mean, in0=PX[:, 0:TS], scalar1=1.0 / D, scalar2=None, op0=ALU.mult),
             r=kpx, w=kmean)
        msq, kmsq = r8.get(TS, F32)
        S.op("pool", lambda e: e.tensor_tensor(out=msq, in0=mean, in1=mean, op=ALU.mult), r=kmean, w=kmsq)
        S.op("dve", lambda e: e.scalar_tensor_tensor(out=rstd, in0=PS[:, 0:TS], scalar=1.0 / D, in1=msq, op0=ALU.mult,
                                                     op1=ALU.subtract), r=kps + kmsq, w=krstd)
        rsqrt_tile(rstd, krstd, rstd, krstd, 1.0)
        S.op("dve", lambda e: e.tensor_tensor(out=mean, in0=mean, in1=rstd, op=ALU.mult), r=kmean + krstd, w=kmean)
        for c in range(NCH):
            for th in range(2):
                sv, ksv = B0.view(c * 4096 + th * 2048, 1024, BF16)
                t, kt = r4.get(1024, F32)
                S.op("dve", lambda e, t=t, sv=sv, th=th: e.tensor_tensor(out=t, in0=sv, in1=rstd[:, th * 1024:(th + 1) * 1024],
                                                                        op=ALU.mult), r=ksv + krstd, w=kt)
                S.op("dve", lambda e, t=t, th=th: e.tensor_tensor(out=t, in0=t, in1=mean[:, th * 1024:(th + 1) * 1024],
                                                                  op=ALU.subtract), r=kt + kmean, w=kt)
                S.op("act", lambda e, t=t, sv=sv, c=c: e.activation(out=sv, in_=t, func=AF.Silu, bias=pvcol(l, G_LNB, c),
                                                                    scale=pvcol(l, G_LNG, c)), r=kt + pvs[l][1], w=ksv)
        def rhs_y(c, t0, n):
            return B0.view(c * 4096 + t0 * 2, n, BF16)
        pw_loaded = {}
        wslot[0] = 2

        def pw_pre(i):
            for ii in (i, i + 1):
                if ii < 4 and ii not in pw_loaded:
                    pw_loaded[ii] = load_w(w_pw[l], ii * 512)
            return pw_loaded[i]

        def pw_mm(i, j, th):
            wb3, kw = pw_pre(i)
            e_ = i * 4 + j
            sgt, ksgt = r2k.get(1024, BF16)
            S.dma("sp", lambda e: e.dma_start(out=sgt, in_=SGC[e_][:, th * 1024:(th + 1) * 1024]), r=[("SGC", e_)], w=ksgt)
            return ws_unit(wb3, kw, j, th, rhs_y, split=(i == 0 and j == 0)) + (sgt, ksgt)

        def pw_epi(st, e_, th):
            pa, kpa, sgt, ksgt = st
            o, ko = r4.get(1024, BF16)
            S.op("dve", lambda e: e.scalar_tensor_tensor(out=o, in0=pa, scalar=pvcol(l, G_CONVO, e_), in1=sgt, op0=ALU.mult,
                                                         op1=ALU.mult), r=kpa + ksgt + pvs[l][1], w=ko)
            S.dma("act", lambda e: e.dma_start(out=AY[16 + e_][:, th * 1024:(th + 1) * 1024], in_=o), r=ko, w=[("AY", 16 + e_)])
            sq, ksq = r4.get(1024, BF16)
            S.op("act", lambda e: e.activation(out=sq, in_=pa, func=AF.Square), r=kpa, w=ksq)

            def mms(e):
                for tb in range(2):
                    o_ = th * 1024 + tb * 512
                    ins = e.matmul(PX[:, o_:o_ + 512], lhsT=ones, rhs=sq[:, tb * 512:(tb + 1) * 512],
                                   start=(e_ == 0), stop=(e_ == NCH - 1))
                return ins
            S.op("pe", mms, r=ksq + k_ones, w=[("PX", th * 2), ("PX", th * 2 + 1)])
        prevu = None
        for i in range(4):
            for j in range(4):
                for th in range(2):
                    st = pw_mm(i, j, th)
                    if prevu is not None:
                        pw_epi(*prevu)
                    prevu = (st, i * 4 + j, th)
        pw_epi(*prevu)
        rsqrt_tile(rstd, krstd, PX[:, 0:TS], kpx, 1.0 / D)
        pr, kpr = psv(PS, "PS", 0, 16)

        def mmt(e):
            for tt in range(16):
                ins = e.matmul(pr[:, tt:tt + 1], lhsT=rstd[0:1, tt * 128:(tt + 1) * 128], rhs=one_f[0:1, 0:1], start=True, stop=True)
            return ins
        S.op("pe", mmt, r=krstd + k_onef, w=kpr)
        S.op("dve", lambda e: e.tensor_copy(out=rc16, in_=pr), r=kpr, w=k_rc16)

    def phase6(l, xsrc, xdst, tok0, hook=None):
        for th in range(2):
            ayv, kay = B0.view(0, 32 * 1024, BF16)
            ay3 = ayv.rearrange("p (c n) -> p c n", c=32)
            for c in range(32):
                v1, k1 = B0.view(c * 2048, 1024, BF16)
                S.dma("sp", lambda e, v1=v1, c=c, th=th: e.dma_start(out=v1, in_=AY[c][:, th * 1024:(th + 1) * 1024]),
                      r=[("AY", c)], w=k1)
            nxt = load_w(w_out[l], 0, nchunks=32, slots=2)
            for cb in range(4):
                wb3, kw = nxt
                if cb + 1 < 4:
                    nxt = load_w(w_out[l], (cb + 1) * 512, nchunks=32, slots=2)
                if hook is not None and th == 0 and cb == 0:
                    hook()
                for tt in range(8):
                    a = acc6[0]
                    acc6[0] = (a + 1) % 2
                    pa, kpa = psv(PS, "PS", a * 1024, 512)
                    pc, kpc = psv(PS, "PS", a * 1024 + 512, 512)
                    t0 = tok0 + th * 1024 + tt * 128
                    ttg = th * 8 + tt
                    xt, kxt = r4.get(512, F32)
                    S.dma("sp", lambda e, xt=xt, t0=t0, cb=cb: e.dma_start(out=xt, in_=xsrc[t0:t0 + 128, cb * 512:(cb + 1) * 512]),
                          w=kxt)

                    def mm(e, pa=pa, pc=pc, tt=tt, wb3=wb3):
                        for c in range(32):
                            ins = e.matmul(pa if c < 16 else pc, lhsT=ay3[:, c, tt * 128:(tt + 1) * 128], rhs=wb3[:, c, :],
                                           start=(c % 16 == 0), stop=(c % 16 == 15))
                        return ins
                    if cb == 0 and tt == 0:
                        for c in range(32):
                            S.op("pe", lambda e, pa=pa, pc=pc, tt=tt, wb3=wb3, c=c: e.matmul(
                                pa if c < 16 else pc, lhsT=ay3[:, c, tt * 128:(tt + 1) * 128], rhs=wb3[:, c, :],
                                start=(c % 16 == 0), stop=(c % 16 == 15)),
                                r=kw + B0.view(c * 2048, 1024, BF16)[1], w=(kpa if c < 16 else kpc))
                    else:
                        S.op("pe", mm, r=kw + kay, w=kpa + kpc)
                    ot, kot = r4.get(512, F32)
                    S.op("dve", lambda e, ot=ot, pc=pc, xt=xt, ttg=ttg: e.scalar_tensor_tensor(
                        out=ot, in0=pc, scalar=rc16[:, ttg:ttg + 1], in1=xt, op0=ALU.mult, op1=ALU.add), r=kpc + kxt + k_rc16, w=kot)
                    S.op("dve", lambda e, ot=ot, pa=pa: e.tensor_tensor(out=ot, in0=pa, in1=ot, op=ALU.add), r=kpa + kot, w=kot)
                    S.dma("act", lambda e, ot=ot, t0=t0, cb=cb: e.dma_start(out=xdst[t0:t0 + 128, cb * 512:(cb + 1) * 512], in_=ot),
                          r=kot, w=[("XD", l, t0 // TS)])

    done = False
    for l in range(DEPTH):
        xsrc = x_in if l == 0 else X1
        xdst = X1 if l == 0 else out
        for seg in range(nseg):
            if nseg > 1 or l == 0:
                rope_tables(seg)
            if l == 0 or nseg > 1:
                build_gb(l)
            phase1(l, xsrc, seg * TS)
            if stop_after == "p1":
                done = True
                break
            phase2(l, seg)
            if stop_after == "p2":
                done = True
                break
            if pair:
                S.cc(lambda e: e.collective_compute("AllGather", ALU.bypass, replica_groups=RGS, ins=[XBH[:, :]], outs=[XGH[:, :]]),
                     r=[("XBH", c) for c in range(NCH)], w=[("XGH",)])
            phase3(l, seg)
            if stop_after == "p3":
                done = True
                break
            phase4_conv(l, seg)
            if debug:
                for c in range(NCH):
                    S.dma("sp", lambda e, c=c: e.dma_start(out=Y[c], in_=B0.view(c * 4096, TS, BF16)[0]),
                          r=B0.view(c * 4096, TS, BF16)[1], w=[("Y", c)])
            phase5(l)
            if stop_after == "p5":
                done = True
                break
            hook = (lambda l=l: build_gb(l + 1)) if (l + 1 < DEPTH and seg == nseg - 1) else None
            phase6(l, xsrc, xdst, seg * TS, hook)
        if done or stop_after == "l0":
            break
    S.emit()
    return nc


def host_consts():
    c = np.zeros((128, 512), np.float32)
    idx = np.arange(128)
    c[idx, idx] = 1.0
    c[(idx + 64) % 128, 128 + idx] = 1.0
    kk = idx[:, None]
    qq = idx[None, :]
    c[:, 256:384] = (kk >= qq).astype(np.float32)
    c[:, 384:512] = (kk <= qq).astype(np.float32)
    return c


def pack_params(norm_g, att_out_g, conv_out_g, dw_bias, conv_ln_g, conv_ln_b, q_norm_g, k_norm_g, dw_kernel):
    pv = np.zeros((DEPTH, 128, NPV), np.float32)
    for l in range(DEPTH):
        for gi, a in enumerate((norm_g, att_out_g, conv_out_g, dw_bias, conv_ln_g, conv_ln_b)):
            pv[l, :, gi * 16:(gi + 1) * 16] = a[l].reshape(16, 128).T
        pv[l, :, 96] = q_norm_g[l]
        pv[l, :, 97] = k_norm_g[l]
        pv[l, :, 98:] = dw_kernel[l].reshape(CW, 16, 128).transpose(2, 1, 0).reshape(128, 16 * CW)
    return pv


def kernel(x, norm_g, w_in, q_norm_g, k_norm_g, dw_kernel, dw_bias, conv_ln_g, conv_ln_b, w_pw, att_out_g, conv_out_g, w_out):
    x = np.asarray(x, np.float32)
    B, SEQ = x.shape[0], x.shape[1]
    assert SEQ == 2 * TS and B == 4
    pv = pack_params(*[np.asarray(a, np.float32) for a in (norm_g, att_out_g, conv_out_g, dw_bias, conv_ln_g, conv_ln_b,
                                                             q_norm_g, k_norm_g, dw_kernel)])
    cst = host_consts()
    w_in = np.ascontiguousarray(w_in, np.float32)
    w_pw = np.ascontiguousarray(w_pw, np.float32)
    w_out = np.ascontiguousarray(w_out, np.float32)
    nc = build(1, pair=True)
    in_maps = []
    for core in range(8):
        b, half = core // 2, core % 2
        ci = np.zeros((128, 4), np.float32)
        ci[:, 0] = half * TS
        ci[:, 1] = 0.0 if half == 1 else -30000.0
        ci[:, 2] = float(half)
        in_maps.append({"x": np.ascontiguousarray(x[b, half * TS:(half + 1) * TS]), "w_in": w_in, "w_pw": w_pw, "w_out": w_out,
                        "pv": pv, "cst": cst, "cinfo": ci})
    res = run_bass_kernel_spmd(nc, in_maps, core_ids=list(range(8)))
    out = np.zeros((B, SEQ, D), np.float32)
    for core in range(8):
        b, half = core // 2, core % 2
        out[b, half * TS:(half + 1) * TS] = np.asarray(res.results[core]["out"]).reshape(TS, D)
    return out
```

```python
import math
import numpy as np
import concourse.bass as bass
import concourse.mybir as mybir
from concourse.bass_utils import run_bass_kernel_spmd

F32 = mybir.dt.float32
BF16 = mybir.dt.bfloat16
I32 = mybir.dt.int32
AF = mybir.ActivationFunctionType
ALU = mybir.AluOpType

D = 2048
DIN = 14336
TS = 2048
NCH = 16
HD = 128
NH = 16
CW = 31
EPS = 1e-6
DEPTH = 2
NPV = 6 * 16 + 2 + 16 * CW
GRAN = 2048
PIPE = True
RGS = [[0, 1], [2, 3], [4, 5], [6, 7]]


class Sched:
    NDS = 12

    def __init__(self, nc):
        self.nc = nc
        self.ops = []
        self.esem = {e: nc.alloc_semaphore("es_" + e) for e in ("pe", "act", "dve", "pool")}
        self.dsem = {q: [nc.alloc_semaphore("ds_%s%d" % (q, i)) for i in range(self.NDS)]
                     for q in ("sp", "pool", "act")}
        self.ccsem = nc.alloc_semaphore("ccsem")
        self.cccount = 0

    def op(self, eng, fn, r=(), w=()):
        self.ops.append(dict(eng=eng, fn=fn, r=tuple(r), w=tuple(w), dma=False))

    def dma(self, q, fn, r=(), w=()):
        self.ops.append(dict(eng=q, fn=fn, r=tuple(r), w=tuple(w), dma=True))

    def cc(self, fn, r=(), w=()):
        self.ops.append(dict(eng="pool", fn=fn, r=tuple(r), w=tuple(w), dma=True, cc=True))

    def analyse(self):
        ops = self.ops
        last_w = {}
        readers = {}
        for i, o in enumerate(ops):
            pr = tuple(k for k in o["r"] if k[0] in ("PS", "PX"))
            if pr:
                o["w"] = tuple(o["w"]) + pr
                o["r"] = tuple(k for k in o["r"] if k[0] not in ("PS", "PX"))
            deps = set()
            for k in o["r"]:
                if k in last_w:
                    deps.add(last_w[k])
            for k in o["w"]:
                if k in last_w:
                    deps.add(last_w[k])
                deps.update(readers.get(k, ()))
            deps.discard(i)
            o["deps"] = deps
            for k in o["r"]:
                readers.setdefault(k, []).append(i)
            for k in o["w"]:
                last_w[k] = i
                readers[k] = []
        for o in ops:
            o["ms"] = False
        for o in ops:
            for d in o["deps"]:
                p = ops[d]
                if p["dma"]:
                    continue
                if p["eng"] == "pe" and o["eng"] == "pe" and not o["dma"]:
                    continue
                p["ms"] = True
        cnt = {e: 0 for e in self.esem}
        dstate = {q: [0] * self.NDS for q in self.dsem}
        drr = {q: 0 for q in self.dsem}
        for o in ops:
            if o.get("cc"):
                o["dsem"] = self.ccsem
                o["guard"] = self.cccount
                self.cccount += 1
                o["target"] = self.cccount
            elif o["dma"]:
                q = o["eng"]
                j = drr[q]
                drr[q] = (j + 1) % self.NDS
                o["dsem"] = self.dsem[q][j]
                o["guard"] = dstate[q][j]
                dstate[q][j] += 16
                o["target"] = dstate[q][j]
            elif o["ms"]:
                cnt[o["eng"]] += 1
                o["count"] = cnt[o["eng"]]
        self.final_dma = {q: list(dstate[q]) for q in dstate}
        waited = {}
        for o in ops:
            e = o["eng"]
            wl = {}
            if o["dma"] and o["guard"] > 0:
                wl[id(o["dsem"])] = (o["dsem"], o["guard"])
            for d in sorted(o["deps"]):
                p = ops[d]
                if p["dma"]:
                    s, v = p["dsem"], p["target"]
                else:
                    if p["eng"] == "pe" and e == "pe" and not o["dma"]:
                        continue
                    s, v = self.esem[p["eng"]], p["count"]
                if id(s) not in wl or wl[id(s)][1] < v:
                    wl[id(s)] = (s, v)
            out = []
            wd = waited.setdefault(e, {})
            for sid, (s, v) in wl.items():
                if wd.get(sid, 0) >= v:
                    continue
                wd[sid] = v
                out.append((s, v))
            o["waits"] = out

    def emit(self):
        self.analyse()
        nc = self.nc
        per = {e: [] for e in ("sp", "act", "pe", "dve", "pool")}
        for o in self.ops:
            per[o["eng"]].append(o)

        def make_body(ename):
            def body(e):
                for o in per[ename]:
                    for (s, v) in o["waits"]:
                        e.wait_ge(s, v)
                    inst = o["fn"](e)
                    if o.get("cc"):
                        inst.then_inc(o["dsem"], 1)
                    elif o["dma"]:
                        inst.then_inc(o["dsem"], 16)
                    elif o["ms"]:
                        inst.then_inc(self.esem[ename], 1)
                if ename == "sp":
                    for q in self.dsem:
                        for j, s in enumerate(self.dsem[q]):
                            if self.final_dma[q][j] > 0:
                                e.wait_ge(s, self.final_dma[q][j])
                    if self.cccount > 0:
                        e.wait_ge(self.ccsem, self.cccount)
            return body

        with nc.Block() as block:
            block.sync(make_body("sp"))
            block.scalar(make_body("act"))
            block.tensor(make_body("pe"))
            block.vector(make_body("dve"))
            block.gpsimd(make_body("pool"))


class Region:
    def __init__(self, nc, name, nbytes):
        self.t = nc.alloc_sbuf_tensor(name, [128, nbytes // 4], F32)
        self.name = name
        self.nbytes = nbytes

    def view(self, off, n, dt):
        sz = 2 if dt == BF16 else 4
        assert off % 4 == 0 and (n * sz) % 4 == 0 and off + n * sz <= self.nbytes, (off, n, self.nbytes)
        ap = self.t[:, off // 4:(off + n * sz) // 4]
        if dt != F32:
            ap = ap.bitcast(dt)
        keys = [(self.name, g) for g in range(off // GRAN, (off + n * sz - 1) // GRAN + 1)]
        return ap, keys


def build(nseg, debug=False, stop_after=None, pair=False):
    nc = bass.Bass("TRN2", target_bir_lowering=False)
    S = Sched(nc)
    NT = nseg * TS
    okind = "ExternalOutput" if debug else "Internal"

    x_in = nc.dram_tensor("x", [NT, D], F32, kind="ExternalInput").ap()
    w_in = nc.dram_tensor("w_in", [DEPTH, D, DIN], F32, kind="ExternalInput").ap()
    w_pw = nc.dram_tensor("w_pw", [DEPTH, D, D], F32, kind="ExternalInput").ap()
    w_out = nc.dram_tensor("w_out", [DEPTH, 2 * D, D], F32, kind="ExternalInput").ap()
    pv_in = nc.dram_tensor("pv", [DEPTH, 128, NPV], F32, kind="ExternalInput").ap()
    cst_in = nc.dram_tensor("cst", [128, 512], F32, kind="ExternalInput").ap()
    cinfo_in = nc.dram_tensor("cinfo", [128, 4], F32, kind="ExternalInput").ap()
    out = nc.dram_tensor("out", [NT, D], F32, kind="ExternalOutput").ap()

    X1 = nc.dram_tensor("X1", [NT, D], F32, kind=okind).ap()
    QT = nc.dram_tensor("QT", [NH, 128, TS], BF16, kind=okind).ap()
    if pair:
        assert nseg == 1
        XBK = [nc.dram_tensor("XBK%d" % i, [512, TS], BF16).ap() for i in range(4)]
        XGK = [nc.dram_tensor("XGK%d" % i, [1024, TS], BF16).ap() for i in range(4)]
        XBV = [nc.dram_tensor("XBV%d" % i, [512, TS], BF16).ap() for i in range(4)]
        XGV = [nc.dram_tensor("XGV%d" % i, [1024, TS], BF16).ap() for i in range(4)]
        XBH = nc.dram_tensor("XBH", [D, 32], BF16).ap()
        XGH = nc.dram_tensor("XGH", [2 * D, 32], BF16).ap()
        KT = V = None
    else:
        KT = nc.dram_tensor("KT", [NH, 128, NT], BF16, kind=okind).ap()
        V = nc.dram_tensor("V", [NH, 128, NT], BF16, kind=okind).ap()
    SGA = nc.dram_tensor("SGA", [NH, 128, TS], BF16, kind="Internal").ap()
    SGC = nc.dram_tensor("SGC", [NCH, 128, TS], BF16, kind="Internal").ap()
    Y = nc.dram_tensor("Y", [NCH, 128, TS], BF16, kind=okind).ap()
    ATT = nc.dram_tensor("ATT", [NH, 128, TS], BF16, kind=okind).ap()
    CV = nc.dram_tensor("CV", [NCH, 128, TS], BF16, kind="Internal").ap()
    AY = nc.dram_tensor("AY", [2 * NCH, 128, TS], BF16, kind=okind).ap()
    U = nc.dram_tensor("U", [NCH, 128, TS], BF16, kind="Internal").ap()
    UH = nc.dram_tensor("UH", [NCH, 128, 32], BF16, kind="Internal").ap()

    B0 = Region(nc, "B0", 65536)
    WR = Region(nc, "WR", 65536)
    RP = Region(nc, "RP", 16384)
    R4 = Region(nc, "R4", 24576)
    R8 = Region(nc, "R8", 16384)
    MS = Region(nc, "MS", 8192)
    CVR = Region(nc, "CVR", 3 * 4160)
    PS = nc.alloc_psum_tensor("PSA", [128, 2048], F32)
    PX = nc.alloc_psum_tensor("PSX", [128, 2048], F32)

    def psv(t, name, off, n, dt=F32):
        ap = t[:, off:off + n]
        if dt == BF16:
            ap = ap.bitcast(BF16)
        keys = [(name, b) for b in range(off // 512, (off + n - 1) // 512 + 1)]
        return ap, keys

    class Ring:
        def __init__(self, reg, size, count, base=0):
            self.reg, self.size, self.count, self.i, self.base = reg, size, count, 0, base

        def get(self, n, dt):
            off = self.base + self.i * self.size
            self.i = (self.i + 1) % self.count
            return self.reg.view(off, n, dt)

    r4 = Ring(R4, 4096, 6)
    r8 = Ring(R8, 8192, 2)
    r2k = Ring(R8, 2048, 8)

    ident, k_ident = MS.view(0, 128, BF16)
    ones, k_ones = MS.view(256, 128, BF16)
    swp, k_swp = MS.view(512, 128, BF16)
    mask2, k_mask = MS.view(768, 256, BF16)
    pvs = [MS.view(2048 + l * 2560, NPV, F32) for l in range(DEPTH)]
    sc_off = 2048 + 2 * 2560
    sgn2, k_sgn2 = MS.view(sc_off, 1, F32)
    invf, k_invf = MS.view(sc_off + 4, 1, F32)
    gq2 = [MS.view(sc_off + 8 + 8 * l, 2, F32) for l in range(DEPTH)]
    ssq_t = [MS.view(sc_off + 64 + 8 * i, 1, F32) for i in range(2)]
    rstd_t = [MS.view(sc_off + 128 + 8 * i, 1, F32) for i in range(2)]
    negpi, k_negpi = MS.view(sc_off + 192, 1, F32)
    cinfo, k_cinfo = MS.view(sc_off + 256, 4, F32)
    rc16, k_rc16 = MS.view(sc_off + 448, 16, F32)
    one_f, k_onef = MS.view(sc_off + 512, 1, F32)
    cosT, k_cos = RP.view(0, TS, F32)
    sinT, k_sin = RP.view(8192, TS, F32)

    cst_f, k_cstf = r4.get(512, F32)
    S.dma("sp", lambda e: e.dma_start(out=cst_f, in_=cst_in[:, :]), w=k_cstf)
    S.op("dve", lambda e: e.tensor_copy(out=ident, in_=cst_f[:, 0:128]), r=k_cstf, w=k_ident)
    S.op("dve", lambda e: e.tensor_copy(out=swp, in_=cst_f[:, 128:256]), r=k_cstf, w=k_swp)
    S.op("dve", lambda e: e.tensor_copy(out=mask2, in_=cst_f[:, 256:512]), r=k_cstf, w=k_mask)
    S.op("dve", lambda e: e.memset(ones, 1.0), w=k_ones)
    S.op("dve", lambda e: e.memset(one_f, 1.0), w=k_onef)
    S.dma("sp", lambda e: e.dma_start(out=cinfo, in_=cinfo_in[:, :]), w=k_cinfo)
    S.op("dve", lambda e: e.memset(negpi, EPS), w=k_negpi)
    for l in range(DEPTH):
        S.dma("sp", lambda e, l=l: e.dma_start(out=pvs[l][0], in_=pv_in[l]), w=pvs[l][1])
        S.op("dve", lambda e, l=l: e.tensor_scalar(out=gq2[l][0], in0=pvs[l][0][:, 96:98], scalar1=1.0,
                                                   scalar2=None, op0=ALU.mult), r=pvs[l][1], w=gq2[l][1])
    pidx_i, k_pi = r4.get(1, I32)
    pidx_f, k_pf = r4.get(1, F32)
    S.op("pool", lambda e: e.iota(pidx_i, pattern=[[0, 1]], base=0, channel_multiplier=1), w=k_pi)
    S.op("dve", lambda e: e.tensor_copy(out=pidx_f, in_=pidx_i), r=k_pi, w=k_pf)
    S.op("dve", lambda e: e.tensor_scalar(out=sgn2, in0=pidx_f, scalar1=64.0, scalar2=2.0, op0=ALU.is_ge, op1=ALU.mult),
         r=k_pf, w=k_sgn2)
    S.op("dve", lambda e: e.tensor_scalar(out=sgn2, in0=sgn2, scalar1=-1.0, scalar2=None, op0=ALU.add), r=k_sgn2, w=k_sgn2)
    pm, k_pm = r4.get(1, F32)
    S.op("dve", lambda e: e.tensor_scalar(out=pm, in0=pidx_f, scalar1=64.0, scalar2=-64.0, op0=ALU.is_ge, op1=ALU.mult),
         r=k_pf, w=k_pm)
    S.op("dve", lambda e: e.tensor_tensor(out=pm, in0=pm, in1=pidx_f, op=ALU.add), r=k_pf + k_pm, w=k_pm)
    S.op("act", lambda e: e.activation(out=invf, in_=pm, func=AF.Exp, scale=-2.0 * math.log(10000.0) / 128.0),
         r=k_pm, w=k_invf)

    def pvcol(l, grp, c):
        return pvs[l][0][:, grp * 16 + c: grp * 16 + c + 1]
    G_NORM, G_ATTO, G_CONVO, G_DWB, G_LNG, G_LNB = range(6)

    def rsqrt_tile(dst, kdst, src, ksrc, scale):
        S.op("act", lambda e: e.activation(out=dst, in_=src, func=AF.Ln, bias=negpi, scale=scale), r=ksrc + k_negpi, w=kdst)
        S.op("act", lambda e: e.activation(out=dst, in_=dst, func=AF.Exp, scale=-0.5), r=kdst, w=kdst)

    def dwk(l, c, j):
        o = 98 + c * CW + j
        return pvs[l][0][:, o:o + 1]

    def rope_tables(seg):
        C1 = 6.28125
        C2 = 2 * math.pi - C1
        pos_i, k1 = B0.view(0, TS, I32)
        S.op("pool", lambda e: e.iota(pos_i, pattern=[[1, TS]], base=seg * TS, channel_multiplier=0), w=k1)
        ang, k2 = B0.view(8192, TS, F32)
        S.op("dve", lambda e: e.tensor_copy(out=ang, in_=pos_i), r=k1, w=k2)
        if pair:
            S.op("dve", lambda e: e.tensor_scalar(out=ang, in0=ang, scalar1=cinfo[:, 0:1], scalar2=None, op0=ALU.add),
                 r=k2 + k_cinfo, w=k2)
        S.op("dve", lambda e: e.tensor_scalar(out=ang, in0=ang, scalar1=invf, scalar2=None, op0=ALU.mult),
             r=k2 + k_invf, w=k2)
        for which, phase in ((0, 0.0), (1, math.pi / 2)):
            dst, kd = (sinT, k_sin) if which == 0 else (cosT, k_cos)
            base = 16384 + which * 24576
            angp, ka = B0.view(base, TS, F32)
            ki, kki = B0.view(base + 8192, TS, I32)
            kf, kkf = B0.view(base + 16384, TS, F32)
            S.op("dve", lambda e, angp=angp, phase=phase: e.tensor_scalar(out=angp, in0=ang, scalar1=phase, scalar2=None,
                                                                          op0=ALU.add), r=k2, w=ka)
            S.op("dve", lambda e, angp=angp, kf=kf: e.tensor_scalar(out=kf, in0=angp, scalar1=1.0 / (2 * math.pi), scalar2=None,
                                                                    op0=ALU.mult), r=ka, w=kkf)
            S.op("dve", lambda e, ki=ki, kf=kf: e.tensor_copy(out=ki, in_=kf), r=kkf, w=kki)
            S.op("dve", lambda e, ki=ki, kf=kf: e.tensor_copy(out=kf, in_=ki), r=kki, w=kkf)
            S.op("dve", lambda e, angp=angp, kf=kf: e.scalar_tensor_tensor(out=angp, in0=kf, scalar=-C1, in1=angp, op0=ALU.mult,
                                                                           op1=ALU.add), r=ka + kkf, w=ka)
            S.op("dve", lambda e, angp=angp, kf=kf: e.scalar_tensor_tensor(out=angp, in0=kf, scalar=-C2, in1=angp, op0=ALU.mult,
                                                                           op1=ALU.add), r=ka + kkf, w=ka)
            S.op("dve", lambda e, angp=angp, kf=kf: e.tensor_scalar(out=kf, in0=angp, scalar1=math.pi, scalar2=-2 * math.pi,
                                                                    op0=ALU.is_gt, op1=ALU.mult), r=ka, w=kkf)
            S.op("dve", lambda e, angp=angp, kf=kf: e.tensor_tensor(out=angp, in0=angp, in1=kf, op=ALU.add), r=ka + kkf, w=ka)
            S.op("dve", lambda e, angp=angp, kf=kf: e.tensor_scalar(out=kf, in0=angp, scalar1=-math.pi, scalar2=2 * math.pi,
                                                                    op0=ALU.is_lt, op1=ALU.mult), r=ka, w=kkf)
            S.op("dve", lambda e, angp=angp, kf=kf: e.tensor_tensor(out=angp, in0=angp, in1=kf, op=ALU.add), r=ka + kkf, w=ka)
            S.op("act", lambda e, angp=angp, dst=dst: e.activation(out=dst, in_=angp, func=AF.Sin), r=ka, w=kd)
        S.op("dve", lambda e: e.tensor_scalar(out=sinT, in0=sinT, scalar1=sgn2, scalar2=None, op0=ALU.mult),
             r=k_sin + k_sgn2, w=k_sin)

    def hT(c, t0, n):
        ap, keys = B0.view(c * 4096 + t0 * 2, n, BF16)
        return ap, keys

    def build_gb(l):
        gb, kgb = CVR.view(0, 16 * 128, BF16)
        for c in range(16):
            S.op("pool", lambda e, c=c: e.tensor_scalar(out=gb[:, c * 128:(c + 1) * 128], in0=ones, scalar1=pvcol(l, G_NORM, c),
                                                        scalar2=None, op0=ALU.mult), r=k_ones + pvs[l][1], w=kgb)

    def phase1(l, xsrc, tok0):
        ssq16, kss = MS.view(sc_off + 320, 16, F32)
        rstd16, krs = MS.view(sc_off + 384, 16, F32)
        S.op("dve", lambda e: e.memset(ssq16, 0.0), w=kss)
        gb, kgb = CVR.view(0, 16 * 128, BF16)
        hall, _ = B0.view(0, 16 * TS, BF16)
        h3 = hall.rearrange("p (c t) -> p c t", c=16)

        def load(tt):
            xt, kx = r8.get(D, F32)
            S.dma("sp", lambda e: e.dma_start(out=xt, in_=xsrc[tok0 + tt * 128: tok0 + (tt + 1) * 128, :]), w=kx)
            return xt, kx
        nxt = load(0)
        for tt in range(16):
            xt, kx = nxt
            junk, kj = r4.get(D, BF16)
            S.op("act", lambda e, xt=xt, junk=junk, tt=tt: e.activation(out=junk, in_=xt, func=AF.Square, accum_out=ssq16[:, tt:tt + 1]),
                 r=kx + kss, w=kj + [("ssq", tt)])
            S.op("act", lambda e, tt=tt: e.activation(out=rstd16[:, tt:tt + 1], in_=ssq16[:, tt:tt + 1], func=AF.Ln, bias=negpi, scale=1.0 / D),
                 r=[("ssq", tt)] + k_negpi, w=[("rstd", tt)])
            S.op("act", lambda e, tt=tt: e.activation(out=rstd16[:, tt:tt + 1], in_=rstd16[:, tt:tt + 1], func=AF.Exp, scale=-0.5),
                 r=[("rstd", tt)], w=[("rstd", tt)])
            xn, kxn = r4.get(D, BF16)
            S.op("act", lambda e, xt=xt, xn=xn, tt=tt: e.activation(out=xn, in_=xt, func=AF.Copy, scale=rstd16[:, tt:tt + 1]),
                 r=kx + [("rstd", tt)], w=kxn)
            if tt + 1 < 16:
                nxt = load(tt + 1)
            for g in range(4):
                pt, kpt = psv(PX, "PX", ((tt * 4 + g) % 4) * 512, 256, BF16)

                def tr(e, xn=xn, pt=pt, g=g):
                    for k in range(4):
                        c = g * 4 + k
                        ins = e.transpose(out=pt[:, k * 128:(k + 1) * 128], in_=xn[:, c * 128:(c + 1) * 128], identity=ident)
                    return ins
                S.op("pe", tr, r=kxn + k_ident, w=kpt)
                kh = []
                for k in range(4):
                    kh += hT(g * 4 + k, tt * 128, 128)[1]
                S.op("dve", lambda e, pt=pt, g=g, tt=tt: e.tensor_tensor(
                    out=h3[:, g * 4:(g + 1) * 4, tt * 128:(tt + 1) * 128], in0=pt.rearrange("p (c t) -> p c t", c=4),
                    in1=gb[:, g * 512:(g + 1) * 512].rearrange("p (c t) -> p c t", c=4), op=ALU.mult), r=kpt + kgb, w=kh)

    wslot = [0, 0]

    def load_w(src, col0, nchunks=16, ncols=512, slots=1):
        if slots == 1:
            i = wslot[0]
            wslot[0] = (i + 1) % 3
        else:
            i = wslot[1]
            wslot[1] = (i + 2) % 4
        wb, kw = WR.view(i * 16384, nchunks * ncols, BF16)
        wb3 = wb.rearrange("p (c n) -> p c n", c=nchunks)
        sv = src.rearrange("(c p) n -> p c n", p=128)
        for c0 in range(0, nchunks, 4):
            S.dma("pool", lambda e, c0=c0: e.dma_start(out=wb3[:, c0:c0 + 4, :], in_=sv[:, c0:c0 + 4, col0:col0 + ncols]),
                  w=kw)
        return wb3, kw

    acc_i = [0]
    acc6 = [0]

    def ws_unit(wb3, kw, j, th, rhs_fn, split=False):
        a = acc_i[0]
        acc_i[0] ^= 1
        pa, kpa = psv(PS, "PS", a * 1024, 1024)
        rk = []
        for c in range(16):
            rk += rhs_fn(c, th * 1024, 1024)[1]

        def mm(e):
            for c in range(16):
                for tb in range(2):
                    rv = rhs_fn(c, th * 1024 + tb * 512, 512)[0]
                    ins = e.matmul(pa[:, tb * 512:(tb + 1) * 512], lhsT=wb3[:, c, j * 128:(j + 1) * 128], rhs=rv,
                                   start=(c == 0), stop=(c == 15))
            return ins
        if not split:
            S.op("pe", mm, r=kw + rk, w=kpa)
        else:
            for c in range(16):
                def mmc(e, c=c):
                    for tb in range(2):
                        rv = rhs_fn(c, th * 1024 + tb * 512, 512)[0]
                        ins = e.matmul(pa[:, tb * 512:(tb + 1) * 512], lhsT=wb3[:, c, j * 128:(j + 1) * 128], rhs=rv,
                                       start=(c == 0), stop=(c == 15))
                    return ins
                S.op("pe", mmc, r=kw + rhs_fn(c, th * 1024, 1024)[1], w=kpa)
        return pa, kpa

    def qk_part1(pa, kpa, l, which):
        sqb, ksq = r2k.get(1024, BF16)
        S.op("act", lambda e: e.activation(out=sqb, in_=pa, func=AF.Square), r=kpa, w=ksq)
        qg, kqg = r2k.get(1024, BF16)
        S.op("act", lambda e: e.activation(out=qg, in_=pa, func=AF.Copy, scale=gq2[l][0][:, which:which + 1]),
             r=kpa + gq2[l][1], w=kqg)
        return sqb, ksq, qg, kqg

    def qk_epilogue(l, sqb, ksq, qg, kqg, which, h, th, tokg0):
        px1, kpx1 = psv(PX, "PX", 0, 1024)
        px2, kpx2 = psv(PX, "PX", 1024, 1024)

        def mm1(e):
            for tb in range(2):
                ins = e.matmul(px1[:, tb * 512:(tb + 1) * 512], lhsT=ones, rhs=sqb[:, tb * 512:(tb + 1) * 512], start=True, stop=True)
            return ins
        S.op("pe", mm1, r=ksq + k_ones, w=kpx1)

        def mm2(e):
            for tb in range(2):
                ins = e.matmul(px2[:, tb * 512:(tb + 1) * 512], lhsT=swp, rhs=qg[:, tb * 512:(tb + 1) * 512], start=True, stop=True)
            return ins
        S.op("pe", mm2, r=kqg + k_swp, w=kpx2)
        rs, krs = r4.get(1024, F32)
        rsqrt_tile(rs, krs, px1, kpx1, 1.0 / HD)
        t1, kt1 = r4.get(1024, F32)
        S.op("dve", lambda e: e.tensor_tensor(out=t1, in0=qg, in1=cosT[:, th * 1024:(th + 1) * 1024], op=ALU.mult),
             r=kqg + k_cos, w=kt1)
        t2, kt2 = r4.get(1024, F32)
        S.op("dve", lambda e: e.tensor_tensor(out=t2, in0=px2, in1=sinT[:, th * 1024:(th + 1) * 1024], op=ALU.mult),
             r=kpx2 + k_sin, w=kt2)
        S.op("dve", lambda e: e.tensor_tensor(out=t1, in0=t1, in1=t2, op=ALU.add), r=kt1 + kt2, w=kt1)
        qo, kqo = r4.get(1024, BF16)
        S.op("dve", lambda e: e.tensor_tensor(out=qo, in0=t1, in1=rs, op=ALU.mult), r=kt1 + krs, w=kqo)
        if which == 0:
            S.dma("sp", lambda e: e.dma_start(out=QT[h][:, th * 1024:(th + 1) * 1024], in_=qo), r=kqo, w=[("QT", h)])
        else:
            if pair:
                kdst = XBK[h // 4][(h % 4) * 128:(h % 4 + 1) * 128, th * 1024:(th + 1) * 1024]
            else:
                kdst = KT[h][:, tokg0 + th * 1024: tokg0 + (th + 1) * 1024]
            S.dma("sp", lambda e: e.dma_start(out=kdst, in_=qo), r=kqo, w=[("KT", h)])

    def gate_epilogue(pa, kpa, dst, idx, th):
        sg, ksg = r4.get(1024, BF16)
        S.op("act", lambda e: e.activation(out=sg, in_=pa, func=AF.Silu), r=kpa, w=ksg)
        S.dma("sp", lambda e: e.dma_start(out=dst[idx][:, th * 1024:(th + 1) * 1024], in_=sg), r=ksg,
              w=[(dst.tensor.name, idx)])

    def phase2(l, seg):
        tokg0 = seg * TS
        wl = w_in[l]

        def rhs_h(c, t0, n):
            return hT(c, t0, n)

        blocks = []
        for i in range(4):
            blocks += [("q", i), ("k", i), ("v", i)]
        blocks += [("ga", i) for i in range(4)]
        for i in range(4):
            blocks += [("gb", i), ("ua", i)]
        blocks += [("gc", i) for i in range(4)]
        colbase = {"q": 0, "k": 2048, "v": 4096, "ga": 6144, "ua": 8192, "gb": 10240, "gc": 12288}
        loaded = {}

        def prefetch(bi):
            if bi < len(blocks) and bi not in loaded:
                kind, i = blocks[bi]
                loaded[bi] = load_w(wl, colbase[kind] + i * 512)

        units = []
        for bi, (kind, i) in enumerate(blocks):
            first = [True]

            def pre(bi=bi, first=first):
                if first[0]:
                    first[0] = False
                    prefetch(bi)
                    prefetch(bi + 1)
                    prefetch(bi + 2)
                return loaded[bi]
            for j in range(4):
                for th in range(2):
                    def mmw(pre=pre, j=j, th=th, kind=kind):
                        wb3, kw = pre()
                        pa, kpa = ws_unit(wb3, kw, j, th, rhs_h)
                        if kind in ("q", "k"):
                            return qk_part1(pa, kpa, l, 0 if kind == "q" else 1)
                        return pa, kpa
                    if kind in ("q", "k"):
                        def ep(st, kind=kind, h=i * 4 + j, th=th, i=i, j=j):
                            qk_epilogue(l, st[0], st[1], st[2], st[3], 0 if kind == "q" else 1, h, th, tokg0)
                            if pair and kind == "k" and j == 3 and th == 1:
                                S.cc(lambda e: e.collective_compute("AllGather", ALU.bypass, replica_groups=RGS, ins=[XBK[i][:, :]],
                                                                    outs=[XGK[i][:, :]]),
                                     r=[("KT", hh) for hh in range(4 * i, 4 * i + 4)], w=[("XGK", i)])
                    elif kind == "v":
                        def ep(st, h=i * 4 + j, th=th, i=i, j=j):
                            vt, kvt = r4.get(1024, BF16)
                            S.op("act", lambda e: e.activation(out=vt, in_=st[0], func=AF.Copy), r=st[1], w=kvt)
                            if pair:
                                vdst = XBV[i][j * 128:(j + 1) * 128, th * 1024:(th + 1) * 1024]
                            else:
                                vdst = V[h][:, tokg0 + th * 1024: tokg0 + (th + 1) * 1024]
                            S.dma("sp", lambda e: e.dma_start(out=vdst, in_=vt), r=kvt, w=[("V", h)])
                            if pair and j == 3 and th == 1:
                                S.cc(lambda e: e.collective_compute("AllGather", ALU.bypass, replica_groups=RGS, ins=[XBV[i][:, :]],
                                                                    outs=[XGV[i][:, :]]),
                                     r=[("V", hh) for hh in range(4 * i, 4 * i + 4)], w=[("XGV", i)])
                    elif kind in ("ga", "gc"):
                        def ep(st, kind=kind, idx=i * 4 + j, th=th):
                            gate_epilogue(st[0], st[1], SGA if kind == "ga" else SGC, idx, th)
                    elif kind == "gb":
                        def ep(st, j=j, th=th):
                            sb, ksb = WR.view(49152 + (j * 2 + th) * 2048, 1024, BF16)
                            S.op("act", lambda e: e.activation(out=sb, in_=st[0], func=AF.Sigmoid), r=st[1], w=ksb)
                    else:
                        def ep(st, c=i * 4 + j, j=j, th=th):
                            sb, ksb = WR.view(49152 + (j * 2 + th) * 2048, 1024, BF16)
                            ut, kut = r4.get(1024, BF16)
                            S.op("dve", lambda e: e.tensor_tensor(out=ut, in0=st[0], in1=sb, op=ALU.mult), r=st[1] + ksb, w=kut)
                            S.dma("sp", lambda e: e.dma_start(out=U[c][:, th * 1024:(th + 1) * 1024], in_=ut), r=kut, w=[("U", c)])
                            if pair and th == 1:
                                S.dma("sp", lambda e: e.dma_start(out=XBH[c * 128:(c + 1) * 128, :], in_=ut[:, 992:1024]),
                                      r=kut, w=[("XBH", c)])
                    units.append((mmw, ep))
        prev = None
        for (mmf, epf) in units:
            st = mmf()
            if not PIPE:
                epf(st)
                continue
            if prev is not None:
                prev[0](prev[1])
            prev = (epf, st)
        if PIPE:
            prev[0](prev[1])

    def conv_prep(l, seg, c):
        ue, kue = CVR.view((c % 3) * 4160, 2080, BF16)
        if pair:
            S.dma("sp", lambda e: e.dma_start(out=ue[:, 0:32], in_=XGH[c * 128:(c + 1) * 128, :]), r=[("XGH",)], w=kue)
            S.op("pool", lambda e: e.tensor_scalar(out=ue[:, 0:32], in0=ue[:, 0:32], scalar1=cinfo[:, 2:3], scalar2=None,
                                                   op0=ALU.mult), r=kue + k_cinfo, w=kue)
        elif seg == 0:
            S.op("pool", lambda e: e.memset(ue[:, 0:32], 0.0), w=kue)
        else:
            S.dma("sp", lambda e: e.dma_start(out=ue[:, 0:32], in_=UH[c]), r=[("UH", c)], w=kue)
        S.dma("sp", lambda e: e.dma_start(out=ue[:, 32:32 + TS], in_=U[c]), r=[("U", c)], w=kue)
        if seg + 1 < nseg:
            S.dma("sp", lambda e: e.dma_start(out=UH[c], in_=ue[:, TS:TS + 32]), r=kue, w=[("UH", c)])
        dg, kdg = WR.view((c % 2) * 8192, CW * 128, BF16)
        dg3 = dg.rearrange("p (j n) -> p j n", j=CW)
        wv = pvs[l][0][:, 98 + c * CW: 98 + (c + 1) * CW]
        S.op("pool", lambda e: e.tensor_tensor(
            out=dg3, in0=ident.unsqueeze(1).broadcast_to([128, CW, 128]), in1=wv.unsqueeze(2).broadcast_to([128, CW, 128]),
            op=ALU.mult), r=k_ident + pvs[l][1], w=kdg)
        return ue, kue, dg3, kdg

    def conv_tile(l, seg, c, prep):
        ue, kue, dg3, kdg = prep
        for th in range(2):
            px, kpx = psv(PX, "PX", ((c * 2 + th) % 2) * 1024, 1024)

            def mm(e, px=px, th=th):
                for j in range(CW):
                    for tb in range(2):
                        o = 2 + j + th * 1024 + tb * 512
                        ins = e.matmul(px[:, tb * 512:(tb + 1) * 512], lhsT=dg3[:, j, :], rhs=ue[:, o:o + 512],
                                       start=(j == 0), stop=(j == CW - 1))
                return ins
            S.op("pe", mm, r=kdg + kue, w=kpx)
            yv, kyv = B0.view(c * 4096 + th * 2048, 1024, BF16)
            S.op("act", lambda e, yv=yv, px=px: e.activation(out=yv, in_=px, func=AF.Identity, bias=pvcol(l, G_DWB, c),
                                                             scale=1.0), r=kpx + pvs[l][1], w=kyv)

    def phase3(l, seg):
        koff = TS if (seg > 0 or pair) else 0
        nk = koff + TS
        hb = 1 if (seg > 0 or pair) else 0
        scale = HD ** -0.5
        small = Ring(R4, 2048, 12)
        vbase = {}
        nslot = 0
        for d in (1, 4, 16):
            nb = TS // (128 * d)
            for r in range(d):
                vbase[(d, r)] = nslot
                nslot += nb + hb
        assert nslot * 256 <= 24576 and 49152 + 2 * 8192 <= 65536
        def head_bufs(h):
            s = h % 2
            qT, kq = B0.view(s * 28672, TS, BF16)
            kT, kk = B0.view(s * 28672 + 4096, nk, BF16)
            acc, ka = B0.view(s * 28672 + 12288, 2 * TS, F32)
            vb, kvb = WR.view(s * 24576, nslot * 128, BF16)
            return qT, kq, kT, kk, acc, ka, vb, kvb

        def load_head(h):
            qT, kq, kT, kk, acc, ka, vb, kvb = head_bufs(h)
            vb3 = vb.rearrange("p (m f) -> p m f", f=128)
            S.dma("sp", lambda e: e.dma_start(out=qT, in_=QT[h]), r=[("QT", h)], w=kq)
            hs = slice((h % 4) * 128, (h % 4 + 1) * 128)
            if pair:
                S.dma("sp", lambda e: e.dma_start(out=kT[:, 0:TS], in_=XGK[h // 4][hs, :]), r=[("XGK", h // 4)], w=kk)
                S.dma("sp", lambda e: e.dma_start(out=kT[:, TS:2 * TS], in_=XBK[h // 4][hs, :]), r=[("KT", h)], w=kk)
            else:
                S.dma("sp", lambda e: e.dma_start(out=kT, in_=KT[h][:, seg * TS - koff: seg * TS + TS]), r=[("KT", h)], w=kk)
            vT, kvt = WR.view(49152 + (h % 2) * 8192, nk, BF16)
            if pair:
                S.dma("sp", lambda e: e.dma_start(out=vT[:, 0:TS], in_=XGV[h // 4][hs, :]), r=[("XGV", h // 4)], w=kvt)
                S.dma("sp", lambda e: e.dma_start(out=vT[:, TS:2 * TS], in_=XBV[h // 4][hs, :]), r=[("V", h)], w=kvt)
            else:
                S.dma("sp", lambda e: e.dma_start(out=vT, in_=V[h][:, seg * TS - koff: seg * TS + TS]), r=[("V", h)], w=kvt)

        slot_src = []
        for d in (1, 4, 16):
            for r in range(d):
                for m in range(TS // (128 * d) + hb):
                    slot_src.append((koff - hb * 128 * d + r + m * 128 * d, d))
        assert len(slot_src) == nslot
        NG = (nslot + 7) // 8
        tbank = [0]

        def prep_v(h, g):
            qT, kq, kT, kk, acc, ka, vb, kvb = head_bufs(h)
            vT, kvt = WR.view(49152 + (h % 2) * 8192, nk, BF16)
            s0, s1 = 8 * g, min(nslot, 8 * g + 8)
            bk = 2 + tbank[0] % 2
            tbank[0] += 1
            pt, kpt = psv(PS, "PS", bk * 512, 512, BF16)

            def tr(e):
                for k, sl in enumerate(range(s0, s1)):
                    c0, d = slot_src[sl]
                    ins = e.transpose(out=pt[:, k * 128:(k + 1) * 128], in_=vT[:, c0: c0 + 127 * d + 1: d], identity=ident)
                return ins
            S.op("pe", tr, r=kvt + k_ident, w=kpt)
            n = (s1 - s0) * 128
            dstv = vb[:, s0 * 128: s0 * 128 + n]
            kd = [(WR.name, gg) for gg in range(((h % 2) * 24576 + s0 * 256) // GRAN, ((h % 2) * 24576 + s1 * 256 - 1) // GRAN + 1)]
            if g % 2 == 0:
                S.op("act", lambda e: e.activation(out=dstv, in_=pt[:, 0:n], func=AF.Copy), r=kpt, w=kd)
            else:
                S.op("dve", lambda e: e.tensor_copy(out=dstv, in_=pt[:, 0:n]), r=kpt, w=kd)

        pending_norm = []
        load_head(0)
        for g in range(NG):
            prep_v(0, g)
        for h in range(NH):
            if h + 1 < NH:
                load_head(h + 1)
            qT, kq, kT, kk, acc, ka, vb, kvb = head_bufs(h)
            acc3 = acc.rearrange("p (a n) -> p a n", a=2)
            vb3 = vb.rearrange("p (m f) -> p m f", f=128)
            ulist = []
            for d in (1, 4, 16):
                nbo = TS // (128 * d)
                for r in range(d):
                    for n in range(nbo):
                        ulist.append((d, r, n))
            pslot = [0, 0]
            state = {}

            def stage1(u, h=h, qT=qT, kT=kT, kq=kq, kk=kk):
                d, r, n = u
                bq = n * 128 * d + r
                has_prev = (koff + bq - 128 * d) >= 0
                qv = qT[:, bq: bq + 127 * d + 1: d]
                kc = kT[:, koff + bq: koff + bq + 127 * d + 1: d]
                ps, kps = psv(PX, "PX", (pslot[0] % 3) * 512, 256)
                pslot[0] += 1
                c0 = 0 if has_prev else 128

                def mm1(e):
                    if has_prev:
                        kp = kT[:, koff + bq - 128 * d: koff + bq - d + 1: d]
                        e.matmul(ps[:, 0:128], lhsT=kp, rhs=qv, start=True, stop=True)
                    return e.matmul(ps[:, 128:256], lhsT=kc, rhs=qv, start=True, stop=True)
                S.op("pe", mm1, r=kq + kk, w=kps)
                pt, kpt = small.get(256, BF16)
                if pair and n == 0:
                    S.op("act", lambda e: e.activation(out=pt[:, 0:128], in_=ps[:, 0:128], func=AF.Exp, bias=cinfo[:, 1:2], scale=scale),
                         r=kps + k_cinfo, w=kpt)
                    S.op("act", lambda e: e.activation(out=pt[:, 128:256], in_=ps[:, 128:256], func=AF.Exp, scale=scale), r=kps, w=kpt)
                else:
                    S.op("act", lambda e: e.activation(out=pt[:, c0:256], in_=ps[:, c0:256], func=AF.Exp, scale=scale), r=kps, w=kpt)
                S.op("pool", lambda e: e.tensor_tensor(out=pt[:, c0:256], in0=pt[:, c0:256], in1=mask2[:, c0:256], op=ALU.mult),
                     r=kpt + k_mask, w=kpt)
                state[u] = (pt, kpt, has_prev, bq)

            def stage2(u, vb3=vb3, kvb=kvb, acc3=acc3, ka=ka):
                d, r, n = u
                pt, kpt, has_prev, bq = state.pop(u)
                pi_ = pslot[1] % 3
                po, kpo = psv(PX, "PX", 1536, 256) if pi_ == 0 else psv(PS, "PS", (pi_ - 1) * 512, 256)
                pslot[1] += 1
                sl = vbase[(d, r)] + n + hb

                def mm2(e):
                    if has_prev:
                        e.matmul(po[:, 0:128], lhsT=vb3[:, sl - 1, :], rhs=pt[:, 0:128], start=True, stop=False)
                    e.matmul(po[:, 0:128], lhsT=vb3[:, sl, :], rhs=pt[:, 128:256], start=(not has_prev), stop=True)
                    if has_prev:
                        e.matmul(po[:, 128:256], lhsT=ones, rhs=pt[:, 0:128], start=True, stop=False)
                    return e.matmul(po[:, 128:256], lhsT=ones, rhs=pt[:, 128:256], start=(not has_prev), stop=True)
                S.op("pe", mm2, r=kpt + kvb + k_ones, w=kpo)
                po3 = po.rearrange("p (a n) -> p a n", a=2)
                av = acc3[:, :, bq: bq + 127 * d + 1: d]
                if d == 1:
                    S.op("dve", lambda e: e.tensor_copy(out=av, in_=po3), r=kpo, w=ka)
                else:
                    S.op("dve", lambda e: e.tensor_tensor(out=av, in0=po3, in1=av, op=ALU.add), r=kpo + ka, w=ka)
            LA = 3
            gnext = 0
            for idx in range(len(ulist) + LA):
                if idx < len(ulist):
                    stage1(ulist[idx])
                if idx >= LA:
                    stage2(ulist[idx - LA])
                if idx == 5 and pending_norm:
                    pending_norm.pop(0)()
                if h + 1 < NH and idx >= 6 and idx % 4 == 0 and gnext < NG:
                    prep_v(h + 1, gnext)
                    gnext += 1
            while h + 1 < NH and gnext < NG:
                prep_v(h + 1, gnext)
                gnext += 1
            def normalise(acc=acc, ka=ka, h=h):
                S.op("act", lambda e: e.activation(out=acc[:, TS:2 * TS], in_=acc[:, TS:2 * TS], func=AF.Ln), r=ka, w=ka)
                S.op("act", lambda e: e.activation(out=acc[:, TS:2 * TS], in_=acc[:, TS:2 * TS], func=AF.Exp, scale=-1.0), r=ka, w=ka)
                at, kat = r8.get(TS, BF16)
                S.op("dve", lambda e: e.tensor_tensor(out=at, in0=acc[:, 0:TS], in1=acc[:, TS:2 * TS], op=ALU.mult), r=ka, w=kat)
                S.dma("sp", lambda e: e.dma_start(out=ATT[h], in_=at), r=kat, w=[("ATT", h)])
            pending_norm.append(normalise)
        while pending_norm:
            pending_norm.pop(0)()

    def colsum_sq(src_fn, nsrc, pt, ptname, pbase):
        for c in range(nsrc):
            sv, ksv = src_fn(c)
            sq, ksq = r4.get(TS, BF16)
            S.op("act", lambda e, sq=sq, sv=sv: e.activation(out=sq, in_=sv, func=AF.Square), r=ksv, w=ksq)
            pk = [(ptname, b) for b in range(pbase // 512, pbase // 512 + 4)]

            def mm(e, sq=sq, c=c):
                for tb in range(4):
                    ins = e.matmul(pt[:, pbase + tb * 512: pbase + (tb + 1) * 512], lhsT=ones, rhs=sq[:, tb * 512:(tb + 1) * 512],
                                   start=(c == 0), stop=(c == nsrc - 1))
                return ins
            S.op("pe", mm, r=ksq + k_ones, w=pk)

    def gate_finalize(l, src_fn, rstd, krstd, ggrp, sgsrc, aybase):
        for c in range(16):
            sv, ksv = src_fn(c)
            sg, ksg = r4.get(TS, BF16)
            S.dma("sp", lambda e, sg=sg, c=c: e.dma_start(out=sg, in_=sgsrc[c]), r=[(sgsrc.tensor.name, c)], w=ksg)
            t, kt = r8.get(TS, F32)
            S.op("dve", lambda e, t=t, sv=sv, c=c: e.scalar_tensor_tensor(out=t, in0=sv, scalar=pvcol(l, ggrp, c), in1=rstd,
                                                                         op0=ALU.mult, op1=ALU.mult),
                 r=ksv + krstd + pvs[l][1], w=kt)
            o, ko = r4.get(TS, BF16)
            S.op("pool", lambda e, o=o, t=t, sg=sg: e.tensor_tensor(out=o, in0=t, in1=sg, op=ALU.mult), r=kt + ksg, w=ko)
            S.dma("act", lambda e, o=o, c=c: e.dma_start(out=AY[aybase + c], in_=o), r=ko, w=[("AY", aybase + c)])

    def finalize_one(l, c, sv, ksv, rstd, krstd, ggrp, sgsrc, aybase):
        sg, ksg = r4.get(TS, BF16)
        S.dma("sp", lambda e: e.dma_start(out=sg, in_=sgsrc[c]), r=[(sgsrc.tensor.name, c)], w=ksg)
        t, kt = r8.get(TS, F32)
        S.op("dve", lambda e: e.scalar_tensor_tensor(out=t, in0=sv, scalar=pvcol(l, ggrp, c), in1=rstd, op0=ALU.mult, op1=ALU.mult),
             r=ksv + krstd + pvs[l][1], w=kt)
        o, ko = r4.get(TS, BF16)
        S.op("dve", lambda e: e.tensor_tensor(out=o, in0=t, in1=sg, op=ALU.mult), r=kt + ksg, w=ko)
        S.dma("act", lambda e: e.dma_start(out=AY[aybase + c], in_=o), r=ko, w=[("AY", aybase + c)])

    def phase4_conv(l, seg):
        for h in range(NH):
            sv, ksv = r4.get(TS, BF16)
            S.dma("sp", lambda e, sv=sv, h=h: e.dma_start(out=sv, in_=ATT[h]), r=[("ATT", h)], w=ksv)
            sq, ksq = r4.get(TS, BF16)
            S.op("act", lambda e, sq=sq, sv=sv: e.activation(out=sq, in_=sv, func=AF.Square), r=ksv, w=ksq)

            def mm(e, sq=sq, h=h):
                for tb in range(4):
                    ins = e.matmul(PX[:, tb * 512:(tb + 1) * 512], lhsT=ones, rhs=sq[:, tb * 512:(tb + 1) * 512],
                                   start=(h == 0), stop=(h == NH - 1))
                return ins
            S.op("pe", mm, r=ksq + k_ones, w=[("PX", b_) for b_ in range(4)])
        rstd, krstd = WR.view(49152, TS, F32)
        rsqrt_tile(rstd, krstd, PX[:, 0:TS], [("PX", b_) for b_ in range(4)], 1.0 / D)
        prep = conv_prep(l, seg, 0)
        for c in range(NCH):
            nprep = conv_prep(l, seg, c + 1) if c + 1 < NCH else None
            conv_tile(l, seg, c, prep)
            prep = nprep
            sv, ksv = r4.get(TS, BF16)
            S.dma("sp", lambda e, sv=sv, c=c: e.dma_start(out=sv, in_=ATT[c]), r=[("ATT", c)], w=ksv)
            finalize_one(l, c, sv, ksv, rstd, krstd, G_ATTO, SGA, 0)

    def phase5(l):
        def src(c):
            return B0.view(c * 4096, TS, BF16)
        for c in range(NCH):
            sv, ksv = src(c)

            def mm(e, sv=sv, c=c):
                for tb in range(4):
                    ins = e.matmul(PX[:, tb * 512:(tb + 1) * 512], lhsT=ones, rhs=sv[:, tb * 512:(tb + 1) * 512],
                                   start=(c == 0), stop=(c == NCH - 1))
                return ins
            S.op("pe", mm, r=ksv + k_ones, w=[("PX", b) for b in range(4)])
        colsum_sq(src, NCH, PS, "PS", 0)
        mean, kmean = WR.view(57344, TS, F32)
        rstd, krstd = WR.view(49152, TS, F32)
        kpx = [("PX", b) for b in range(4)]
        kps = [("PS", b) for b in range(4)]
        S.op("dve", lambda e: e.tensor_scalar(out=mean, in0=PX[:, 0:TS], scalar1=1.0 / D, scalar2=None, op0=ALU.mult),
             r=kpx, w=kmean)
        msq, kmsq = r8.get(TS, F32)
        S.op("pool", lambda e: e.tensor_tensor(out=msq, in0=mean, in1=mean, op=ALU.mult), r=kmean, w=kmsq)
        S.op("dve", lambda e: e.scalar_tensor_tensor(out=rstd, in0=PS[:, 0:TS], scalar=1.0 / D, in1=msq, op0=ALU.mult,
                                                     op1=ALU.subtract), r=kps + kmsq, w=krstd)
        rsqrt_tile(rstd, krstd, rstd, krstd, 1.0)
        S.op("dve", lambda e: e.tensor_tensor(out=mean, in0=mean, in1=rstd, op=ALU.mult), r=kmean + krstd, w=kmean)
        for th in range(2):
            for c in range(NCH):
                sv, ksv = B0.view(c * 4096 + th * 2048, 1024, BF16)
                t, kt = r4.get(1024, F32)
                S.op("dve", lambda e, t=t, sv=sv, th=th: e.tensor_tensor(out=t, in0=sv, in1=rstd[:, th * 1024:(th + 1) * 1024],
                                                                        op=ALU.mult), r=ksv + krstd, w=kt)
                S.op("dve", lambda e, t=t, th=th: e.tensor_tensor(out=t, in0=t, in1=mean[:, th * 1024:(th + 1) * 1024],
                                                                  op=ALU.subtract), r=kt + kmean, w=kt)
                S.op("act", lambda e, t=t, sv=sv, c=c: e.activation(out=sv, in_=t, func=AF.Silu, bias=pvcol(l, G_LNB, c),
                                                                    scale=pvcol(l, G_LNG, c)), r=kt + pvs[l][1], w=ksv)
        def rhs_y(c, t0, n):
            return B0.view(c * 4096 + t0 * 2, n, BF16)
        pw_loaded = {}
        wslot[0] = 1

        def pw_pre(k):
            for kk in (k, k + 1):
                if kk < 8 and kk not in pw_loaded:
                    pw_loaded[kk] = load_w(w_pw[l], (kk % 4) * 512)
            return pw_loaded[k]

        def pw_mm(i, j, th):
            wb3, kw = pw_pre(th * 4 + i)
            e_ = i * 4 + j
            sgt, ksgt = r2k.get(1024, BF16)
            S.dma("sp", lambda e: e.dma_start(out=sgt, in_=SGC[e_][:, th * 1024:(th + 1) * 1024]), r=[("SGC", e_)], w=ksgt)
            return ws_unit(wb3, kw, j, th, rhs_y, split=(i == 0 and j == 0 and th == 0)) + (sgt, ksgt)

        def pw_epi(st, e_, th):
            pa, kpa, sgt, ksgt = st
            o, ko = r4.get(1024, BF16)
            S.op("dve", lambda e: e.scalar_tensor_tensor(out=o, in0=pa, scalar=pvcol(l, G_CONVO, e_), in1=sgt, op0=ALU.mult,
                                                         op1=ALU.mult), r=kpa + ksgt + pvs[l][1], w=ko)
            S.dma("act", lambda e: e.dma_start(out=AY[16 + e_][:, th * 1024:(th + 1) * 1024], in_=o), r=ko, w=[("AY", 16 + e_)])
            sq, ksq = r4.get(1024, BF16)
            S.op("act", lambda e: e.activation(out=sq, in_=pa, func=AF.Square), r=kpa, w=ksq)

            def mms(e):
                for tb in range(2):
                    o_ = th * 1024 + tb * 512
                    ins = e.matmul(PX[:, o_:o_ + 512], lhsT=ones, rhs=sq[:, tb * 512:(tb + 1) * 512],
                                   start=(e_ == 0), stop=(e_ == NCH - 1))
                return ins
            S.op("pe", mms, r=ksq + k_ones, w=[("PX", th * 2), ("PX", th * 2 + 1)])
        prevu = None
        for th in range(2):
            for i in range(4):
                for j in range(4):
                    st = pw_mm(i, j, th)
                    if prevu is not None:
                        pw_epi(*prevu)
                    prevu = (st, i * 4 + j, th)
        pw_epi(*prevu)
        rsqrt_tile(rstd, krstd, PX[:, 0:TS], kpx, 1.0 / D)
        pr, kpr = psv(PS, "PS", 0, 16)

        def mmt(e):
            for tt in range(16):
                ins = e.matmul(pr[:, tt:tt + 1], lhsT=rstd[0:1, tt * 128:(tt + 1) * 128], rhs=one_f[0:1, 0:1], start=True, stop=True)
            return ins
        S.op("pe", mmt, r=krstd + k_onef, w=kpr)
        S.op("dve", lambda e: e.tensor_copy(out=rc16, in_=pr), r=kpr, w=k_rc16)

    def phase6(l, xsrc, xdst, tok0, hook=None):
        for th in range(2):
            ayv, kay = B0.view(0, 32 * 1024, BF16)
            ay3 = ayv.rearrange("p (c n) -> p c n", c=32)
            for c in range(32):
                v1, k1 = B0.view(c * 2048, 1024, BF16)
                S.dma("sp", lambda e, v1=v1, c=c, th=th: e.dma_start(out=v1, in_=AY[c][:, th * 1024:(th + 1) * 1024]),
                      r=[("AY", c)], w=k1)
            nxt = load_w(w_out[l], 0, nchunks=32, slots=2)
            for cb in range(4):
                wb3, kw = nxt
                if cb + 1 < 4:
                    nxt = load_w(w_out[l], (cb + 1) * 512, nchunks=32, slots=2)
                if hook is not None and th == 0 and cb == 0:
                    hook()
                for tt in range(8):
                    a = acc6[0]
                    acc6[0] = (a + 1) % 2
                    pa, kpa = psv(PS, "PS", a * 1024, 512)
                    pc, kpc = psv(PS, "PS", a * 1024 + 512, 512)
                    t0 = tok0 + th * 1024 + tt * 128
                    ttg = th * 8 + tt
                    xt, kxt = r4.get(512, F32)
                    S.dma("sp", lambda e, xt=xt, t0=t0, cb=cb: e.dma_start(out=xt, in_=xsrc[t0:t0 + 128, cb * 512:(cb + 1) * 512]),
                          w=kxt)

                    def mm(e, pa=pa, pc=pc, tt=tt, wb3=wb3):
                        for c in range(32):
                            ins = e.matmul(pa if c < 16 else pc, lhsT=ay3[:, c, tt * 128:(tt + 1) * 128], rhs=wb3[:, c, :],
                                           start=(c % 16 == 0), stop=(c % 16 == 15))
                        return ins
                    if cb == 0 and tt == 0:
                        for c in range(32):
                            S.op("pe", lambda e, pa=pa, pc=pc, tt=tt, wb3=wb3, c=c: e.matmul(
                                pa if c < 16 else pc, lhsT=ay3[:, c, tt * 128:(tt + 1) * 128], rhs=wb3[:, c, :],
                                start=(c % 16 == 0), stop=(c % 16 == 15)),
                                r=kw + B0.view(c * 2048, 1024, BF16)[1], w=(kpa if c < 16 else kpc))
                    else:
                        S.op("pe", mm, r=kw + kay, w=kpa + kpc)
                    ot, kot = r4.get(512, F32)
                    S.op("dve", lambda e, ot=ot, pc=pc, xt=xt, ttg=ttg: e.scalar_tensor_tensor(
                        out=ot, in0=pc, scalar=rc16[:, ttg:ttg + 1], in1=xt, op0=ALU.mult, op1=ALU.add), r=kpc + kxt + k_rc16, w=kot)
                    S.op("dve", lambda e, ot=ot, pa=pa: e.tensor_tensor(out=ot, in0=pa, in1=ot, op=ALU.add), r=kpa + kot, w=kot)
                    S.dma("act", lambda e, ot=ot, t0=t0, cb=cb: e.dma_start(out=xdst[t0:t0 + 128, cb * 512:(cb + 1) * 512], in_=ot),
                          r=kot, w=[("XD", l, t0 // TS)])

    done = False
    for l in range(DEPTH):
        xsrc = x_in if l == 0 else X1
        xdst = X1 if l == 0 else out
        for seg in range(nseg):
            if nseg > 1 or l == 0:
                rope_tables(seg)
            if l == 0 or nseg > 1:
                build_gb(l)
            phase1(l, xsrc, seg * TS)
            if stop_after == "p1":
                done = True
                break
            phase2(l, seg)
            if stop_after == "p2":
                done = True
                break
            if pair:
                S.cc(lambda e: e.collective_compute("AllGather", ALU.bypass, replica_groups=RGS, ins=[XBH[:, :]], outs=[XGH[:, :]]),
                     r=[("XBH", c) for c in range(NCH)], w=[("XGH",)])
            phase3(l, seg)
            if stop_after == "p3":
                done = True
                break
            phase4_conv(l, seg)
            if debug:
                for c in range(NCH):
                    S.dma("sp", lambda e, c=c: e.dma_start(out=Y[c], in_=B0.view(c * 4096, TS, BF16)[0]),
                          r=B0.view(c * 4096, TS, BF16)[1], w=[("Y", c)])
            phase5(l)
            if stop_after == "p5":
                done = True
                break
            hook = (lambda l=l: build_gb(l + 1)) if (l + 1 < DEPTH and seg == nseg - 1) else None
            phase6(l, xsrc, xdst, seg * TS, hook)
        if done or stop_after == "l0":
            break
    S.emit()
    return nc


def host_consts():
    c = np.zeros((128, 512), np.float32)
    idx = np.arange(128)
    c[idx, idx] = 1.0
    c[(idx + 64) % 128, 128 + idx] = 1.0
    kk = idx[:, None]
    qq = idx[None, :]
    c[:, 256:384] = (kk >= qq).astype(np.float32)
    c[:, 384:512] = (kk <= qq).astype(np.float32)
    return c


def pack_params(norm_g, att_out_g, conv_out_g, dw_bias, conv_ln_g, conv_ln_b, q_norm_g, k_norm_g, dw_kernel):
    pv = np.zeros((DEPTH, 128, NPV), np.float32)
    for l in range(DEPTH):
        for gi, a in enumerate((norm_g, att_out_g, conv_out_g, dw_bias, conv_ln_g, conv_ln_b)):
            pv[l, :, gi * 16:(gi + 1) * 16] = a[l].reshape(16, 128).T
        pv[l, :, 96] = q_norm_g[l]
        pv[l, :, 97] = k_norm_g[l]
        pv[l, :, 98:] = dw_kernel[l].reshape(CW, 16, 128).transpose(2, 1, 0).reshape(128, 16 * CW)
    return pv


def kernel(x, norm_g, w_in, q_norm_g, k_norm_g, dw_kernel, dw_bias, conv_ln_g, conv_ln_b, w_pw, att_out_g, conv_out_g, w_out):
    x = np.asarray(x, np.float32)
    B, SEQ = x.shape[0], x.shape[1]
    assert SEQ == 2 * TS and B == 4
    pv = pack_params(*[np.asarray(a, np.float32) for a in (norm_g, att_out_g, conv_out_g, dw_bias, conv_ln_g, conv_ln_b,
                                                             q_norm_g, k_norm_g, dw_kernel)])
    cst = host_consts()
    w_in = np.ascontiguousarray(w_in, np.float32)
    w_pw = np.ascontiguousarray(w_pw, np.float32)
    w_out = np.ascontiguousarray(w_out, np.float32)
    nc = build(1, pair=True)
    in_maps = []
    for core in range(8):
        b, half = core // 2, core % 2
        ci = np.zeros((128, 4), np.float32)
        ci[:, 0] = half * TS
        ci[:, 1] = 0.0 if half == 1 else -30000.0
        ci[:, 2] = float(half)
        in_maps.append({"x": np.ascontiguousarray(x[b, half * TS:(half + 1) * TS]), "w_in": w_in, "w_pw": w_pw, "w_out": w_out,
                        "pv": pv, "cst": cst, "cinfo": ci})
    res = run_bass_kernel_spmd(nc, in_maps, core_ids=list(range(8)))
    out = np.zeros((B, SEQ, D), np.float32)
    for core in range(8):
        b, half = core // 2, core % 2
        out[b, half * TS:(half + 1) * TS] = np.asarray(res.results[core]["out"]).reshape(TS, D)
    return out
```

```python
import math
import numpy as np
import concourse.bass as bass
import concourse.mybir as mybir
from concourse.bass_utils import run_bass_kernel_spmd

F32 = mybir.dt.float32
BF16 = mybir.dt.bfloat16
I32 = mybir.dt.int32
AF = mybir.ActivationFunctionType
ALU = mybir.AluOpType

D = 2048
DIN = 14336
TS = 2048
NCH = 16
HD = 128
NH = 16
CW = 31
EPS = 1e-6
DEPTH = 2
NPV = 6 * 16 + 2 + 16 * CW
GRAN = 2048
PIPE = True
RGS = [[0, 1], [2, 3], [4, 5], [6, 7]]


class Sched:
    NDS = 12

    def __init__(self, nc):
        self.nc = nc
        self.ops = []
        self.esem = {e: nc.alloc_semaphore("es_" + e) for e in ("pe", "act", "dve", "pool")}
        self.dsem = {q: [nc.alloc_semaphore("ds_%s%d" % (q, i)) for i in range(self.NDS)]
                     for q in ("sp", "pool", "act")}
        self.ccsem = nc.alloc_semaphore("ccsem")
        self.cccount = 0

    def op(self, eng, fn, r=(), w=()):
        self.ops.append(dict(eng=eng, fn=fn, r=tuple(r), w=tuple(w), dma=False))

    def dma(self, q, fn, r=(), w=()):
        self.ops.append(dict(eng=q, fn=fn, r=tuple(r), w=tuple(w), dma=True))

    def cc(self, fn, r=(), w=()):
        self.ops.append(dict(eng="pool", fn=fn, r=tuple(r), w=tuple(w), dma=True, cc=True))

    def analyse(self):
        ops = self.ops
        last_w = {}
        readers = {}
        for i, o in enumerate(ops):
            pr = tuple(k for k in o["r"] if k[0] in ("PS", "PX"))
            if pr:
                o["w"] = tuple(o["w"]) + pr
                o["r"] = tuple(k for k in o["r"] if k[0] not in ("PS", "PX"))
            deps = set()
            for k in o["r"]:
                if k in last_w:
                    deps.add(last_w[k])
            for k in o["w"]:
                if k in last_w:
                    deps.add(last_w[k])
                deps.update(readers.get(k, ()))
            deps.discard(i)
            o["deps"] = deps
            for k in o["r"]:
                readers.setdefault(k, []).append(i)
            for k in o["w"]:
                last_w[k] = i
                readers[k] = []
        for o in ops:
            o["ms"] = False
        for o in ops:
            for d in o["deps"]:
                p = ops[d]
                if p["dma"]:
                    continue
                if p["eng"] == "pe" and o["eng"] == "pe" and not o["dma"]:
                    continue
                p["ms"] = True
        cnt = {e: 0 for e in self.esem}
        dstate = {q: [0] * self.NDS for q in self.dsem}
        drr = {q: 0 for q in self.dsem}
        for o in ops:
            if o.get("cc"):
                o["dsem"] = self.ccsem
                o["guard"] = self.cccount
                self.cccount += 1
                o["target"] = self.cccount
            elif o["dma"]:
                q = o["eng"]
                j = drr[q]
                drr[q] = (j + 1) % self.NDS
                o["dsem"] = self.dsem[q][j]
                o["guard"] = dstate[q][j]
                dstate[q][j] += 16
                o["target"] = dstate[q][j]
            elif o["ms"]:
                cnt[o["eng"]] += 1
                o["count"] = cnt[o["eng"]]
        self.final_dma = {q: list(dstate[q]) for q in dstate}
        waited = {}
        for o in ops:
            e = o["eng"]
            wl = {}
            if o["dma"] and o["guard"] > 0:
                wl[id(o["dsem"])] = (o["dsem"], o["guard"])
            for d in sorted(o["deps"]):
                p = ops[d]
                if p["dma"]:
                    s, v = p["dsem"], p["target"]
                else:
                    if p["eng"] == "pe" and e == "pe" and not o["dma"]:
                        continue
                    s, v = self.esem[p["eng"]], p["count"]
                if id(s) not in wl or wl[id(s)][1] < v:
                    wl[id(s)] = (s, v)
            out = []
            wd = waited.setdefault(e, {})
            for sid, (s, v) in wl.items():
                if wd.get(sid, 0) >= v:
                    continue
                wd[sid] = v
                out.append((s, v))
            o["waits"] = out

    def emit(self):
        self.analyse()
        nc = self.nc
        per = {e: [] for e in ("sp", "act", "pe", "dve", "pool")}
        for o in self.ops:
            per[o["eng"]].append(o)

        def make_body(ename):
            def body(e):
                for o in per[ename]:
                    for (s, v) in o["waits"]:
                        e.wait_ge(s, v)
                    inst = o["fn"](e)
                    if o.get("cc"):
                        inst.then_inc(o["dsem"], 1)
                    elif o["dma"]:
                        inst.then_inc(o["dsem"], 16)
                    elif o["ms"]:
                        inst.then_inc(self.esem[ename], 1)
                if ename == "sp":
                    for q in self.dsem:
                        for j, s in enumerate(self.dsem[q]):
                            if self.final_dma[q][j] > 0:
                                e.wait_ge(s, self.final_dma[q][j])
                    if self.cccount > 0:
                        e.wait_ge(self.ccsem, self.cccount)
            return body

        with nc.Block() as block:
            block.sync(make_body("sp"))
            block.scalar(make_body("act"))
            block.tensor(make_body("pe"))
            block.vector(make_body("dve"))
            block.gpsimd(make_body("pool"))


class Region:
    def __init__(self, nc, name, nbytes):
        self.t = nc.alloc_sbuf_tensor(name, [128, nbytes // 4], F32)
        self.name = name
        self.nbytes = nbytes

    def view(self, off, n, dt):
        sz = 2 if dt == BF16 else 4
        assert off % 4 == 0 and (n * sz) % 4 == 0 and off + n * sz <= self.nbytes, (off, n, self.nbytes)
        ap = self.t[:, off // 4:(off + n * sz) // 4]
        if dt != F32:
            ap = ap.bitcast(dt)
        keys = [(self.name, g) for g in range(off // GRAN, (off + n * sz - 1) // GRAN + 1)]
        return ap, keys


def build(nseg, debug=False, stop_after=None, pair=False):
    nc = bass.Bass("TRN2", target_bir_lowering=False)
    S = Sched(nc)
    NT = nseg * TS
    okind = "ExternalOutput" if debug else "Internal"

    x_in = nc.dram_tensor("x", [NT, D], F32, kind="ExternalInput").ap()
    w_in = nc.dram_tensor("w_in", [DEPTH, D, DIN], F32, kind="ExternalInput").ap()
    w_pw = nc.dram_tensor("w_pw", [DEPTH, D, D], F32, kind="ExternalInput").ap()
    w_out = nc.dram_tensor("w_out", [DEPTH, 2 * D, D], F32, kind="ExternalInput").ap()
    pv_in = nc.dram_tensor("pv", [DEPTH, 128, NPV], F32, kind="ExternalInput").ap()
    cst_in = nc.dram_tensor("cst", [128, 512], F32, kind="ExternalInput").ap()
    cinfo_in = nc.dram_tensor("cinfo", [128, 4], F32, kind="ExternalInput").ap()
    out = nc.dram_tensor("out", [NT, D], F32, kind="ExternalOutput").ap()

    X1 = nc.dram_tensor("X1", [NT, D], F32, kind=okind).ap()
    QT = nc.dram_tensor("QT", [NH, 128, TS], BF16, kind=okind).ap()
    if pair:
        assert nseg == 1
        XBK = [nc.dram_tensor("XBK%d" % i, [512, TS], BF16).ap() for i in range(4)]
        XGK = [nc.dram_tensor("XGK%d" % i, [1024, TS], BF16).ap() for i in range(4)]
        XBV = [nc.dram_tensor("XBV%d" % i, [512, TS], BF16).ap() for i in range(4)]
        XGV = [nc.dram_tensor("XGV%d" % i, [1024, TS], BF16).ap() for i in range(4)]
        XBH = nc.dram_tensor("XBH", [D, 32], BF16).ap()
        XGH = nc.dram_tensor("XGH", [2 * D, 32], BF16).ap()
        KT = V = None
    else:
        KT = nc.dram_tensor("KT", [NH, 128, NT], BF16, kind=okind).ap()
        V = nc.dram_tensor("V", [NH, 128, NT], BF16, kind=okind).ap()
    SGA = nc.dram_tensor("SGA", [NH, 128, TS], BF16, kind="Internal").ap()
    SGC = nc.dram_tensor("SGC", [NCH, 128, TS], BF16, kind="Internal").ap()
    Y = nc.dram_tensor("Y", [NCH, 128, TS], BF16, kind=okind).ap()
    ATT = nc.dram_tensor("ATT", [NH, 128, TS], BF16, kind=okind).ap()
    CV = nc.dram_tensor("CV", [NCH, 128, TS], BF16, kind="Internal").ap()
    AY = nc.dram_tensor("AY", [2 * NCH, 128, TS], BF16, kind=okind).ap()
    U = nc.dram_tensor("U", [NCH, 128, TS], BF16, kind="Internal").ap()
    UH = nc.dram_tensor("UH", [NCH, 128, 32], BF16, kind="Internal").ap()

    B0 = Region(nc, "B0", 65536)
    WR = Region(nc, "WR", 65536)
    RP = Region(nc, "RP", 16384)
    R4 = Region(nc, "R4", 24576)
    R8 = Region(nc, "R8", 16384)
    MS = Region(nc, "MS", 8192)
    CVR = Region(nc, "CVR", 3 * 4160)
    PS = nc.alloc_psum_tensor("PSA", [128, 2048], F32)
    PX = nc.alloc_psum_tensor("PSX", [128, 2048], F32)

    def psv(t, name, off, n, dt=F32):
        ap = t[:, off:off + n]
        if dt == BF16:
            ap = ap.bitcast(BF16)
        keys = [(name, b) for b in range(off // 512, (off + n - 1) // 512 + 1)]
        return ap, keys

    class Ring:
        def __init__(self, reg, size, count, base=0):
            self.reg, self.size, self.count, self.i, self.base = reg, size, count, 0, base

        def get(self, n, dt):
            off = self.base + self.i * self.size
            self.i = (self.i + 1) % self.count
            return self.reg.view(off, n, dt)

    r4 = Ring(R4, 4096, 6)
    r8 = Ring(R8, 8192, 2)
    r2k = Ring(R8, 2048, 8)

    ident, k_ident = MS.view(0, 128, BF16)
    ones, k_ones = MS.view(256, 128, BF16)
    swp, k_swp = MS.view(512, 128, BF16)
    mask2, k_mask = MS.view(768, 256, BF16)
    pvs = [MS.view(2048 + l * 2560, NPV, F32) for l in range(DEPTH)]
    sc_off = 2048 + 2 * 2560
    sgn2, k_sgn2 = MS.view(sc_off, 1, F32)
    invf, k_invf = MS.view(sc_off + 4, 1, F32)
    gq2 = [MS.view(sc_off + 8 + 8 * l, 2, F32) for l in range(DEPTH)]
    ssq_t = [MS.view(sc_off + 64 + 8 * i, 1, F32) for i in range(2)]
    rstd_t = [MS.view(sc_off + 128 + 8 * i, 1, F32) for i in range(2)]
    negpi, k_negpi = MS.view(sc_off + 192, 1, F32)
    cinfo, k_cinfo = MS.view(sc_off + 256, 4, F32)
    rc16, k_rc16 = MS.view(sc_off + 448, 16, F32)
    one_f, k_onef = MS.view(sc_off + 512, 1, F32)
    cosT, k_cos = RP.view(0, TS, F32)
    sinT, k_sin = RP.view(8192, TS, F32)

    cst_f, k_cstf = r4.get(512, F32)
    S.dma("sp", lambda e: e.dma_start(out=cst_f, in_=cst_in[:, :]), w=k_cstf)
    S.op("dve", lambda e: e.tensor_copy(out=ident, in_=cst_f[:, 0:128]), r=k_cstf, w=k_ident)
    S.op("dve", lambda e: e.tensor_copy(out=swp, in_=cst_f[:, 128:256]), r=k_cstf, w=k_swp)
    S.op("dve", lambda e: e.tensor_copy(out=mask2, in_=cst_f[:, 256:512]), r=k_cstf, w=k_mask)
    S.op("dve", lambda e: e.memset(ones, 1.0), w=k_ones)
    S.op("dve", lambda e: e.memset(one_f, 1.0), w=k_onef)
    S.dma("sp", lambda e: e.dma_start(out=cinfo, in_=cinfo_in[:, :]), w=k_cinfo)
    S.op("dve", lambda e: e.memset(negpi, EPS), w=k_negpi)
    for l in range(DEPTH):
        S.dma("sp", lambda e, l=l: e.dma_start(out=pvs[l][0], in_=pv_in[l]), w=pvs[l][1])
        S.op("dve", lambda e, l=l: e.tensor_scalar(out=gq2[l][0], in0=pvs[l][0][:, 96:98], scalar1=1.0,
                                                   scalar2=None, op0=ALU.mult), r=pvs[l][1], w=gq2[l][1])
    pidx_i, k_pi = r4.get(1, I32)
    pidx_f, k_pf = r4.get(1, F32)
    S.op("pool", lambda e: e.iota(pidx_i, pattern=[[0, 1]], base=0, channel_multiplier=1), w=k_pi)
    S.op("dve", lambda e: e.tensor_copy(out=pidx_f, in_=pidx_i), r=k_pi, w=k_pf)
    S.op("dve", lambda e: e.tensor_scalar(out=sgn2, in0=pidx_f, scalar1=64.0, scalar2=2.0, op0=ALU.is_ge, op1=ALU.mult),
         r=k_pf, w=k_sgn2)
    S.op("dve", lambda e: e.tensor_scalar(out=sgn2, in0=sgn2, scalar1=-1.0, scalar2=None, op0=ALU.add), r=k_sgn2, w=k_sgn2)
    pm, k_pm = r4.get(1, F32)
    S.op("dve", lambda e: e.tensor_scalar(out=pm, in0=pidx_f, scalar1=64.0, scalar2=-64.0, op0=ALU.is_ge, op1=ALU.mult),
         r=k_pf, w=k_pm)
    S.op("dve", lambda e: e.tensor_tensor(out=pm, in0=pm, in1=pidx_f, op=ALU.add), r=k_pf + k_pm, w=k_pm)
    S.op("act", lambda e: e.activation(out=invf, in_=pm, func=AF.Exp, scale=-2.0 * math.log(10000.0) / 128.0),
         r=k_pm, w=k_invf)

    def pvcol(l, grp, c):
        return pvs[l][0][:, grp * 16 + c: grp * 16 + c + 1]
    G_NORM, G_ATTO, G_CONVO, G_DWB, G_LNG, G_LNB = range(6)

    def rsqrt_tile(dst, kdst, src, ksrc, scale):
        S.op("act", lambda e: e.activation(out=dst, in_=src, func=AF.Ln, bias=negpi, scale=scale), r=ksrc + k_negpi, w=kdst)
        S.op("act", lambda e: e.activation(out=dst, in_=dst, func=AF.Exp, scale=-0.5), r=kdst, w=kdst)

    def dwk(l, c, j):
        o = 98 + c * CW + j
        return pvs[l][0][:, o:o + 1]

    def rope_tables(seg):
        C1 = 6.28125
        C2 = 2 * math.pi - C1
        pos_i, k1 = B0.view(0, TS, I32)
        S.op("pool", lambda e: e.iota(pos_i, pattern=[[1, TS]], base=seg * TS, channel_multiplier=0), w=k1)
        ang, k2 = B0.view(8192, TS, F32)
        S.op("dve", lambda e: e.tensor_copy(out=ang, in_=pos_i), r=k1, w=k2)
        if pair:
            S.op("dve", lambda e: e.tensor_scalar(out=ang, in0=ang, scalar1=cinfo[:, 0:1], scalar2=None, op0=ALU.add),
                 r=k2 + k_cinfo, w=k2)
        S.op("dve", lambda e: e.tensor_scalar(out=ang, in0=ang, scalar1=invf, scalar2=None, op0=ALU.mult),
             r=k2 + k_invf, w=k2)
        for which, phase in ((0, 0.0), (1, math.pi / 2)):
            dst, kd = (sinT, k_sin) if which == 0 else (cosT, k_cos)
            base = 16384 + which * 24576
            angp, ka = B0.view(base, TS, F32)
            ki, kki = B0.view(base + 8192, TS, I32)
            kf, kkf = B0.view(base + 16384, TS, F32)
            S.op("dve", lambda e, angp=angp, phase=phase: e.tensor_scalar(out=angp, in0=ang, scalar1=phase, scalar2=None,
                                                                          op0=ALU.add), r=k2, w=ka)
            S.op("dve", lambda e, angp=angp, kf=kf: e.tensor_scalar(out=kf, in0=angp, scalar1=1.0 / (2 * math.pi), scalar2=None,
                                                                    op0=ALU.mult), r=ka, w=kkf)
            S.op("dve", lambda e, ki=ki, kf=kf: e.tensor_copy(out=ki, in_=kf), r=kkf, w=kki)
            S.op("dve", lambda e, ki=ki, kf=kf: e.tensor_copy(out=kf, in_=ki), r=kki, w=kkf)
            S.op("dve", lambda e, angp=angp, kf=kf: e.scalar_tensor_tensor(out=angp, in0=kf, scalar=-C1, in1=angp, op0=ALU.mult,
                                                                           op1=ALU.add), r=ka + kkf, w=ka)
            S.op("dve", lambda e, angp=angp, kf=kf: e.scalar_tensor_tensor(out=angp, in0=kf, scalar=-C2, in1=angp, op0=ALU.mult,
                                                                           op1=ALU.add), r=ka + kkf, w=ka)
            S.op("dve", lambda e, angp=angp, kf=kf: e.tensor_scalar(out=kf, in0=angp, scalar1=math.pi, scalar2=-2 * math.pi,
                                                                    op0=ALU.is_gt, op1=ALU.mult), r=ka, w=kkf)
            S.op("dve", lambda e, angp=angp, kf=kf: e.tensor_tensor(out=angp, in0=angp, in1=kf, op=ALU.add), r=ka + kkf, w=ka)
            S.op("dve", lambda e, angp=angp, kf=kf: e.tensor_scalar(out=kf, in0=angp, scalar1=-math.pi, scalar2=2 * math.pi,
                                                                    op0=ALU.is_lt, op1=ALU.mult), r=ka, w=kkf)
            S.op("dve", lambda e, angp=angp, kf=kf: e.tensor_tensor(out=angp, in0=angp, in1=kf, op=ALU.add), r=ka + kkf, w=ka)
            S.op("act", lambda e, angp=angp, dst=dst: e.activation(out=dst, in_=angp, func=AF.Sin), r=ka, w=kd)
        S.op("dve", lambda e: e.tensor_scalar(out=sinT, in0=sinT, scalar1=sgn2, scalar2=None, op0=ALU.mult),
             r=k_sin + k_sgn2, w=k_sin)

    def hT(c, t0, n):
        ap, keys = B0.view(c * 4096 + t0 * 2, n, BF16)
        return ap, keys

    def build_gb(l):
        gb, kgb = CVR.view(0, 16 * 128, BF16)
        for c in range(16):
            S.op("pool", lambda e, c=c: e.tensor_scalar(out=gb[:, c * 128:(c + 1) * 128], in0=ones, scalar1=pvcol(l, G_NORM, c),
                                                        scalar2=None, op0=ALU.mult), r=k_ones + pvs[l][1], w=kgb)

    def phase1(l, xsrc, tok0):
        ssq16, kss = MS.view(sc_off + 320, 16, F32)
        rstd16, krs = MS.view(sc_off + 384, 16, F32)
        S.op("dve", lambda e: e.memset(ssq16, 0.0), w=kss)
        gb, kgb = CVR.view(0, 16 * 128, BF16)
        hall, _ = B0.view(0, 16 * TS, BF16)
        h3 = hall.rearrange("p (c t) -> p c t", c=16)

        def load(tt):
            xt, kx = r8.get(D, F32)
            S.dma("sp", lambda e: e.dma_start(out=xt, in_=xsrc[tok0 + tt * 128: tok0 + (tt + 1) * 128, :]), w=kx)
            return xt, kx
        nxt = load(0)
        for tt in range(16):
            xt, kx = nxt
            junk, kj = r4.get(D, BF16)
            S.op("act", lambda e, xt=xt, junk=junk, tt=tt: e.activation(out=junk, in_=xt, func=AF.Square, accum_out=ssq16[:, tt:tt + 1]),
                 r=kx + kss, w=kj + [("ssq", tt)])
            S.op("act", lambda e, tt=tt: e.activation(out=rstd16[:, tt:tt + 1], in_=ssq16[:, tt:tt + 1], func=AF.Ln, bias=negpi, scale=1.0 / D),
                 r=[("ssq", tt)] + k_negpi, w=[("rstd", tt)])
            S.op("act", lambda e, tt=tt: e.activation(out=rstd16[:, tt:tt + 1], in_=rstd16[:, tt:tt + 1], func=AF.Exp, scale=-0.5),
                 r=[("rstd", tt)], w=[("rstd", tt)])
            xn, kxn = r4.get(D, BF16)
            S.op("act", lambda e, xt=xt, xn=xn, tt=tt: e.activation(out=xn, in_=xt, func=AF.Copy, scale=rstd16[:, tt:tt + 1]),
                 r=kx + [("rstd", tt)], w=kxn)
            if tt + 1 < 16:
                nxt = load(tt + 1)
            for g in range(4):
                pt, kpt = psv(PX, "PX", ((tt * 4 + g) % 4) * 512, 256, BF16)

                def tr(e, xn=xn, pt=pt, g=g):
                    for k in range(4):
                        c = g * 4 + k
                        ins = e.transpose(out=pt[:, k * 128:(k + 1) * 128], in_=xn[:, c * 128:(c + 1) * 128], identity=ident)
                    return ins
                S.op("pe", tr, r=kxn + k_ident, w=kpt)
                kh = []
                for k in range(4):
                    kh += hT(g * 4 + k, tt * 128, 128)[1]
                S.op("dve", lambda e, pt=pt, g=g, tt=tt: e.tensor_tensor(
                    out=h3[:, g * 4:(g + 1) * 4, tt * 128:(tt + 1) * 128], in0=pt.rearrange("p (c t) -> p c t", c=4),
                    in1=gb[:, g * 512:(g + 1) * 512].rearrange("p (c t) -> p c t", c=4), op=ALU.mult), r=kpt + kgb, w=kh)

    wslot = [0, 0]

    def load_w(src, col0, nchunks=16, ncols=512, slots=1):
        if slots == 1:
            i = wslot[0]
            wslot[0] = (i + 1) % 3
        else:
            i = wslot[1]
            wslot[1] = (i + 2) % 4
        wb, kw = WR.view(i * 16384, nchunks * ncols, BF16)
        wb3 = wb.rearrange("p (c n) -> p c n", c=nchunks)
        sv = src.rearrange("(c p) n -> p c n", p=128)
        for c0 in range(0, nchunks, 4):
            S.dma("pool", lambda e, c0=c0: e.dma_start(out=wb3[:, c0:c0 + 4, :], in_=sv[:, c0:c0 + 4, col0:col0 + ncols]),
                  w=kw)
        return wb3, kw

    acc_i = [0]
    acc6 = [0]

    def ws_unit(wb3, kw, j, th, rhs_fn, split=False):
        a = acc_i[0]
        acc_i[0] ^= 1
        pa, kpa = psv(PS, "PS", a * 1024, 1024)
        rk = []
        for c in range(16):
            rk += rhs_fn(c, th * 1024, 1024)[1]

        def mm(e):
            for c in range(16):
                for tb in range(2):
                    rv = rhs_fn(c, th * 1024 + tb * 512, 512)[0]
                    ins = e.matmul(pa[:, tb * 512:(tb + 1) * 512], lhsT=wb3[:, c, j * 128:(j + 1) * 128], rhs=rv,
                                   start=(c == 0), stop=(c == 15))
            return ins
        if not split:
            S.op("pe", mm, r=kw + rk, w=kpa)
        else:
            for c in range(16):
                def mmc(e, c=c):
                    for tb in range(2):
                        rv = rhs_fn(c, th * 1024 + tb * 512, 512)[0]
                        ins = e.matmul(pa[:, tb * 512:(tb + 1) * 512], lhsT=wb3[:, c, j * 128:(j + 1) * 128], rhs=rv,
                                       start=(c == 0), stop=(c == 15))
                    return ins
                S.op("pe", mmc, r=kw + rhs_fn(c, th * 1024, 1024)[1], w=kpa)
        return pa, kpa

    def qk_part1(pa, kpa, l, which):
        sqb, ksq = r2k.get(1024, BF16)
        S.op("act", lambda e: e.activation(out=sqb, in_=pa, func=AF.Square), r=kpa, w=ksq)
        qg, kqg = r2k.get(1024, BF16)
        S.op("act", lambda e: e.activation(out=qg, in_=pa, func=AF.Copy, scale=gq2[l][0][:, which:which + 1]),
             r=kpa + gq2[l][1], w=kqg)
        return sqb, ksq, qg, kqg

    def qk_epilogue(l, sqb, ksq, qg, kqg, which, h, th, tokg0):
        px1, kpx1 = psv(PX, "PX", 0, 1024)
        px2, kpx2 = psv(PX, "PX", 1024, 1024)

        def mm1(e):
            for tb in range(2):
                ins = e.matmul(px1[:, tb * 512:(tb + 1) * 512], lhsT=ones, rhs=sqb[:, tb * 512:(tb + 1) * 512], start=True, stop=True)
            return ins
        S.op("pe", mm1, r=ksq + k_ones, w=kpx1)

        def mm2(e):
            for tb in range(2):
                ins = e.matmul(px2[:, tb * 512:(tb + 1) * 512], lhsT=swp, rhs=qg[:, tb * 512:(tb + 1) * 512], start=True, stop=True)
            return ins
        S.op("pe", mm2, r=kqg + k_swp, w=kpx2)
        rs, krs = r4.get(1024, F32)
        rsqrt_tile(rs, krs, px1, kpx1, 1.0 / HD)
        t1, kt1 = r4.get(1024, F32)
        S.op("dve", lambda e: e.tensor_tensor(out=t1, in0=qg, in1=cosT[:, th * 1024:(th + 1) * 1024], op=ALU.mult),
             r=kqg + k_cos, w=kt1)
        t2, kt2 = r4.get(1024, F32)
        S.op("dve", lambda e: e.tensor_tensor(out=t2, in0=px2, in1=sinT[:, th * 1024:(th + 1) * 1024], op=ALU.mult),
             r=kpx2 + k_sin, w=kt2)
        S.op("dve", lambda e: e.tensor_tensor(out=t1, in0=t1, in1=t2, op=ALU.add), r=kt1 + kt2, w=kt1)
        qo, kqo = r4.get(1024, BF16)
        S.op("dve", lambda e: e.tensor_tensor(out=qo, in0=t1, in1=rs, op=ALU.mult), r=kt1 + krs, w=kqo)
        if which == 0:
            S.dma("sp", lambda e: e.dma_start(out=QT[h][:, th * 1024:(th + 1) * 1024], in_=qo), r=kqo, w=[("QT", h)])
        else:
            if pair:
                kdst = XBK[h // 4][(h % 4) * 128:(h % 4 + 1) * 128, th * 1024:(th + 1) * 1024]
            else:
                kdst = KT[h][:, tokg0 + th * 1024: tokg0 + (th + 1) * 1024]
            S.dma("sp", lambda e: e.dma_start(out=kdst, in_=qo), r=kqo, w=[("KT", h)])

    def gate_epilogue(pa, kpa, dst, idx, th):
        sg, ksg = r4.get(1024, BF16)
        S.op("act", lambda e: e.activation(out=sg, in_=pa, func=AF.Silu), r=kpa, w=ksg)
        S.dma("sp", lambda e: e.dma_start(out=dst[idx][:, th * 1024:(th + 1) * 1024], in_=sg), r=ksg,
              w=[(dst.tensor.name, idx)])

    def phase2(l, seg):
        tokg0 = seg * TS
        wl = w_in[l]

        def rhs_h(c, t0, n):
            return hT(c, t0, n)

        blocks = []
        for i in range(4):
            blocks += [("q", i), ("k", i), ("v", i)]
        blocks += [("ga", i) for i in range(4)]
        for i in range(4):
            blocks += [("gb", i), ("ua", i)]
        blocks += [("gc", i) for i in range(4)]
        colbase = {"q": 0, "k": 2048, "v": 4096, "ga": 6144, "ua": 8192, "gb": 10240, "gc": 12288}
        loaded = {}

        def prefetch(bi):
            if bi < len(blocks) and bi not in loaded:
                kind, i = blocks[bi]
                loaded[bi] = load_w(wl, colbase[kind] + i * 512)

        units = []
        for bi, (kind, i) in enumerate(blocks):
            first = [True]

            def pre(bi=bi, first=first):
                if first[0]:
                    first[0] = False
                    prefetch(bi)
                    prefetch(bi + 1)
                    prefetch(bi + 2)
                return loaded[bi]
            for j in range(4):
                for th in range(2):
                    def mmw(pre=pre, j=j, th=th, kind=kind):
                        wb3, kw = pre()
                        pa, kpa = ws_unit(wb3, kw, j, th, rhs_h)
                        if kind in ("q", "k"):
                            return qk_part1(pa, kpa, l, 0 if kind == "q" else 1)
                        return pa, kpa
                    if kind in ("q", "k"):
                        def ep(st, kind=kind, h=i * 4 + j, th=th, i=i, j=j):
                            qk_epilogue(l, st[0], st[1], st[2], st[3], 0 if kind == "q" else 1, h, th, tokg0)
                            if pair and kind == "k" and j == 3 and th == 1:
                                S.cc(lambda e: e.collective_compute("AllGather", ALU.bypass, replica_groups=RGS, ins=[XBK[i][:, :]],
                                                                    outs=[XGK[i][:, :]]),
                                     r=[("KT", hh) for hh in range(4 * i, 4 * i + 4)], w=[("XGK", i)])
                    elif kind == "v":
                        def ep(st, h=i * 4 + j, th=th, i=i, j=j):
                            vt, kvt = r4.get(1024, BF16)
                            S.op("act", lambda e: e.activation(out=vt, in_=st[0], func=AF.Copy), r=st[1], w=kvt)
                            if pair:
                                vdst = XBV[i][j * 128:(j + 1) * 128, th * 1024:(th + 1) * 1024]
                            else:
                                vdst = V[h][:, tokg0 + th * 1024: tokg0 + (th + 1) * 1024]
                            S.dma("sp", lambda e: e.dma_start(out=vdst, in_=vt), r=kvt, w=[("V", h)])
                            if pair and j == 3 and th == 1:
                                S.cc(lambda e: e.collective_compute("AllGather", ALU.bypass, replica_groups=RGS, ins=[XBV[i][:, :]],
                                                                    outs=[XGV[i][:, :]]),
                                     r=[("V", hh) for hh in range(4 * i, 4 * i + 4)], w=[("XGV", i)])
                    elif kind in ("ga", "gc"):
                        def ep(st, kind=kind, idx=i * 4 + j, th=th):
                            gate_epilogue(st[0], st[1], SGA if kind == "ga" else SGC, idx, th)
                    elif kind == "gb":
                        def ep(st, j=j, th=th):
                            sb, ksb = WR.view(49152 + (j * 2 + th) * 2048, 1024, BF16)
                            S.op("act", lambda e: e.activation(out=sb, in_=st[0], func=AF.Sigmoid), r=st[1], w=ksb)
                    else:
                        def ep(st, c=i * 4 + j, j=j, th=th):
                            sb, ksb = WR.view(49152 + (j * 2 + th) * 2048, 1024, BF16)
                            ut, kut = r4.get(1024, BF16)
                            S.op("dve", lambda e: e.tensor_tensor(out=ut, in0=st[0], in1=sb, op=ALU.mult), r=st[1] + ksb, w=kut)
                            S.dma("sp", lambda e: e.dma_start(out=U[c][:, th * 1024:(th + 1) * 1024], in_=ut), r=kut, w=[("U", c)])
                            if pair and th == 1:
                                S.dma("sp", lambda e: e.dma_start(out=XBH[c * 128:(c + 1) * 128, :], in_=ut[:, 992:1024]),
                                      r=kut, w=[("XBH", c)])
                    units.append((mmw, ep))
        prev = None
        for (mmf, epf) in units:
            st = mmf()
            if not PIPE:
                epf(st)
                continue
            if prev is not None:
                prev[0](prev[1])
            prev = (epf, st)
        if PIPE:
            prev[0](prev[1])

    def conv_prep(l, seg, c):
        ue, kue = CVR.view((c % 3) * 4160, 2080, BF16)
        if pair:
            S.dma("sp", lambda e: e.dma_start(out=ue[:, 0:32], in_=XGH[c * 128:(c + 1) * 128, :]), r=[("XGH",)], w=kue)
            S.op("pool", lambda e: e.tensor_scalar(out=ue[:, 0:32], in0=ue[:, 0:32], scalar1=cinfo[:, 2:3], scalar2=None,
                                                   op0=ALU.mult), r=kue + k_cinfo, w=kue)
        elif seg == 0:
            S.op("pool", lambda e: e.memset(ue[:, 0:32], 0.0), w=kue)
        else:
            S.dma("sp", lambda e: e.dma_start(out=ue[:, 0:32], in_=UH[c]), r=[("UH", c)], w=kue)
        S.dma("sp", lambda e: e.dma_start(out=ue[:, 32:32 + TS], in_=U[c]), r=[("U", c)], w=kue)
        if seg + 1 < nseg:
            S.dma("sp", lambda e: e.dma_start(out=UH[c], in_=ue[:, TS:TS + 32]), r=kue, w=[("UH", c)])
        dg, kdg = WR.view((c % 2) * 8192, CW * 128, BF16)
        dg3 = dg.rearrange("p (j n) -> p j n", j=CW)
        wv = pvs[l][0][:, 98 + c * CW: 98 + (c + 1) * CW]
        S.op("pool", lambda e: e.tensor_tensor(
            out=dg3, in0=ident.unsqueeze(1).broadcast_to([128, CW, 128]), in1=wv.unsqueeze(2).broadcast_to([128, CW, 128]),
            op=ALU.mult), r=k_ident + pvs[l][1], w=kdg)
        return ue, kue, dg3, kdg

    NDT = 7

    def conv_tile(l, seg, c, prep):
        ue, kue, dg3, kdg = prep
        yd, kyd = r8.get(TS, F32)
        S.op("dve", lambda e: e.tensor_scalar(out=yd, in0=ue[:, 2:2 + TS], scalar1=dwk(l, c, 0), scalar2=None, op0=ALU.mult),
             r=kue + pvs[l][1], w=kyd)
        for j in range(1, NDT):
            S.op("dve", lambda e, j=j: e.scalar_tensor_tensor(out=yd, in0=ue[:, 2 + j:2 + j + TS], scalar=dwk(l, c, j), in1=yd,
                                                              op0=ALU.mult, op1=ALU.add), r=kue + kyd + pvs[l][1], w=kyd)
        for th in range(2):
            px, kpx = psv(PX, "PX", ((c * 2 + th) % 2) * 1024, 1024)

            def mm(e, px=px, th=th):
                for j in range(NDT, CW):
                    for tb in range(2):
                        o = 2 + j + th * 1024 + tb * 512
                        ins = e.matmul(px[:, tb * 512:(tb + 1) * 512], lhsT=dg3[:, j, :], rhs=ue[:, o:o + 512],
                                       start=(j == NDT), stop=(j == CW - 1))
                return ins
            S.op("pe", mm, r=kdg + kue, w=kpx)
            yv, kyv = B0.view(c * 4096 + th * 2048, 1024, BF16)
            S.op("dve", lambda e, yv=yv, px=px, th=th: e.scalar_tensor_tensor(
                out=yv, in0=px, scalar=pvcol(l, G_DWB, c), in1=yd[:, th * 1024:(th + 1) * 1024], op0=ALU.add, op1=ALU.add),
                r=kpx + kyd + pvs[l][1], w=kyv)

    def phase3(l, seg):
        koff = TS if (seg > 0 or pair) else 0
        nk = koff + TS
        hb = 1 if (seg > 0 or pair) else 0
        scale = HD ** -0.5
        small = Ring(R4, 2048, 12)
        vbase = {}
        nslot = 0
        for d in (1, 4, 16):
            nb = TS // (128 * d)
            for r in range(d):
                vbase[(d, r)] = nslot
                nslot += nb + hb
        assert nslot * 256 <= 24576 and 49152 + 2 * 8192 <= 65536
        def head_bufs(h):
            s = h % 2
            qT, kq = B0.view(s * 28672, TS, BF16)
            kT, kk = B0.view(s * 28672 + 4096, nk, BF16)
            acc, ka = B0.view(s * 28672 + 12288, 2 * TS, F32)
            vb, kvb = WR.view(s * 24576, nslot * 128, BF16)
            return qT, kq, kT, kk, acc, ka, vb, kvb

        def load_head(h):
            qT, kq, kT, kk, acc, ka, vb, kvb = head_bufs(h)
            vb3 = vb.rearrange("p (m f) -> p m f", f=128)
            S.dma("sp", lambda e: e.dma_start(out=qT, in_=QT[h]), r=[("QT", h)], w=kq)
            hs = slice((h % 4) * 128, (h % 4 + 1) * 128)
            if pair:
                S.dma("sp", lambda e: e.dma_start(out=kT[:, 0:TS], in_=XGK[h // 4][hs, :]), r=[("XGK", h // 4)], w=kk)
                S.dma("sp", lambda e: e.dma_start(out=kT[:, TS:2 * TS], in_=XBK[h // 4][hs, :]), r=[("KT", h)], w=kk)
            else:
                S.dma("sp", lambda e: e.dma_start(out=kT, in_=KT[h][:, seg * TS - koff: seg * TS + TS]), r=[("KT", h)], w=kk)
            vT, kvt = WR.view(49152 + (h % 2) * 8192, nk, BF16)
            if pair:
                S.dma("sp", lambda e: e.dma_start(out=vT[:, 0:TS], in_=XGV[h // 4][hs, :]), r=[("XGV", h // 4)], w=kvt)
                S.dma("sp", lambda e: e.dma_start(out=vT[:, TS:2 * TS], in_=XBV[h // 4][hs, :]), r=[("V", h)], w=kvt)
            else:
                S.dma("sp", lambda e: e.dma_start(out=vT, in_=V[h][:, seg * TS - koff: seg * TS + TS]), r=[("V", h)], w=kvt)

        slot_src = []
        for d in (1, 4, 16):
            for r in range(d):
                for m in range(TS // (128 * d) + hb):
                    slot_src.append((koff - hb * 128 * d + r + m * 128 * d, d))
        assert len(slot_src) == nslot
        NG = (nslot + 7) // 8
        tbank = [0]

        def prep_v(h, g):
            qT, kq, kT, kk, acc, ka, vb, kvb = head_bufs(h)
            vT, kvt = WR.view(49152 + (h % 2) * 8192, nk, BF16)
            s0, s1 = 8 * g, min(nslot, 8 * g + 8)
            bk = 2 + tbank[0] % 2
            tbank[0] += 1
            pt, kpt = psv(PS, "PS", bk * 512, 512, BF16)

            def tr(e):
                for k, sl in enumerate(range(s0, s1)):
                    c0, d = slot_src[sl]
                    ins = e.transpose(out=pt[:, k * 128:(k + 1) * 128], in_=vT[:, c0: c0 + 127 * d + 1: d], identity=ident)
                return ins
            S.op("pe", tr, r=kvt + k_ident, w=kpt)
            n = (s1 - s0) * 128
            dstv = vb[:, s0 * 128: s0 * 128 + n]
            kd = [(WR.name, gg) for gg in range(((h % 2) * 24576 + s0 * 256) // GRAN, ((h % 2) * 24576 + s1 * 256 - 1) // GRAN + 1)]
            if g % 2 == 0:
                S.op("act", lambda e: e.activation(out=dstv, in_=pt[:, 0:n], func=AF.Copy), r=kpt, w=kd)
            else:
                S.op("dve", lambda e: e.tensor_copy(out=dstv, in_=pt[:, 0:n]), r=kpt, w=kd)

        pending_norm = []
        load_head(0)
        for g in range(NG):
            prep_v(0, g)
        for h in range(NH):
            if h + 1 < NH:
                load_head(h + 1)
            qT, kq, kT, kk, acc, ka, vb, kvb = head_bufs(h)
            acc3 = acc.rearrange("p (a n) -> p a n", a=2)
            vb3 = vb.rearrange("p (m f) -> p m f", f=128)
            ulist = []
            for d in (1, 4, 16):
                nbo = TS // (128 * d)
                for r in range(d):
                    for n in range(nbo):
                        ulist.append((d, r, n))
            pslot = [0, 0]
            state = {}

            def stage1(u, h=h, qT=qT, kT=kT, kq=kq, kk=kk):
                d, r, n = u
                bq = n * 128 * d + r
                has_prev = (koff + bq - 128 * d) >= 0
                qv = qT[:, bq: bq + 127 * d + 1: d]
                kc = kT[:, koff + bq: koff + bq + 127 * d + 1: d]
                ps, kps = psv(PX, "PX", (pslot[0] % 3) * 512, 256)
                pslot[0] += 1
                c0 = 0 if has_prev else 128

                def mm1(e):
                    if has_prev:
                        kp = kT[:, koff + bq - 128 * d: koff + bq - d + 1: d]
                        e.matmul(ps[:, 0:128], lhsT=kp, rhs=qv, start=True, stop=True)
                    return e.matmul(ps[:, 128:256], lhsT=kc, rhs=qv, start=True, stop=True)
                S.op("pe", mm1, r=kq + kk, w=kps)
                pt, kpt = small.get(256, BF16)
                if pair and n == 0:
                    S.op("act", lambda e: e.activation(out=pt[:, 0:128], in_=ps[:, 0:128], func=AF.Exp, bias=cinfo[:, 1:2], scale=scale),
                         r=kps + k_cinfo, w=kpt)
                    S.op("act", lambda e: e.activation(out=pt[:, 128:256], in_=ps[:, 128:256], func=AF.Exp, scale=scale), r=kps, w=kpt)
                else:
                    S.op("act", lambda e: e.activation(out=pt[:, c0:256], in_=ps[:, c0:256], func=AF.Exp, scale=scale), r=kps, w=kpt)
                S.op("pool", lambda e: e.tensor_tensor(out=pt[:, c0:256], in0=pt[:, c0:256], in1=mask2[:, c0:256], op=ALU.mult),
                     r=kpt + k_mask, w=kpt)
                state[u] = (pt, kpt, has_prev, bq)

            def stage2(u, vb3=vb3, kvb=kvb, acc3=acc3, ka=ka):
                d, r, n = u
                pt, kpt, has_prev, bq = state.pop(u)
                pi_ = pslot[1] % 3
                po, kpo = psv(PX, "PX", 1536, 256) if pi_ == 0 else psv(PS, "PS", (pi_ - 1) * 512, 256)
                pslot[1] += 1
                sl = vbase[(d, r)] + n + hb

                def mm2(e):
                    if has_prev:
                        e.matmul(po[:, 0:128], lhsT=vb3[:, sl - 1, :], rhs=pt[:, 0:128], start=True, stop=False)
                    e.matmul(po[:, 0:128], lhsT=vb3[:, sl, :], rhs=pt[:, 128:256], start=(not has_prev), stop=True)
                    if has_prev:
                        e.matmul(po[:, 128:256], lhsT=ones, rhs=pt[:, 0:128], start=True, stop=False)
                    return e.matmul(po[:, 128:256], lhsT=ones, rhs=pt[:, 128:256], start=(not has_prev), stop=True)
                S.op("pe", mm2, r=kpt + kvb + k_ones, w=kpo)
                po3 = po.rearrange("p (a n) -> p a n", a=2)
                av = acc3[:, :, bq: bq + 127 * d + 1: d]
                if d == 1:
                    S.op("dve", lambda e: e.tensor_copy(out=av, in_=po3), r=kpo, w=ka)
                else:
                    S.op("dve", lambda e: e.tensor_tensor(out=av, in0=po3, in1=av, op=ALU.add), r=kpo + ka, w=ka)
            LA = 3
            gnext = 0
            for idx in range(len(ulist) + LA):
                if idx < len(ulist):
                    stage1(ulist[idx])
                if idx >= LA:
                    stage2(ulist[idx - LA])
                if idx == 5 and pending_norm:
                    pending_norm.pop(0)()
                if h + 1 < NH and idx >= 6 and idx % 4 == 0 and gnext < NG:
                    prep_v(h + 1, gnext)
                    gnext += 1
            while h + 1 < NH and gnext < NG:
                prep_v(h + 1, gnext)
                gnext += 1
            def normalise(acc=acc, ka=ka, h=h):
                S.op("act", lambda e: e.activation(out=acc[:, TS:2 * TS], in_=acc[:, TS:2 * TS], func=AF.Ln), r=ka, w=ka)
                S.op("act", lambda e: e.activation(out=acc[:, TS:2 * TS], in_=acc[:, TS:2 * TS], func=AF.Exp, scale=-1.0), r=ka, w=ka)
                at, kat = r8.get(TS, BF16)
                S.op("dve", lambda e: e.tensor_tensor(out=at, in0=acc[:, 0:TS], in1=acc[:, TS:2 * TS], op=ALU.mult), r=ka, w=kat)
                S.dma("sp", lambda e: e.dma_start(out=ATT[h], in_=at), r=kat, w=[("ATT", h)])
            pending_norm.append(normalise)
        while pending_norm:
            pending_norm.pop(0)()

    def colsum_sq(src_fn, nsrc, pt, ptname, pbase):
        for c in range(nsrc):
            sv, ksv = src_fn(c)
            sq, ksq = r4.get(TS, BF16)
            S.op("act", lambda e, sq=sq, sv=sv: e.activation(out=sq, in_=sv, func=AF.Square), r=ksv, w=ksq)
            pk = [(ptname, b) for b in range(pbase // 512, pbase // 512 + 4)]

            def mm(e, sq=sq, c=c):
                for tb in range(4):
                    ins = e.matmul(pt[:, pbase + tb * 512: pbase + (tb + 1) * 512], lhsT=ones, rhs=sq[:, tb * 512:(tb + 1) * 512],
                                   start=(c == 0), stop=(c == nsrc - 1))
                return ins
            S.op("pe", mm, r=ksq + k_ones, w=pk)

    def gate_finalize(l, src_fn, rstd, krstd, ggrp, sgsrc, aybase):
        for c in range(16):
            sv, ksv = src_fn(c)
            sg, ksg = r4.get(TS, BF16)
            S.dma("sp", lambda e, sg=sg, c=c: e.dma_start(out=sg, in_=sgsrc[c]), r=[(sgsrc.tensor.name, c)], w=ksg)
            t, kt = r8.get(TS, F32)
            S.op("dve", lambda e, t=t, sv=sv, c=c: e.scalar_tensor_tensor(out=t, in0=sv, scalar=pvcol(l, ggrp, c), in1=rstd,
                                                                         op0=ALU.mult, op1=ALU.mult),
                 r=ksv + krstd + pvs[l][1], w=kt)
            o, ko = r4.get(TS, BF16)
            S.op("pool", lambda e, o=o, t=t, sg=sg: e.tensor_tensor(out=o, in0=t, in1=sg, op=ALU.mult), r=kt + ksg, w=ko)
            S.dma("act", lambda e, o=o, c=c: e.dma_start(out=AY[aybase + c], in_=o), r=ko, w=[("AY", aybase + c)])

    def finalize_one(l, c, sv, ksv, rstd, krstd, ggrp, sgsrc, aybase):
        sg, ksg = r4.get(TS, BF16)
        S.dma("sp", lambda e: e.dma_start(out=sg, in_=sgsrc[c]), r=[(sgsrc.tensor.name, c)], w=ksg)
        t, kt = r8.get(TS, F32)
        S.op("dve", lambda e: e.scalar_tensor_tensor(out=t, in0=sv, scalar=pvcol(l, ggrp, c), in1=rstd, op0=ALU.mult, op1=ALU.mult),
             r=ksv + krstd + pvs[l][1], w=kt)
        o, ko = r4.get(TS, BF16)
        S.op("dve", lambda e: e.tensor_tensor(out=o, in0=t, in1=sg, op=ALU.mult), r=kt + ksg, w=ko)
        S.dma("act", lambda e: e.dma_start(out=AY[aybase + c], in_=o), r=ko, w=[("AY", aybase + c)])

    def phase4_conv(l, seg):
        for h in range(NH):
            sv, ksv = r4.get(TS, BF16)
            S.dma("sp", lambda e, sv=sv, h=h: e.dma_start(out=sv, in_=ATT[h]), r=[("ATT", h)], w=ksv)
            sq, ksq = r4.get(TS, BF16)
            S.op("act", lambda e, sq=sq, sv=sv: e.activation(out=sq, in_=sv, func=AF.Square), r=ksv, w=ksq)

            def mm(e, sq=sq, h=h):
                for tb in range(4):
                    ins = e.matmul(PX[:, tb * 512:(tb + 1) * 512], lhsT=ones, rhs=sq[:, tb * 512:(tb + 1) * 512],
                                   start=(h == 0), stop=(h == NH - 1))
                return ins
            S.op("pe", mm, r=ksq + k_ones, w=[("PX", b_) for b_ in range(4)])
        rstd, krstd = WR.view(49152, TS, F32)
        rsqrt_tile(rstd, krstd, PX[:, 0:TS], [("PX", b_) for b_ in range(4)], 1.0 / D)
        prep = conv_prep(l, seg, 0)
        for c in range(NCH):
            nprep = conv_prep(l, seg, c + 1) if c + 1 < NCH else None
            conv_tile(l, seg, c, prep)
            prep = nprep
            sv, ksv = r4.get(TS, BF16)
            S.dma("sp", lambda e, sv=sv, c=c: e.dma_start(out=sv, in_=ATT[c]), r=[("ATT", c)], w=ksv)
            finalize_one(l, c, sv, ksv, rstd, krstd, G_ATTO, SGA, 0)

    def phase5(l):
        def src(c):
            return B0.view(c * 4096, TS, BF16)
        for c in range(NCH):
            sv, ksv = src(c)

            def mm(e, sv=sv, c=c):
                for tb in range(4):
                    ins = e.matmul(PX[:, tb * 512:(tb + 1) * 512], lhsT=ones, rhs=sv[:, tb * 512:(tb + 1) * 512],
                                   start=(c == 0), stop=(c == NCH - 1))
                return ins
            S.op("pe", mm, r=ksv + k_ones, w=[("PX", b) for b in range(4)])
        colsum_sq(src, NCH, PS, "PS", 0)
        mean, kmean = WR.view(57344, TS, F32)
        rstd, krstd = WR.view(49152, TS, F32)
        kpx = [("PX", b) for b in range(4)]
        kps = [("PS", b) for b in range(4)]
        S.op("dve", lambda e: e.tensor_scalar(out=mean, in0=PX[:, 0:TS], scalar1=1.0 / D, scalar2=None, op0=ALU.mult),
             r=kpx, w=kmean)
        msq, kmsq = r8.get(TS, F32)
        S.op("pool", lambda e: e.tensor_tensor(out=msq, in0=mean, in1=mean, op=ALU.mult), r=kmean, w=kmsq)
        S.op("dve", lambda e: e.scalar_tensor_tensor(out=rstd, in0=PS[:, 0:TS], scalar=1.0 / D, in1=msq, op0=ALU.mult,
                                                     op1=ALU.subtract), r=kps + kmsq, w=krstd)
        rsqrt_tile(rstd, krstd, rstd, krstd, 1.0)
        S.op("dve", lambda e: e.tensor_tensor(out=mean, in0=mean, in1=rstd, op=ALU.mult), r=kmean + krstd, w=kmean)
        for th in range(2):
            for c in range(NCH):
                sv, ksv = B0.view(c * 4096 + th * 2048, 1024, BF16)
                t, kt = r4.get(1024, F32)
                S.op("dve", lambda e, t=t, sv=sv, th=th: e.tensor_tensor(out=t, in0=sv, in1=rstd[:, th * 1024:(th + 1) * 1024],
                                                                        op=ALU.mult), r=ksv + krstd, w=kt)
                S.op("dve", lambda e, t=t, th=th: e.tensor_tensor(out=t, in0=t, in1=mean[:, th * 1024:(th + 1) * 1024],
                                                                  op=ALU.subtract), r=kt + kmean, w=kt)
                S.op("act", lambda e, t=t, sv=sv, c=c: e.activation(out=sv, in_=t, func=AF.Silu, bias=pvcol(l, G_LNB, c),
                                                                    scale=pvcol(l, G_LNG, c)), r=kt + pvs[l][1], w=ksv)
        def rhs_y(c, t0, n):
            return B0.view(c * 4096 + t0 * 2, n, BF16)
        pw_loaded = {}
        wslot[0] = 1

        def pw_pre(k):
            for kk in (k, k + 1):
                if kk < 8 and kk not in pw_loaded:
                    pw_loaded[kk] = load_w(w_pw[l], (kk % 4) * 512)
            return pw_loaded[k]

        def pw_mm(i, j, th):
            wb3, kw = pw_pre(th * 4 + i)
            e_ = i * 4 + j
            sgt, ksgt = r2k.get(1024, BF16)
            S.dma("sp", lambda e: e.dma_start(out=sgt, in_=SGC[e_][:, th * 1024:(th + 1) * 1024]), r=[("SGC", e_)], w=ksgt)
            return ws_unit(wb3, kw, j, th, rhs_y, split=(i == 0 and j == 0 and th == 0)) + (sgt, ksgt)

        def pw_epi(st, e_, th):
            pa, kpa, sgt, ksgt = st
            o, ko = r4.get(1024, BF16)
            S.op("dve", lambda e: e.scalar_tensor_tensor(out=o, in0=pa, scalar=pvcol(l, G_CONVO, e_), in1=sgt, op0=ALU.mult,
                                                         op1=ALU.mult), r=kpa + ksgt + pvs[l][1], w=ko)
            S.dma("act", lambda e: e.dma_start(out=AY[16 + e_][:, th * 1024:(th + 1) * 1024], in_=o), r=ko, w=[("AY", 16 + e_)])
            sq, ksq = r4.get(1024, BF16)
            S.op("act", lambda e: e.activation(out=sq, in_=pa, func=AF.Square), r=kpa, w=ksq)

            def mms(e):
                for tb in range(2):
                    o_ = th * 1024 + tb * 512
                    ins = e.matmul(PX[:, o_:o_ + 512], lhsT=ones, rhs=sq[:, tb * 512:(tb + 1) * 512],
                                   start=(e_ == 0), stop=(e_ == NCH - 1))
                return ins
            S.op("pe", mms, r=ksq + k_ones, w=[("PX", th * 2), ("PX", th * 2 + 1)])
        prevu = None
        for th in range(2):
            for i in range(4):
                for j in range(4):
                    st = pw_mm(i, j, th)
                    if prevu is not None:
                        pw_epi(*prevu)
                    prevu = (st, i * 4 + j, th)
        pw_epi(*prevu)
        rsqrt_tile(rstd, krstd, PX[:, 0:TS], kpx, 1.0 / D)
        pr, kpr = psv(PS, "PS", 0, 16)

        def mmt(e):
            for tt in range(16):
                ins = e.matmul(pr[:, tt:tt + 1], lhsT=rstd[0:1, tt * 128:(tt + 1) * 128], rhs=one_f[0:1, 0:1], start=True, stop=True)
            return ins
        S.op("pe", mmt, r=krstd + k_onef, w=kpr)
        S.op("dve", lambda e: e.tensor_copy(out=rc16, in_=pr), r=kpr, w=k_rc16)

    def phase6(l, xsrc, xdst, tok0, hook=None):
        for th in range(2):
            ayv, kay = B0.view(0, 32 * 1024, BF16)
            ay3 = ayv.rearrange("p (c n) -> p c n", c=32)
            for c in range(32):
                v1, k1 = B0.view(c * 2048, 1024, BF16)
                S.dma("sp", lambda e, v1=v1, c=c, th=th: e.dma_start(out=v1, in_=AY[c][:, th * 1024:(th + 1) * 1024]),
                      r=[("AY", c)], w=k1)
            nxt = load_w(w_out[l], 0, nchunks=32, slots=2)
            for cb in range(4):
                wb3, kw = nxt
                if cb + 1 < 4:
                    nxt = load_w(w_out[l], (cb + 1) * 512, nchunks=32, slots=2)
                if hook is not None and th == 0 and cb == 0:
                    hook()
                for tt in range(8):
                    a = acc6[0]
                    acc6[0] = (a + 1) % 2
                    pa, kpa = psv(PS, "PS", a * 1024, 512)
                    pc, kpc = psv(PS, "PS", a * 1024 + 512, 512)
                    t0 = tok0 + th * 1024 + tt * 128
                    ttg = th * 8 + tt
                    xt, kxt = r4.get(512, F32)
                    S.dma("sp", lambda e, xt=xt, t0=t0, cb=cb: e.dma_start(out=xt, in_=xsrc[t0:t0 + 128, cb * 512:(cb + 1) * 512]),
                          w=kxt)

                    def mm(e, pa=pa, pc=pc, tt=tt, wb3=wb3):
                        for c in range(32):
                            ins = e.matmul(pa if c < 16 else pc, lhsT=ay3[:, c, tt * 128:(tt + 1) * 128], rhs=wb3[:, c, :],
                                           start=(c % 16 == 0), stop=(c % 16 == 15))
                        return ins
                    if cb == 0 and tt == 0:
                        for c in range(32):
                            S.op("pe", lambda e, pa=pa, pc=pc, tt=tt, wb3=wb3, c=c: e.matmul(
                                pa if c < 16 else pc, lhsT=ay3[:, c, tt * 128:(tt + 1) * 128], rhs=wb3[:, c, :],
                                start=(c % 16 == 0), stop=(c % 16 == 15)),
                                r=kw + B0.view(c * 2048, 1024, BF16)[1], w=(kpa if c < 16 else kpc))
                    else:
                        S.op("pe", mm, r=kw + kay, w=kpa + kpc)
                    ot, kot = r4.get(512, F32)
                    S.op("dve", lambda e, ot=ot, pc=pc, xt=xt, ttg=ttg: e.scalar_tensor_tensor(
                        out=ot, in0=pc, scalar=rc16[:, ttg:ttg + 1], in1=xt, op0=ALU.mult, op1=ALU.add), r=kpc + kxt + k_rc16, w=kot)
                    S.op("dve", lambda e, ot=ot, pa=pa: e.tensor_tensor(out=ot, in0=pa, in1=ot, op=ALU.add), r=kpa + kot, w=kot)
                    S.dma("act", lambda e, ot=ot, t0=t0, cb=cb: e.dma_start(out=xdst[t0:t0 + 128, cb * 512:(cb + 1) * 512], in_=ot),
                          r=kot, w=[("XD", l, t0 // TS)])

    done = False
    for l in range(DEPTH):
        xsrc = x_in if l == 0 else X1
        xdst = X1 if l == 0 else out
        for seg in range(nseg):
            if nseg > 1 or l == 0:
                rope_tables(seg)
            if l == 0 or nseg > 1:
                build_gb(l)
            phase1(l, xsrc, seg * TS)
            if stop_after == "p1":
                done = True
                break
            phase2(l, seg)
            if stop_after == "p2":
                done = True
                break
            if pair:
                S.cc(lambda e: e.collective_compute("AllGather", ALU.bypass, replica_groups=RGS, ins=[XBH[:, :]], outs=[XGH[:, :]]),
                     r=[("XBH", c) for c in range(NCH)], w=[("XGH",)])
            phase3(l, seg)
            if stop_after == "p3":
                done = True
                break
            phase4_conv(l, seg)
            if debug:
                for c in range(NCH):
                    S.dma("sp", lambda e, c=c: e.dma_start(out=Y[c], in_=B0.view(c * 4096, TS, BF16)[0]),
                          r=B0.view(c * 4096, TS, BF16)[1], w=[("Y", c)])
            phase5(l)
            if stop_after == "p5":
                done = True
                break
            hook = (lambda l=l: build_gb(l + 1)) if (l + 1 < DEPTH and seg == nseg - 1) else None
            phase6(l, xsrc, xdst, seg * TS, hook)
        if done or stop_after == "l0":
            break
    S.emit()
    return nc


def host_consts():
    c = np.zeros((128, 512), np.float32)
    idx = np.arange(128)
    c[idx, idx] = 1.0
    c[(idx + 64) % 128, 128 + idx] = 1.0
    kk = idx[:, None]
    qq = idx[None, :]
    c[:, 256:384] = (kk >= qq).astype(np.float32)
    c[:, 384:512] = (kk <= qq).astype(np.float32)
    return c


def pack_params(norm_g, att_out_g, conv_out_g, dw_bias, conv_ln_g, conv_ln_b, q_norm_g, k_norm_g, dw_kernel):
    pv = np.zeros((DEPTH, 128, NPV), np.float32)
    for l in range(DEPTH):
        for gi, a in enumerate((norm_g, att_out_g, conv_out_g, dw_bias, conv_ln_g, conv_ln_b)):
            pv[l, :, gi * 16:(gi + 1) * 16] = a[l].reshape(16, 128).T
        pv[l, :, 96] = q_norm_g[l]
        pv[l, :, 97] = k_norm_g[l]
        pv[l, :, 98:] = dw_kernel[l].reshape(CW, 16, 128).transpose(2, 1, 0).reshape(128, 16 * CW)
    return pv


def kernel(x, norm_g, w_in, q_norm_g, k_norm_g, dw_kernel, dw_bias, conv_ln_g, conv_ln_b, w_pw, att_out_g, conv_out_g, w_out):
    x = np.asarray(x, np.float32)
    B, SEQ = x.shape[0], x.shape[1]
    assert SEQ == 2 * TS and B == 4
    pv = pack_params(*[np.asarray(a, np.float32) for a in (norm_g, att_out_g, conv_out_g, dw_bias, conv_ln_g, conv_ln_b,
                                                             q_norm_g, k_norm_g, dw_kernel)])
    cst = host_consts()
    w_in = np.ascontiguousarray(w_in, np.float32)
    w_pw = np.ascontiguousarray(w_pw, np.float32)
    w_out = np.ascontiguousarray(w_out, np.float32)
    nc = build(1, pair=True)
    in_maps = []
    for core in range(8):
        b, half = core // 2, core % 2
        ci = np.zeros((128, 4), np.float32)
        ci[:, 0] = half * TS
        ci[:, 1] = 0.0 if half == 1 else -30000.0
        ci[:, 2] = float(half)
        in_maps.append({"x": np.ascontiguousarray(x[b, half * TS:(half + 1) * TS]), "w_in": w_in, "w_pw": w_pw, "w_out": w_out,
                        "pv": pv, "cst": cst, "cinfo": ci})
    res = run_bass_kernel_spmd(nc, in_maps, core_ids=list(range(8)))
    out = np.zeros((B, SEQ, D), np.float32)
    for core in range(8):
        b, half = core // 2, core % 2
        out[b, half * TS:(half + 1) * TS] = np.asarray(res.results[core]["out"]).reshape(TS, D)
    return out
```

```python
import math
import numpy as np
import concourse.bass as bass
import concourse.mybir as mybir
from concourse.bass_utils import run_bass_kernel_spmd

F32 = mybir.dt.float32
BF16 = mybir.dt.bfloat16
I32 = mybir.dt.int32
AF = mybir.ActivationFunctionType
ALU = mybir.AluOpType

D = 2048
DIN = 14336
TS = 2048
NCH = 16
HD = 128
NH = 16
CW = 31
EPS = 1e-6
DEPTH = 2
NPV = 6 * 16 + 2 + 16 * CW
GRAN = 2048
PIPE = True
RGS = [[0, 1], [2, 3], [4, 5], [6, 7]]


class Sched:
    NDS = 12

    def __init__(self, nc):
        self.nc = nc
        self.ops = []
        self.esem = {e: nc.alloc_semaphore("es_" + e) for e in ("pe", "act", "dve", "pool")}
        self.dsem = {q: [nc.alloc_semaphore("ds_%s%d" % (q, i)) for i in range(self.NDS)]
                     for q in ("sp", "pool", "act")}
        self.ccsem = nc.alloc_semaphore("ccsem")
        self.cccount = 0

    def op(self, eng, fn, r=(), w=()):
        self.ops.append(dict(eng=eng, fn=fn, r=tuple(r), w=tuple(w), dma=False))

    def dma(self, q, fn, r=(), w=()):
        self.ops.append(dict(eng=q, fn=fn, r=tuple(r), w=tuple(w), dma=True))

    def cc(self, fn, r=(), w=()):
        self.ops.append(dict(eng="pool", fn=fn, r=tuple(r), w=tuple(w), dma=True, cc=True))

    def analyse(self):
        ops = self.ops
        last_w = {}
        readers = {}
        for i, o in enumerate(ops):
            pr = tuple(k for k in o["r"] if k[0] in ("PS", "PX"))
            if pr:
                o["w"] = tuple(o["w"]) + pr
                o["r"] = tuple(k for k in o["r"] if k[0] not in ("PS", "PX"))
            deps = set()
            for k in o["r"]:
                if k in last_w:
                    deps.add(last_w[k])
            for k in o["w"]:
                if k in last_w:
                    deps.add(last_w[k])
                deps.update(readers.get(k, ()))
            deps.discard(i)
            o["deps"] = deps
            for k in o["r"]:
                readers.setdefault(k, []).append(i)
            for k in o["w"]:
                last_w[k] = i
                readers[k] = []
        for o in ops:
            o["ms"] = False
        for o in ops:
            for d in o["deps"]:
                p = ops[d]
                if p["dma"]:
                    continue
                if p["eng"] == "pe" and o["eng"] == "pe" and not o["dma"]:
                    continue
                p["ms"] = True
        cnt = {e: 0 for e in self.esem}
        dstate = {q: [0] * self.NDS for q in self.dsem}
        drr = {q: 0 for q in self.dsem}
        for o in ops:
            if o.get("cc"):
                o["dsem"] = self.ccsem
                o["guard"] = self.cccount
                self.cccount += 1
                o["target"] = self.cccount
            elif o["dma"]:
                q = o["eng"]
                j = drr[q]
                drr[q] = (j + 1) % self.NDS
                o["dsem"] = self.dsem[q][j]
                o["guard"] = dstate[q][j]
                dstate[q][j] += 16
                o["target"] = dstate[q][j]
            elif o["ms"]:
                cnt[o["eng"]] += 1
                o["count"] = cnt[o["eng"]]
        self.final_dma = {q: list(dstate[q]) for q in dstate}
        waited = {}
        for o in ops:
            e = o["eng"]
            wl = {}
            if o["dma"] and o["guard"] > 0:
                wl[id(o["dsem"])] = (o["dsem"], o["guard"])
            for d in sorted(o["deps"]):
                p = ops[d]
                if p["dma"]:
                    s, v = p["dsem"], p["target"]
                else:
                    if p["eng"] == "pe" and e == "pe" and not o["dma"]:
                        continue
                    s, v = self.esem[p["eng"]], p["count"]
                if id(s) not in wl or wl[id(s)][1] < v:
                    wl[id(s)] = (s, v)
            out = []
            wd = waited.setdefault(e, {})
            for sid, (s, v) in wl.items():
                if wd.get(sid, 0) >= v:
                    continue
                wd[sid] = v
                out.append((s, v))
            o["waits"] = out

    def emit(self):
        self.analyse()
        nc = self.nc
        per = {e: [] for e in ("sp", "act", "pe", "dve", "pool")}
        for o in self.ops:
            per[o["eng"]].append(o)

        def make_body(ename):
            def body(e):
                for o in per[ename]:
                    for (s, v) in o["waits"]:
                        e.wait_ge(s, v)
                    inst = o["fn"](e)
                    if o.get("cc"):
                        inst.then_inc(o["dsem"], 1)
                    elif o["dma"]:
                        inst.then_inc(o["dsem"], 16)
                    elif o["ms"]:
                        inst.then_inc(self.esem[ename], 1)
                if ename == "sp":
                    for q in self.dsem:
                        for j, s in enumerate(self.dsem[q]):
                            if self.final_dma[q][j] > 0:
                                e.wait_ge(s, self.final_dma[q][j])
                    if self.cccount > 0:
                        e.wait_ge(self.ccsem, self.cccount)
            return body

        with nc.Block() as block:
            block.sync(make_body("sp"))
            block.scalar(make_body("act"))
            block.tensor(make_body("pe"))
            block.vector(make_body("dve"))
            block.gpsimd(make_body("pool"))


class Region:
    def __init__(self, nc, name, nbytes):
        self.t = nc.alloc_sbuf_tensor(name, [128, nbytes // 4], F32)
        self.name = name
        self.nbytes = nbytes

    def view(self, off, n, dt):
        sz = 2 if dt == BF16 else 4
        assert off % 4 == 0 and (n * sz) % 4 == 0 and off + n * sz <= self.nbytes, (off, n, self.nbytes)
        ap = self.t[:, off // 4:(off + n * sz) // 4]
        if dt != F32:
            ap = ap.bitcast(dt)
        keys = [(self.name, g) for g in range(off // GRAN, (off + n * sz - 1) // GRAN + 1)]
        return ap, keys


def build(nseg, debug=False, stop_after=None, pair=False):
    nc = bass.Bass("TRN2", target_bir_lowering=False)
    S = Sched(nc)
    NT = nseg * TS
    okind = "ExternalOutput" if debug else "Internal"

    x_in = nc.dram_tensor("x", [NT, D], F32, kind="ExternalInput").ap()
    w_in = nc.dram_tensor("w_in", [DEPTH, D, DIN], F32, kind="ExternalInput").ap()
    w_pw = nc.dram_tensor("w_pw", [DEPTH, D, D], F32, kind="ExternalInput").ap()
    w_out = nc.dram_tensor("w_out", [DEPTH, 2 * D, D], F32, kind="ExternalInput").ap()
    pv_in = nc.dram_tensor("pv", [DEPTH, 128, NPV], F32, kind="ExternalInput").ap()
    cst_in = nc.dram_tensor("cst", [128, 512], F32, kind="ExternalInput").ap()
    cinfo_in = nc.dram_tensor("cinfo", [128, 4], F32, kind="ExternalInput").ap()
    out = nc.dram_tensor("out", [NT, D], F32, kind="ExternalOutput").ap()

    X1 = nc.dram_tensor("X1", [NT, D], F32, kind=okind).ap()
    QT = nc.dram_tensor("QT", [NH, 128, TS], BF16, kind=okind).ap()
    if pair:
        assert nseg == 1
        XBK = [nc.dram_tensor("XBK%d" % i, [512, TS], BF16).ap() for i in range(4)]
        XGK = [nc.dram_tensor("XGK%d" % i, [1024, TS], BF16).ap() for i in range(4)]
        XBV = [nc.dram_tensor("XBV%d" % i, [512, TS], BF16).ap() for i in range(4)]
        XGV = [nc.dram_tensor("XGV%d" % i, [1024, TS], BF16).ap() for i in range(4)]
        XBH = nc.dram_tensor("XBH", [D, 32], BF16).ap()
        XGH = nc.dram_tensor("XGH", [2 * D, 32], BF16).ap()
        KT = V = None
    else:
        KT = nc.dram_tensor("KT", [NH, 128, NT], BF16, kind=okind).ap()
        V = nc.dram_tensor("V", [NH, 128, NT], BF16, kind=okind).ap()
    SGA = nc.dram_tensor("SGA", [NH, 128, TS], BF16, kind="Internal").ap()
    SGC = nc.dram_tensor("SGC", [NCH, 128, TS], BF16, kind="Internal").ap()
    Y = nc.dram_tensor("Y", [NCH, 128, TS], BF16, kind=okind).ap()
    ATT = nc.dram_tensor("ATT", [NH, 128, TS], BF16, kind=okind).ap()
    CV = nc.dram_tensor("CV", [NCH, 128, TS], BF16, kind="Internal").ap()
    AY = nc.dram_tensor("AY", [2 * NCH, 128, TS], BF16, kind=okind).ap()
    U = nc.dram_tensor("U", [NCH, 128, TS], BF16, kind="Internal").ap()
    UH = nc.dram_tensor("UH", [NCH, 128, 32], BF16, kind="Internal").ap()

    B0 = Region(nc, "B0", 65536)
    WR = Region(nc, "WR", 65536)
    RP = Region(nc, "RP", 16384)
    R4 = Region(nc, "R4", 24576)
    R8 = Region(nc, "R8", 16384)
    MS = Region(nc, "MS", 8192)
    CVR = Region(nc, "CVR", 3 * 4160)
    PS = nc.alloc_psum_tensor("PSA", [128, 2048], F32)
    PX = nc.alloc_psum_tensor("PSX", [128, 2048], F32)

    def psv(t, name, off, n, dt=F32):
        ap = t[:, off:off + n]
        if dt == BF16:
            ap = ap.bitcast(BF16)
        keys = [(name, b) for b in range(off // 512, (off + n - 1) // 512 + 1)]
        return ap, keys

    class Ring:
        def __init__(self, reg, size, count, base=0):
            self.reg, self.size, self.count, self.i, self.base = reg, size, count, 0, base

        def get(self, n, dt):
            off = self.base + self.i * self.size
            self.i = (self.i + 1) % self.count
            return self.reg.view(off, n, dt)

    r4 = Ring(R4, 4096, 6)
    r8 = Ring(R8, 8192, 2)
    r2k = Ring(R8, 2048, 8)

    ident, k_ident = MS.view(0, 128, BF16)
    ones, k_ones = MS.view(256, 128, BF16)
    swp, k_swp = MS.view(512, 128, BF16)
    mask2, k_mask = MS.view(768, 256, BF16)
    pvs = [MS.view(2048 + l * 2560, NPV, F32) for l in range(DEPTH)]
    sc_off = 2048 + 2 * 2560
    sgn2, k_sgn2 = MS.view(sc_off, 1, F32)
    invf, k_invf = MS.view(sc_off + 4, 1, F32)
    gq2 = [MS.view(sc_off + 8 + 8 * l, 2, F32) for l in range(DEPTH)]
    ssq_t = [MS.view(sc_off + 64 + 8 * i, 1, F32) for i in range(2)]
    rstd_t = [MS.view(sc_off + 128 + 8 * i, 1, F32) for i in range(2)]
    negpi, k_negpi = MS.view(sc_off + 192, 1, F32)
    cinfo, k_cinfo = MS.view(sc_off + 256, 4, F32)
    rc16, k_rc16 = MS.view(sc_off + 448, 16, F32)
    one_f, k_onef = MS.view(sc_off + 512, 1, F32)
    cosT, k_cos = RP.view(0, TS, F32)
    sinT, k_sin = RP.view(8192, TS, F32)

    cst_f, k_cstf = r4.get(512, F32)
    S.dma("sp", lambda e: e.dma_start(out=cst_f, in_=cst_in[:, :]), w=k_cstf)
    S.op("dve", lambda e: e.tensor_copy(out=ident, in_=cst_f[:, 0:128]), r=k_cstf, w=k_ident)
    S.op("dve", lambda e: e.tensor_copy(out=swp, in_=cst_f[:, 128:256]), r=k_cstf, w=k_swp)
    S.op("dve", lambda e: e.tensor_copy(out=mask2, in_=cst_f[:, 256:512]), r=k_cstf, w=k_mask)
    S.op("dve", lambda e: e.memset(ones, 1.0), w=k_ones)
    S.op("dve", lambda e: e.memset(one_f, 1.0), w=k_onef)
    S.dma("sp", lambda e: e.dma_start(out=cinfo, in_=cinfo_in[:, :]), w=k_cinfo)
    S.op("dve", lambda e: e.memset(negpi, EPS), w=k_negpi)
    for l in range(DEPTH):
        S.dma("sp", lambda e, l=l: e.dma_start(out=pvs[l][0], in_=pv_in[l]), w=pvs[l][1])
        S.op("dve", lambda e, l=l: e.tensor_scalar(out=gq2[l][0], in0=pvs[l][0][:, 96:98], scalar1=1.0,
                                                   scalar2=None, op0=ALU.mult), r=pvs[l][1], w=gq2[l][1])
    pidx_i, k_pi = r4.get(1, I32)
    pidx_f, k_pf = r4.get(1, F32)
    S.op("pool", lambda e: e.iota(pidx_i, pattern=[[0, 1]], base=0, channel_multiplier=1), w=k_pi)
    S.op("dve", lambda e: e.tensor_copy(out=pidx_f, in_=pidx_i), r=k_pi, w=k_pf)
    S.op("dve", lambda e: e.tensor_scalar(out=sgn2, in0=pidx_f, scalar1=64.0, scalar2=2.0, op0=ALU.is_ge, op1=ALU.mult),
         r=k_pf, w=k_sgn2)
    S.op("dve", lambda e: e.tensor_scalar(out=sgn2, in0=sgn2, scalar1=-1.0, scalar2=None, op0=ALU.add), r=k_sgn2, w=k_sgn2)
    pm, k_pm = r4.get(1, F32)
    S.op("dve", lambda e: e.tensor_scalar(out=pm, in0=pidx_f, scalar1=64.0, scalar2=-64.0, op0=ALU.is_ge, op1=ALU.mult),
         r=k_pf, w=k_pm)
    S.op("dve", lambda e: e.tensor_tensor(out=pm, in0=pm, in1=pidx_f, op=ALU.add), r=k_pf + k_pm, w=k_pm)
    S.op("act", lambda e: e.activation(out=invf, in_=pm, func=AF.Exp, scale=-2.0 * math.log(10000.0) / 128.0),
         r=k_pm, w=k_invf)

    def pvcol(l, grp, c):
        return pvs[l][0][:, grp * 16 + c: grp * 16 + c + 1]
    G_NORM, G_ATTO, G_CONVO, G_DWB, G_LNG, G_LNB = range(6)

    def rsqrt_tile(dst, kdst, src, ksrc, scale):
        S.op("act", lambda e: e.activation(out=dst, in_=src, func=AF.Ln, bias=negpi, scale=scale), r=ksrc + k_negpi, w=kdst)
        S.op("act", lambda e: e.activation(out=dst, in_=dst, func=AF.Exp, scale=-0.5), r=kdst, w=kdst)

    def dwk(l, c, j):
        o = 98 + c * CW + j
        return pvs[l][0][:, o:o + 1]

    def rope_tables(seg):
        C1 = 6.28125
        C2 = 2 * math.pi - C1
        pos_i, k1 = B0.view(0, TS, I32)
        S.op("pool", lambda e: e.iota(pos_i, pattern=[[1, TS]], base=seg * TS, channel_multiplier=0), w=k1)
        ang, k2 = B0.view(8192, TS, F32)
        S.op("dve", lambda e: e.tensor_copy(out=ang, in_=pos_i), r=k1, w=k2)
        if pair:
            S.op("dve", lambda e: e.tensor_scalar(out=ang, in0=ang, scalar1=cinfo[:, 0:1], scalar2=None, op0=ALU.add),
                 r=k2 + k_cinfo, w=k2)
        S.op("dve", lambda e: e.tensor_scalar(out=ang, in0=ang, scalar1=invf, scalar2=None, op0=ALU.mult),
             r=k2 + k_invf, w=k2)
        for which, phase in ((0, 0.0), (1, math.pi / 2)):
            dst, kd = (sinT, k_sin) if which == 0 else (cosT, k_cos)
            base = 16384 + which * 24576
            angp, ka = B0.view(base, TS, F32)
            ki, kki = B0.view(base + 8192, TS, I32)
            kf, kkf = B0.view(base + 16384, TS, F32)
            S.op("dve", lambda e, angp=angp, phase=phase: e.tensor_scalar(out=angp, in0=ang, scalar1=phase, scalar2=None,
                                                                          op0=ALU.add), r=k2, w=ka)
            S.op("dve", lambda e, angp=angp, kf=kf: e.tensor_scalar(out=kf, in0=angp, scalar1=1.0 / (2 * math.pi), scalar2=None,
                                                                    op0=ALU.mult), r=ka, w=kkf)
            S.op("dve", lambda e, ki=ki, kf=kf: e.tensor_copy(out=ki, in_=kf), r=kkf, w=kki)
            S.op("dve", lambda e, ki=ki, kf=kf: e.tensor_copy(out=kf, in_=ki), r=kki, w=kkf)
            S.op("dve", lambda e, angp=angp, kf=kf: e.scalar_tensor_tensor(out=angp, in0=kf, scalar=-C1, in1=angp, op0=ALU.mult,
                                                                           op1=ALU.add), r=ka + kkf, w=ka)
            S.op("dve", lambda e, angp=angp, kf=kf: e.scalar_tensor_tensor(out=angp, in0=kf, scalar=-C2, in1=angp, op0=ALU.mult,
                                                                           op1=ALU.add), r=ka + kkf, w=ka)
            S.op("dve", lambda e, angp=angp, kf=kf: e.tensor_scalar(out=kf, in0=angp, scalar1=math.pi, scalar2=-2 * math.pi,
                                                                    op0=ALU.is_gt, op1=ALU.mult), r=ka, w=kkf)
            S.op("dve", lambda e, angp=angp, kf=kf: e.tensor_tensor(out=angp, in0=angp, in1=kf, op=ALU.add), r=ka + kkf, w=ka)
            S.op("dve", lambda e, angp=angp, kf=kf: e.tensor_scalar(out=kf, in0=angp, scalar1=-math.pi, scalar2=2 * math.pi,
                                                                    op0=ALU.is_lt, op1=ALU.mult), r=ka, w=kkf)
            S.op("dve", lambda e, angp=angp, kf=kf: e.tensor_tensor(out=angp, in0=angp, in1=kf, op=ALU.add), r=ka + kkf, w=ka)
            S.op("act", lambda e, angp=angp, dst=dst: e.activation(out=dst, in_=angp, func=AF.Sin), r=ka, w=kd)
        S.op("dve", lambda e: e.tensor_scalar(out=sinT, in0=sinT, scalar1=sgn2, scalar2=None, op0=ALU.mult),
             r=k_sin + k_sgn2, w=k_sin)

    def hT(c, t0, n):
        ap, keys = B0.view(c * 4096 + t0 * 2, n, BF16)
        return ap, keys

    def build_gb(l):
        gb, kgb = CVR.view(0, 16 * 128, BF16)
        for c in range(16):
            S.op("pool", lambda e, c=c: e.tensor_scalar(out=gb[:, c * 128:(c + 1) * 128], in0=ones, scalar1=pvcol(l, G_NORM, c),
                                                        scalar2=None, op0=ALU.mult), r=k_ones + pvs[l][1], w=kgb)

    def phase1(l, xsrc, tok0):
        ssq16, kss = MS.view(sc_off + 320, 16, F32)
        rstd16, krs = MS.view(sc_off + 384, 16, F32)
        S.op("dve", lambda e: e.memset(ssq16, 0.0), w=kss)
        gb, kgb = CVR.view(0, 16 * 128, BF16)
        hall, _ = B0.view(0, 16 * TS, BF16)
        h3 = hall.rearrange("p (c t) -> p c t", c=16)

        def load(tt):
            xt, kx = r8.get(D, F32)
            S.dma("sp", lambda e: e.dma_start(out=xt, in_=xsrc[tok0 + tt * 128: tok0 + (tt + 1) * 128, :]), w=kx)
            return xt, kx
        nxt = load(0)
        for tt in range(16):
            xt, kx = nxt
            junk, kj = r4.get(D, BF16)
            S.op("act", lambda e, xt=xt, junk=junk, tt=tt: e.activation(out=junk, in_=xt, func=AF.Square, accum_out=ssq16[:, tt:tt + 1]),
                 r=kx + kss, w=kj + [("ssq", tt)])
            S.op("act", lambda e, tt=tt: e.activation(out=rstd16[:, tt:tt + 1], in_=ssq16[:, tt:tt + 1], func=AF.Ln, bias=negpi, scale=1.0 / D),
                 r=[("ssq", tt)] + k_negpi, w=[("rstd", tt)])
            S.op("act", lambda e, tt=tt: e.activation(out=rstd16[:, tt:tt + 1], in_=rstd16[:, tt:tt + 1], func=AF.Exp, scale=-0.5),
                 r=[("rstd", tt)], w=[("rstd", tt)])
            xn, kxn = r4.get(D, BF16)
            S.op("act", lambda e, xt=xt, xn=xn, tt=tt: e.activation(out=xn, in_=xt, func=AF.Copy, scale=rstd16[:, tt:tt + 1]),
                 r=kx + [("rstd", tt)], w=kxn)
            if tt + 1 < 16:
                nxt = load(tt + 1)
            for g in range(4):
                pt, kpt = psv(PX, "PX", ((tt * 4 + g) % 4) * 512, 256, BF16)

                def tr(e, xn=xn, pt=pt, g=g):
                    for k in range(4):
                        c = g * 4 + k
                        ins = e.transpose(out=pt[:, k * 128:(k + 1) * 128], in_=xn[:, c * 128:(c + 1) * 128], identity=ident)
                    return ins
                S.op("pe", tr, r=kxn + k_ident, w=kpt)
                kh = []
                for k in range(4):
                    kh += hT(g * 4 + k, tt * 128, 128)[1]
                S.op("dve", lambda e, pt=pt, g=g, tt=tt: e.tensor_tensor(
                    out=h3[:, g * 4:(g + 1) * 4, tt * 128:(tt + 1) * 128], in0=pt.rearrange("p (c t) -> p c t", c=4),
                    in1=gb[:, g * 512:(g + 1) * 512].rearrange("p (c t) -> p c t", c=4), op=ALU.mult), r=kpt + kgb, w=kh)

    wslot = [0, 0]

    def load_w(src, col0, nchunks=16, ncols=512, slots=1):
        if slots == 1:
            i = wslot[0]
            wslot[0] = (i + 1) % 3
        else:
            i = wslot[1]
            wslot[1] = (i + 2) % 4
        wb, kw = WR.view(i * 16384, nchunks * ncols, BF16)
        wb3 = wb.rearrange("p (c n) -> p c n", c=nchunks)
        sv = src.rearrange("(c p) n -> p c n", p=128)
        for c0 in range(0, nchunks, 4):
            S.dma("pool", lambda e, c0=c0: e.dma_start(out=wb3[:, c0:c0 + 4, :], in_=sv[:, c0:c0 + 4, col0:col0 + ncols]),
                  w=kw)
        return wb3, kw

    acc_i = [0]
    acc6 = [0]

    def ws_unit(wb3, kw, j, th, rhs_fn, split=False):
        a = acc_i[0]
        acc_i[0] ^= 1
        pa, kpa = psv(PS, "PS", a * 1024, 1024)
        rk = []
        for c in range(16):
            rk += rhs_fn(c, th * 1024, 1024)[1]

        def mm(e):
            for c in range(16):
                for tb in range(2):
                    rv = rhs_fn(c, th * 1024 + tb * 512, 512)[0]
                    ins = e.matmul(pa[:, tb * 512:(tb + 1) * 512], lhsT=wb3[:, c, j * 128:(j + 1) * 128], rhs=rv,
                                   start=(c == 0), stop=(c == 15))
            return ins
        if not split:
            S.op("pe", mm, r=kw + rk, w=kpa)
        else:
            for c in range(16):
                def mmc(e, c=c):
                    for tb in range(2):
                        rv = rhs_fn(c, th * 1024 + tb * 512, 512)[0]
                        ins = e.matmul(pa[:, tb * 512:(tb + 1) * 512], lhsT=wb3[:, c, j * 128:(j + 1) * 128], rhs=rv,
                                       start=(c == 0), stop=(c == 15))
                    return ins
                S.op("pe", mmc, r=kw + rhs_fn(c, th * 1024, 1024)[1], w=kpa)
        return pa, kpa

    def qk_part1(pa, kpa, l, which):
        sqb, ksq = r2k.get(1024, BF16)
        S.op("act", lambda e: e.activation(out=sqb, in_=pa, func=AF.Square), r=kpa, w=ksq)
        qg, kqg = r2k.get(1024, BF16)
        S.op("act", lambda e: e.activation(out=qg, in_=pa, func=AF.Copy, scale=gq2[l][0][:, which:which + 1]),
             r=kpa + gq2[l][1], w=kqg)
        return sqb, ksq, qg, kqg

    def qk_epilogue(l, sqb, ksq, qg, kqg, which, h, th, tokg0):
        px1, kpx1 = psv(PX, "PX", 0, 1024)
        px2, kpx2 = psv(PX, "PX", 1024, 1024)

        def mm1(e):
            for tb in range(2):
                ins = e.matmul(px1[:, tb * 512:(tb + 1) * 512], lhsT=ones, rhs=sqb[:, tb * 512:(tb + 1) * 512], start=True, stop=True)
            return ins
        S.op("pe", mm1, r=ksq + k_ones, w=kpx1)

        def mm2(e):
            for tb in range(2):
                ins = e.matmul(px2[:, tb * 512:(tb + 1) * 512], lhsT=swp, rhs=qg[:, tb * 512:(tb + 1) * 512], start=True, stop=True)
            return ins
        S.op("pe", mm2, r=kqg + k_swp, w=kpx2)
        rs, krs = r4.get(1024, F32)
        rsqrt_tile(rs, krs, px1, kpx1, 1.0 / HD)
        t1, kt1 = r4.get(1024, F32)
        S.op("dve", lambda e: e.tensor_tensor(out=t1, in0=qg, in1=cosT[:, th * 1024:(th + 1) * 1024], op=ALU.mult),
             r=kqg + k_cos, w=kt1)
        t2, kt2 = r4.get(1024, F32)
        S.op("dve", lambda e: e.tensor_tensor(out=t2, in0=px2, in1=sinT[:, th * 1024:(th + 1) * 1024], op=ALU.mult),
             r=kpx2 + k_sin, w=kt2)
        S.op("dve", lambda e: e.tensor_tensor(out=t1, in0=t1, in1=t2, op=ALU.add), r=kt1 + kt2, w=kt1)
        qo, kqo = r4.get(1024, BF16)
        S.op("dve", lambda e: e.tensor_tensor(out=qo, in0=t1, in1=rs, op=ALU.mult), r=kt1 + krs, w=kqo)
        if which == 0:
            S.dma("sp", lambda e: e.dma_start(out=QT[h][:, th * 1024:(th + 1) * 1024], in_=qo), r=kqo, w=[("QT", h)])
        else:
            if pair:
                kdst = XBK[h // 4][(h % 4) * 128:(h % 4 + 1) * 128, th * 1024:(th + 1) * 1024]
            else:
                kdst = KT[h][:, tokg0 + th * 1024: tokg0 + (th + 1) * 1024]
            S.dma("sp", lambda e: e.dma_start(out=kdst, in_=qo), r=kqo, w=[("KT", h)])

    def gate_epilogue(pa, kpa, dst, idx, th):
        sg, ksg = r4.get(1024, BF16)
        S.op("act", lambda e: e.activation(out=sg, in_=pa, func=AF.Silu), r=kpa, w=ksg)
        S.dma("sp", lambda e: e.dma_start(out=dst[idx][:, th * 1024:(th + 1) * 1024], in_=sg), r=ksg,
              w=[(dst.tensor.name, idx)])

    def phase2(l, seg):
        tokg0 = seg * TS
        wl = w_in[l]

        def rhs_h(c, t0, n):
            return hT(c, t0, n)

        blocks = []
        for i in range(4):
            blocks += [("q", i), ("k", i), ("v", i)]
        blocks += [("ga", i) for i in range(4)]
        for i in range(4):
            blocks += [("gb", i), ("ua", i)]
        blocks += [("gc", i) for i in range(4)]
        colbase = {"q": 0, "k": 2048, "v": 4096, "ga": 6144, "ua": 8192, "gb": 10240, "gc": 12288}
        loaded = {}

        def prefetch(bi):
            if bi < len(blocks) and bi not in loaded:
                kind, i = blocks[bi]
                loaded[bi] = load_w(wl, colbase[kind] + i * 512)

        units = []
        for bi, (kind, i) in enumerate(blocks):
            first = [True]

            def pre(bi=bi, first=first):
                if first[0]:
                    first[0] = False
                    prefetch(bi)
                    prefetch(bi + 1)
                    prefetch(bi + 2)
                return loaded[bi]
            for j in range(4):
                for th in range(2):
                    def mmw(pre=pre, j=j, th=th, kind=kind):
                        wb3, kw = pre()
                        pa, kpa = ws_unit(wb3, kw, j, th, rhs_h)
                        if kind in ("q", "k"):
                            return qk_part1(pa, kpa, l, 0 if kind == "q" else 1)
                        return pa, kpa
                    if kind in ("q", "k"):
                        def ep(st, kind=kind, h=i * 4 + j, th=th, i=i, j=j):
                            qk_epilogue(l, st[0], st[1], st[2], st[3], 0 if kind == "q" else 1, h, th, tokg0)
                            if pair and kind == "k" and j == 3 and th == 1:
                                S.cc(lambda e: e.collective_compute("AllGather", ALU.bypass, replica_groups=RGS, ins=[XBK[i][:, :]],
                                                                    outs=[XGK[i][:, :]]),
                                     r=[("KT", hh) for hh in range(4 * i, 4 * i + 4)], w=[("XGK", i)])
                    elif kind == "v":
                        def ep(st, h=i * 4 + j, th=th, i=i, j=j):
                            vt, kvt = r4.get(1024, BF16)
                            S.op("act", lambda e: e.activation(out=vt, in_=st[0], func=AF.Copy), r=st[1], w=kvt)
                            if pair:
                                vdst = XBV[i][j * 128:(j + 1) * 128, th * 1024:(th + 1) * 1024]
                            else:
                                vdst = V[h][:, tokg0 + th * 1024: tokg0 + (th + 1) * 1024]
                            S.dma("sp", lambda e: e.dma_start(out=vdst, in_=vt), r=kvt, w=[("V", h)])
                            if pair and j == 3 and th == 1:
                                S.cc(lambda e: e.collective_compute("AllGather", ALU.bypass, replica_groups=RGS, ins=[XBV[i][:, :]],
                                                                    outs=[XGV[i][:, :]]),
                                     r=[("V", hh) for hh in range(4 * i, 4 * i + 4)], w=[("XGV", i)])
                    elif kind in ("ga", "gc"):
                        def ep(st, kind=kind, idx=i * 4 + j, th=th):
                            gate_epilogue(st[0], st[1], SGA if kind == "ga" else SGC, idx, th)
                    elif kind == "gb":
                        def ep(st, j=j, th=th):
                            sb, ksb = WR.view(49152 + (j * 2 + th) * 2048, 1024, BF16)
                            S.op("act", lambda e: e.activation(out=sb, in_=st[0], func=AF.Sigmoid), r=st[1], w=ksb)
                    else:
                        def ep(st, c=i * 4 + j, j=j, th=th):
                            sb, ksb = WR.view(49152 + (j * 2 + th) * 2048, 1024, BF16)
                            ut, kut = r4.get(1024, BF16)
                            S.op("dve", lambda e: e.tensor_tensor(out=ut, in0=st[0], in1=sb, op=ALU.mult), r=st[1] + ksb, w=kut)
                            S.dma("sp", lambda e: e.dma_start(out=U[c][:, th * 1024:(th + 1) * 1024], in_=ut), r=kut, w=[("U", c)])
                            if pair and th == 1:
                                S.dma("sp", lambda e: e.dma_start(out=XBH[c * 128:(c + 1) * 128, :], in_=ut[:, 992:1024]),
                                      r=kut, w=[("XBH", c)])
                    units.append((mmw, ep))
        prev = None
        for (mmf, epf) in units:
            st = mmf()
            if not PIPE:
                epf(st)
                continue
            if prev is not None:
                prev[0](prev[1])
            prev = (epf, st)
        if PIPE:
            prev[0](prev[1])

    def conv_prep(l, seg, c):
        ue, kue = CVR.view((c % 3) * 4160, 2080, BF16)
        if pair:
            S.dma("sp", lambda e: e.dma_start(out=ue[:, 0:32], in_=XGH[c * 128:(c + 1) * 128, :]), r=[("XGH",)], w=kue)
            S.op("pool", lambda e: e.tensor_scalar(out=ue[:, 0:32], in0=ue[:, 0:32], scalar1=cinfo[:, 2:3], scalar2=None,
                                                   op0=ALU.mult), r=kue + k_cinfo, w=kue)
        elif seg == 0:
            S.op("pool", lambda e: e.memset(ue[:, 0:32], 0.0), w=kue)
        else:
            S.dma("sp", lambda e: e.dma_start(out=ue[:, 0:32], in_=UH[c]), r=[("UH", c)], w=kue)
        S.dma("sp", lambda e: e.dma_start(out=ue[:, 32:32 + TS], in_=U[c]), r=[("U", c)], w=kue)
        if seg + 1 < nseg:
            S.dma("sp", lambda e: e.dma_start(out=UH[c], in_=ue[:, TS:TS + 32]), r=kue, w=[("UH", c)])
        dg, kdg = WR.view((c % 2) * 8192, CW * 128, BF16)
        dg3 = dg.rearrange("p (j n) -> p j n", j=CW)
        wv = pvs[l][0][:, 98 + c * CW: 98 + (c + 1) * CW]
        S.op("pool", lambda e: e.tensor_tensor(
            out=dg3, in0=ident.unsqueeze(1).broadcast_to([128, CW, 128]), in1=wv.unsqueeze(2).broadcast_to([128, CW, 128]),
            op=ALU.mult), r=k_ident + pvs[l][1], w=kdg)
        return ue, kue, dg3, kdg

    NDT = 6

    def conv_tile(l, seg, c, prep):
        ue, kue, dg3, kdg = prep
        yd, kyd = r8.get(TS, F32)
        S.op("dve", lambda e: e.tensor_scalar(out=yd, in0=ue[:, 2:2 + TS], scalar1=dwk(l, c, 0), scalar2=None, op0=ALU.mult),
             r=kue + pvs[l][1], w=kyd)
        for j in range(1, NDT):
            S.op("dve", lambda e, j=j: e.scalar_tensor_tensor(out=yd, in0=ue[:, 2 + j:2 + j + TS], scalar=dwk(l, c, j), in1=yd,
                                                              op0=ALU.mult, op1=ALU.add), r=kue + kyd + pvs[l][1], w=kyd)
        for th in range(2):
            px, kpx = psv(PX, "PX", ((c * 2 + th) % 2) * 1024, 1024)

            def mm(e, px=px, th=th):
                for j in range(NDT, CW):
                    for tb in range(2):
                        o = 2 + j + th * 1024 + tb * 512
                        ins = e.matmul(px[:, tb * 512:(tb + 1) * 512], lhsT=dg3[:, j, :], rhs=ue[:, o:o + 512],
                                       start=(j == NDT), stop=(j == CW - 1))
                return ins
            S.op("pe", mm, r=kdg + kue, w=kpx)
            yv, kyv = B0.view(c * 4096 + th * 2048, 1024, BF16)
            S.op("dve", lambda e, yv=yv, px=px, th=th: e.scalar_tensor_tensor(
                out=yv, in0=px, scalar=pvcol(l, G_DWB, c), in1=yd[:, th * 1024:(th + 1) * 1024], op0=ALU.add, op1=ALU.add),
                r=kpx + kyd + pvs[l][1], w=kyv)

    def phase3(l, seg):
        koff = TS if (seg > 0 or pair) else 0
        nk = koff + TS
        hb = 1 if (seg > 0 or pair) else 0
        scale = HD ** -0.5
        small = Ring(R4, 2048, 12)
        vbase = {}
        nslot = 0
        for d in (1, 4, 16):
            nb = TS // (128 * d)
            for r in range(d):
                vbase[(d, r)] = nslot
                nslot += nb + hb
        assert nslot * 256 <= 24576 and 49152 + 2 * 8192 <= 65536
        def head_bufs(h):
            s = h % 2
            qT, kq = B0.view(s * 28672, TS, BF16)
            kT, kk = B0.view(s * 28672 + 4096, nk, BF16)
            acc, ka = B0.view(s * 28672 + 12288, 2 * TS, F32)
            vb, kvb = WR.view(s * 24576, nslot * 128, BF16)
            return qT, kq, kT, kk, acc, ka, vb, kvb

        def load_head(h):
            qT, kq, kT, kk, acc, ka, vb, kvb = head_bufs(h)
            vb3 = vb.rearrange("p (m f) -> p m f", f=128)
            S.dma("sp", lambda e: e.dma_start(out=qT, in_=QT[h]), r=[("QT", h)], w=kq)
            hs = slice((h % 4) * 128, (h % 4 + 1) * 128)
            if pair:
                S.dma("sp", lambda e: e.dma_start(out=kT[:, 0:TS], in_=XGK[h // 4][hs, :]), r=[("XGK", h // 4)], w=kk)
                S.dma("sp", lambda e: e.dma_start(out=kT[:, TS:2 * TS], in_=XBK[h // 4][hs, :]), r=[("KT", h)], w=kk)
            else:
                S.dma("sp", lambda e: e.dma_start(out=kT, in_=KT[h][:, seg * TS - koff: seg * TS + TS]), r=[("KT", h)], w=kk)
            vT, kvt = WR.view(49152 + (h % 2) * 8192, nk, BF16)
            if pair:
                S.dma("sp", lambda e: e.dma_start(out=vT[:, 0:TS], in_=XGV[h // 4][hs, :]), r=[("XGV", h // 4)], w=kvt)
                S.dma("sp", lambda e: e.dma_start(out=vT[:, TS:2 * TS], in_=XBV[h // 4][hs, :]), r=[("V", h)], w=kvt)
            else:
                S.dma("sp", lambda e: e.dma_start(out=vT, in_=V[h][:, seg * TS - koff: seg * TS + TS]), r=[("V", h)], w=kvt)

        slot_src = []
        for d in (1, 4, 16):
            for r in range(d):
                for m in range(TS // (128 * d) + hb):
                    slot_src.append((koff - hb * 128 * d + r + m * 128 * d, d))
        assert len(slot_src) == nslot
        NG = (nslot + 7) // 8
        tbank = [0]

        def prep_v(h, g):
            qT, kq, kT, kk, acc, ka, vb, kvb = head_bufs(h)
            vT, kvt = WR.view(49152 + (h % 2) * 8192, nk, BF16)
            s0, s1 = 8 * g, min(nslot, 8 * g + 8)
            bk = 2 + tbank[0] % 2
            tbank[0] += 1
            pt, kpt = psv(PS, "PS", bk * 512, 512, BF16)

            def tr(e):
                for k, sl in enumerate(range(s0, s1)):
                    c0, d = slot_src[sl]
                    ins = e.transpose(out=pt[:, k * 128:(k + 1) * 128], in_=vT[:, c0: c0 + 127 * d + 1: d], identity=ident)
                return ins
            S.op("pe", tr, r=kvt + k_ident, w=kpt)
            n = (s1 - s0) * 128
            dstv = vb[:, s0 * 128: s0 * 128 + n]
            kd = [(WR.name, gg) for gg in range(((h % 2) * 24576 + s0 * 256) // GRAN, ((h % 2) * 24576 + s1 * 256 - 1) // GRAN + 1)]
            if g % 2 == 0:
                S.op("act", lambda e: e.activation(out=dstv, in_=pt[:, 0:n], func=AF.Copy), r=kpt, w=kd)
            else:
                S.op("dve", lambda e: e.tensor_copy(out=dstv, in_=pt[:, 0:n]), r=kpt, w=kd)

        pending_norm = []
        load_head(0)
        for g in range(NG):
            prep_v(0, g)
        for h in range(NH):
            if h + 1 < NH:
                load_head(h + 1)
            qT, kq, kT, kk, acc, ka, vb, kvb = head_bufs(h)
            acc3 = acc.rearrange("p (a n) -> p a n", a=2)
            vb3 = vb.rearrange("p (m f) -> p m f", f=128)
            ulist = []
            for d in (1, 4, 16):
                nbo = TS // (128 * d)
                for r in range(d):
                    for n in range(nbo):
                        ulist.append((d, r, n))
            pslot = [0, 0]
            state = {}

            def stage1(u, h=h, qT=qT, kT=kT, kq=kq, kk=kk):
                d, r, n = u
                bq = n * 128 * d + r
                has_prev = (koff + bq - 128 * d) >= 0
                qv = qT[:, bq: bq + 127 * d + 1: d]
                kc = kT[:, koff + bq: koff + bq + 127 * d + 1: d]
                ps, kps = psv(PX, "PX", (pslot[0] % 3) * 512, 256)
                pslot[0] += 1
                c0 = 0 if has_prev else 128

                def mm1(e):
                    if has_prev:
                        kp = kT[:, koff + bq - 128 * d: koff + bq - d + 1: d]
                        e.matmul(ps[:, 0:128], lhsT=kp, rhs=qv, start=True, stop=True)
                    return e.matmul(ps[:, 128:256], lhsT=kc, rhs=qv, start=True, stop=True)
                S.op("pe", mm1, r=kq + kk, w=kps)
                pt, kpt = small.get(256, BF16)
                if pair and n == 0:
                    S.op("act", lambda e: e.activation(out=pt[:, 0:128], in_=ps[:, 0:128], func=AF.Exp, bias=cinfo[:, 1:2], scale=scale),
                         r=kps + k_cinfo, w=kpt)
                    S.op("act", lambda e: e.activation(out=pt[:, 128:256], in_=ps[:, 128:256], func=AF.Exp, scale=scale), r=kps, w=kpt)
                else:
                    S.op("act", lambda e: e.activation(out=pt[:, c0:256], in_=ps[:, c0:256], func=AF.Exp, scale=scale), r=kps, w=kpt)
                S.op("pool", lambda e: e.tensor_tensor(out=pt[:, c0:256], in0=pt[:, c0:256], in1=mask2[:, c0:256], op=ALU.mult),
                     r=kpt + k_mask, w=kpt)
                state[u] = (pt, kpt, has_prev, bq)

            def stage2(u, vb3=vb3, kvb=kvb, acc3=acc3, ka=ka):
                d, r, n = u
                pt, kpt, has_prev, bq = state.pop(u)
                pi_ = pslot[1] % 3
                po, kpo = psv(PX, "PX", 1536, 256) if pi_ == 0 else psv(PS, "PS", (pi_ - 1) * 512, 256)
                pslot[1] += 1
                sl = vbase[(d, r)] + n + hb

                def mm2(e):
                    if has_prev:
                        e.matmul(po[:, 0:128], lhsT=vb3[:, sl - 1, :], rhs=pt[:, 0:128], start=True, stop=False)
                    e.matmul(po[:, 0:128], lhsT=vb3[:, sl, :], rhs=pt[:, 128:256], start=(not has_prev), stop=True)
                    if has_prev:
                        e.matmul(po[:, 128:256], lhsT=ones, rhs=pt[:, 0:128], start=True, stop=False)
                    return e.matmul(po[:, 128:256], lhsT=ones, rhs=pt[:, 128:256], start=(not has_prev), stop=True)
                S.op("pe", mm2, r=kpt + kvb + k_ones, w=kpo)
                po3 = po.rearrange("p (a n) -> p a n", a=2)
                av = acc3[:, :, bq: bq + 127 * d + 1: d]
                if d == 1:
                    S.op("dve", lambda e: e.tensor_copy(out=av, in_=po3), r=kpo, w=ka)
                else:
                    S.op("dve", lambda e: e.tensor_tensor(out=av, in0=po3, in1=av, op=ALU.add), r=kpo + ka, w=ka)
            LA = 3
            gnext = 0
            for idx in range(len(ulist) + LA):
                if idx < len(ulist):
                    stage1(ulist[idx])
                if idx >= LA:
                    stage2(ulist[idx - LA])
                if idx == 5 and pending_norm:
                    pending_norm.pop(0)()
                if h + 1 < NH and idx >= 6 and idx % 4 == 0 and gnext < NG:
                    prep_v(h + 1, gnext)
                    gnext += 1
            while h + 1 < NH and gnext < NG:
                prep_v(h + 1, gnext)
                gnext += 1
            def normalise(acc=acc, ka=ka, h=h):
                S.op("act", lambda e: e.activation(out=acc[:, TS:2 * TS], in_=acc[:, TS:2 * TS], func=AF.Ln), r=ka, w=ka)
                S.op("act", lambda e: e.activation(out=acc[:, TS:2 * TS], in_=acc[:, TS:2 * TS], func=AF.Exp, scale=-1.0), r=ka, w=ka)
                at, kat = r8.get(TS, BF16)
                S.op("dve", lambda e: e.tensor_tensor(out=at, in0=acc[:, 0:TS], in1=acc[:, TS:2 * TS], op=ALU.mult), r=ka, w=kat)
                S.dma("sp", lambda e: e.dma_start(out=ATT[h], in_=at), r=kat, w=[("ATT", h)])
            pending_norm.append(normalise)
        while pending_norm:
            pending_norm.pop(0)()

    def colsum_sq(src_fn, nsrc, pt, ptname, pbase):
        for c in range(nsrc):
            sv, ksv = src_fn(c)
            sq, ksq = r4.get(TS, BF16)
            S.op("act", lambda e, sq=sq, sv=sv: e.activation(out=sq, in_=sv, func=AF.Square), r=ksv, w=ksq)
            pk = [(ptname, b) for b in range(pbase // 512, pbase // 512 + 4)]

            def mm(e, sq=sq, c=c):
                for tb in range(4):
                    ins = e.matmul(pt[:, pbase + tb * 512: pbase + (tb + 1) * 512], lhsT=ones, rhs=sq[:, tb * 512:(tb + 1) * 512],
                                   start=(c == 0), stop=(c == nsrc - 1))
                return ins
            S.op("pe", mm, r=ksq + k_ones, w=pk)

    def gate_finalize(l, src_fn, rstd, krstd, ggrp, sgsrc, aybase):
        for c in range(16):
            sv, ksv = src_fn(c)
            sg, ksg = r4.get(TS, BF16)
            S.dma("sp", lambda e, sg=sg, c=c: e.dma_start(out=sg, in_=sgsrc[c]), r=[(sgsrc.tensor.name, c)], w=ksg)
            t, kt = r8.get(TS, F32)
            S.op("dve", lambda e, t=t, sv=sv, c=c: e.scalar_tensor_tensor(out=t, in0=sv, scalar=pvcol(l, ggrp, c), in1=rstd,
                                                                         op0=ALU.mult, op1=ALU.mult),
                 r=ksv + krstd + pvs[l][1], w=kt)
            o, ko = r4.get(TS, BF16)
            S.op("pool", lambda e, o=o, t=t, sg=sg: e.tensor_tensor(out=o, in0=t, in1=sg, op=ALU.mult), r=kt + ksg, w=ko)
            S.dma("act", lambda e, o=o, c=c: e.dma_start(out=AY[aybase + c], in_=o), r=ko, w=[("AY", aybase + c)])

    def finalize_one(l, c, sv, ksv, rstd, krstd, ggrp, sgsrc, aybase):
        sg, ksg = r4.get(TS, BF16)
        S.dma("sp", lambda e: e.dma_start(out=sg, in_=sgsrc[c]), r=[(sgsrc.tensor.name, c)], w=ksg)
        t, kt = r8.get(TS, F32)
        S.op("dve", lambda e: e.scalar_tensor_tensor(out=t, in0=sv, scalar=pvcol(l, ggrp, c), in1=rstd, op0=ALU.mult, op1=ALU.mult),
             r=ksv + krstd + pvs[l][1], w=kt)
        o, ko = r4.get(TS, BF16)
        S.op("dve", lambda e: e.tensor_tensor(out=o, in0=t, in1=sg, op=ALU.mult), r=kt + ksg, w=ko)
        S.dma("act", lambda e: e.dma_start(out=AY[aybase + c], in_=o), r=ko, w=[("AY", aybase + c)])

    def phase4_conv(l, seg):
        for h in range(NH):
            sv, ksv = r4.get(TS, BF16)
            S.dma("sp", lambda e, sv=sv, h=h: e.dma_start(out=sv, in_=ATT[h]), r=[("ATT", h)], w=ksv)
            sq, ksq = r4.get(TS, BF16)
            S.op("act", lambda e, sq=sq, sv=sv: e.activation(out=sq, in_=sv, func=AF.Square), r=ksv, w=ksq)

            def mm(e, sq=sq, h=h):
                for tb in range(4):
                    ins = e.matmul(PX[:, tb * 512:(tb + 1) * 512], lhsT=ones, rhs=sq[:, tb * 512:(tb + 1) * 512],
                                   start=(h == 0), stop=(h == NH - 1))
                return ins
            S.op("pe", mm, r=ksq + k_ones, w=[("PX", b_) for b_ in range(4)])
        rstd, krstd = WR.view(49152, TS, F32)
        rsqrt_tile(rstd, krstd, PX[:, 0:TS], [("PX", b_) for b_ in range(4)], 1.0 / D)
        prep = conv_prep(l, seg, 0)
        for c in range(NCH):
            nprep = conv_prep(l, seg, c + 1) if c + 1 < NCH else None
            conv_tile(l, seg, c, prep)
            prep = nprep
            sv, ksv = r4.get(TS, BF16)
            S.dma("sp", lambda e, sv=sv, c=c: e.dma_start(out=sv, in_=ATT[c]), r=[("ATT", c)], w=ksv)
            finalize_one(l, c, sv, ksv, rstd, krstd, G_ATTO, SGA, 0)

    def phase5(l):
        def src(c):
            return B0.view(c * 4096, TS, BF16)
        for c in range(NCH):
            sv, ksv = src(c)

            def mm(e, sv=sv, c=c):
                for tb in range(4):
                    ins = e.matmul(PX[:, tb * 512:(tb + 1) * 512], lhsT=ones, rhs=sv[:, tb * 512:(tb + 1) * 512],
                                   start=(c == 0), stop=(c == NCH - 1))
                return ins
            S.op("pe", mm, r=ksv + k_ones, w=[("PX", b) for b in range(4)])
        colsum_sq(src, NCH, PS, "PS", 0)
        mean, kmean = WR.view(57344, TS, F32)
        rstd, krstd = WR.view(49152, TS, F32)
        kpx = [("PX", b) for b in range(4)]
        kps = [("PS", b) for b in range(4)]
        S.op("dve", lambda e: e.tensor_scalar(out=mean, in0=PX[:, 0:TS], scalar1=1.0 / D, scalar2=None, op0=ALU.mult),
             r=kpx, w=kmean)
        msq, kmsq = r8.get(TS, F32)
        S.op("pool", lambda e: e.tensor_tensor(out=msq, in0=mean, in1=mean, op=ALU.mult), r=kmean, w=kmsq)
        S.op("dve", lambda e: e.scalar_tensor_tensor(out=rstd, in0=PS[:, 0:TS], scalar=1.0 / D, in1=msq, op0=ALU.mult,
                                                     op1=ALU.subtract), r=kps + kmsq, w=krstd)
        rsqrt_tile(rstd, krstd, rstd, krstd, 1.0)
        S.op("dve", lambda e: e.tensor_tensor(out=mean, in0=mean, in1=rstd, op=ALU.mult), r=kmean + krstd, w=kmean)
        for th in range(2):
            for c in range(NCH):
                sv, ksv = B0.view(c * 4096 + th * 2048, 1024, BF16)
                t, kt = r4.get(1024, F32)
                S.op("dve", lambda e, t=t, sv=sv, th=th: e.tensor_tensor(out=t, in0=sv, in1=rstd[:, th * 1024:(th + 1) * 1024],
                                                                        op=ALU.mult), r=ksv + krstd, w=kt)
                S.op("dve", lambda e, t=t, th=th: e.tensor_tensor(out=t, in0=t, in1=mean[:, th * 1024:(th + 1) * 1024],
                                                                  op=ALU.subtract), r=kt + kmean, w=kt)
                S.op("act", lambda e, t=t, sv=sv, c=c: e.activation(out=sv, in_=t, func=AF.Silu, bias=pvcol(l, G_LNB, c),
                                                                    scale=pvcol(l, G_LNG, c)), r=kt + pvs[l][1], w=ksv)
        def rhs_y(c, t0, n):
            return B0.view(c * 4096 + t0 * 2, n, BF16)
        pw_loaded = {}
        wslot[0] = 1

        def pw_pre(k):
            for kk in (k, k + 1):
                if kk < 8 and kk not in pw_loaded:
                    pw_loaded[kk] = load_w(w_pw[l], (kk % 4) * 512)
            return pw_loaded[k]

        def pw_mm(i, j, th):
            wb3, kw = pw_pre(th * 4 + i)
            e_ = i * 4 + j
            sgt, ksgt = r2k.get(1024, BF16)
            S.dma("sp", lambda e: e.dma_start(out=sgt, in_=SGC[e_][:, th * 1024:(th + 1) * 1024]), r=[("SGC", e_)], w=ksgt)
            return ws_unit(wb3, kw, j, th, rhs_y, split=(i == 0 and j == 0 and th == 0)) + (sgt, ksgt)

        def pw_epi(st, e_, th):
            pa, kpa, sgt, ksgt = st
            o, ko = r4.get(1024, BF16)
            S.op("dve", lambda e: e.scalar_tensor_tensor(out=o, in0=pa, scalar=pvcol(l, G_CONVO, e_), in1=sgt, op0=ALU.mult,
                                                         op1=ALU.mult), r=kpa + ksgt + pvs[l][1], w=ko)
            S.dma("act", lambda e: e.dma_start(out=AY[16 + e_][:, th * 1024:(th + 1) * 1024], in_=o), r=ko, w=[("AY", 16 + e_)])
            sq, ksq = r4.get(1024, BF16)
            S.op("act", lambda e: e.activation(out=sq, in_=pa, func=AF.Square), r=kpa, w=ksq)

            def mms(e):
                for tb in range(2):
                    o_ = th * 1024 + tb * 512
                    ins = e.matmul(PX[:, o_:o_ + 512], lhsT=ones, rhs=sq[:, tb * 512:(tb + 1) * 512],
                                   start=(e_ == 0), stop=(e_ == NCH - 1))
                return ins
            S.op("pe", mms, r=ksq + k_ones, w=[("PX", th * 2), ("PX", th * 2 + 1)])
        prevu = None
        for th in range(2):
            for i in range(4):
                for j in range(4):
                    st = pw_mm(i, j, th)
                    if prevu is not None:
                        pw_epi(*prevu)
                    prevu = (st, i * 4 + j, th)
        pw_epi(*prevu)
        rsqrt_tile(rstd, krstd, PX[:, 0:TS], kpx, 1.0 / D)
        pr, kpr = psv(PS, "PS", 0, 16)

        def mmt(e):
            for tt in range(16):
                ins = e.matmul(pr[:, tt:tt + 1], lhsT=rstd[0:1, tt * 128:(tt + 1) * 128], rhs=one_f[0:1, 0:1], start=True, stop=True)
            return ins
        S.op("pe", mmt, r=krstd + k_onef, w=kpr)
        S.op("dve", lambda e: e.tensor_copy(out=rc16, in_=pr), r=kpr, w=k_rc16)

    def phase6(l, xsrc, xdst, tok0, hook=None):
        for th in range(2):
            ayv, kay = B0.view(0, 32 * 1024, BF16)
            ay3 = ayv.rearrange("p (c n) -> p c n", c=32)
            for c in range(32):
                v1, k1 = B0.view(c * 2048, 1024, BF16)
                S.dma("sp", lambda e, v1=v1, c=c, th=th: e.dma_start(out=v1, in_=AY[c][:, th * 1024:(th + 1) * 1024]),
                      r=[("AY", c)], w=k1)
            nxt = load_w(w_out[l], 0, nchunks=32, slots=2)
            for cb in range(4):
                wb3, kw = nxt
                if cb + 1 < 4:
                    nxt = load_w(w_out[l], (cb + 1) * 512, nchunks=32, slots=2)
                if hook is not None and th == 0 and cb == 0:
                    hook()
                for tt in range(8):
                    a = acc6[0]
                    acc6[0] = (a + 1) % 2
                    pa, kpa = psv(PS, "PS", a * 1024, 512)
                    pc, kpc = psv(PS, "PS", a * 1024 + 512, 512)
                    t0 = tok0 + th * 1024 + tt * 128
                    ttg = th * 8 + tt
                    xt, kxt = r4.get(512, F32)
                    S.dma("sp", lambda e, xt=xt, t0=t0, cb=cb: e.dma_start(out=xt, in_=xsrc[t0:t0 + 128, cb * 512:(cb + 1) * 512]),
                          w=kxt)

                    def mm(e, pa=pa, pc=pc, tt=tt, wb3=wb3):
                        for c in range(32):
                            ins = e.matmul(pa if c < 16 else pc, lhsT=ay3[:, c, tt * 128:(tt + 1) * 128], rhs=wb3[:, c, :],
                                           start=(c % 16 == 0), stop=(c % 16 == 15))
                        return ins
                    if cb == 0 and tt == 0:
                        for c in range(32):
                            S.op("pe", lambda e, pa=pa, pc=pc, tt=tt, wb3=wb3, c=c: e.matmul(
                                pa if c < 16 else pc, lhsT=ay3[:, c, tt * 128:(tt + 1) * 128], rhs=wb3[:, c, :],
                                start=(c % 16 == 0), stop=(c % 16 == 15)),
                                r=kw + B0.view(c * 2048, 1024, BF16)[1], w=(kpa if c < 16 else kpc))
                    else:
                        S.op("pe", mm, r=kw + kay, w=kpa + kpc)
                    ot, kot = r4.get(512, F32)
                    S.op("dve", lambda e, ot=ot, pc=pc, xt=xt, ttg=ttg: e.scalar_tensor_tensor(
                        out=ot, in0=pc, scalar=rc16[:, ttg:ttg + 1], in1=xt, op0=ALU.mult, op1=ALU.add), r=kpc + kxt + k_rc16, w=kot)
                    S.op("dve", lambda e, ot=ot, pa=pa: e.tensor_tensor(out=ot, in0=pa, in1=ot, op=ALU.add), r=kpa + kot, w=kot)
                    S.dma("act", lambda e, ot=ot, t0=t0, cb=cb: e.dma_start(out=xdst[t0:t0 + 128, cb * 512:(cb + 1) * 512], in_=ot),
                          r=kot, w=[("XD", l, t0 // TS)])

    done = False
    for l in range(DEPTH):
        xsrc = x_in if l == 0 else X1
        xdst = X1 if l == 0 else out
        for seg in range(nseg):
            if nseg > 1 or l == 0:
                rope_tables(seg)
            if l == 0 or nseg > 1:
                build_gb(l)
            phase1(l, xsrc, seg * TS)
            if stop_after == "p1":
                done = True
                break
            phase2(l, seg)
            if stop_after == "p2":
                done = True
                break
            if pair:
                S.cc(lambda e: e.collective_compute("AllGather", ALU.bypass, replica_groups=RGS, ins=[XBH[:, :]], outs=[XGH[:, :]]),
                     r=[("XBH", c) for c in range(NCH)], w=[("XGH",)])
            phase3(l, seg)
            if stop_after == "p3":
                done = True
                break
            phase4_conv(l, seg)
            if debug:
                for c in range(NCH):
                    S.dma("sp", lambda e, c=c: e.dma_start(out=Y[c], in_=B0.view(c * 4096, TS, BF16)[0]),
                          r=B0.view(c * 4096, TS, BF16)[1], w=[("Y", c)])
            phase5(l)
            if stop_after == "p5":
                done = True
                break
            hook = (lambda l=l: build_gb(l + 1)) if (l + 1 < DEPTH and seg == nseg - 1) else None
            phase6(l, xsrc, xdst, seg * TS, hook)
        if done or stop_after == "l0":
            break
    S.emit()
    return nc


def host_consts():
    c = np.zeros((128, 512), np.float32)
    idx = np.arange(128)
    c[idx, idx] = 1.0
    c[(idx + 64) % 128, 128 + idx] = 1.0
    kk = idx[:, None]
    qq = idx[None, :]
    c[:, 256:384] = (kk >= qq).astype(np.float32)
    c[:, 384:512] = (kk <= qq).astype(np.float32)
    return c


def pack_params(norm_g, att_out_g, conv_out_g, dw_bias, conv_ln_g, conv_ln_b, q_norm_g, k_norm_g, dw_kernel):
    pv = np.zeros((DEPTH, 128, NPV), np.float32)
    for l in range(DEPTH):
        for gi, a in enumerate((norm_g, att_out_g, conv_out_g, dw_bias, conv_ln_g, conv_ln_b)):
            pv[l, :, gi * 16:(gi + 1) * 16] = a[l].reshape(16, 128).T
        pv[l, :, 96] = q_norm_g[l]
        pv[l, :, 97] = k_norm_g[l]
        pv[l, :, 98:] = dw_kernel[l].reshape(CW, 16, 128).transpose(2, 1, 0).reshape(128, 16 * CW)
    return pv


def kernel(x, norm_g, w_in, q_norm_g, k_norm_g, dw_kernel, dw_bias, conv_ln_g, conv_ln_b, w_pw, att_out_g, conv_out_g, w_out):
    x = np.asarray(x, np.float32)
    B, SEQ = x.shape[0], x.shape[1]
    assert SEQ == 2 * TS and B == 4
    pv = pack_params(*[np.asarray(a, np.float32) for a in (norm_g, att_out_g, conv_out_g, dw_bias, conv_ln_g, conv_ln_b,
                                                             q_norm_g, k_norm_g, dw_kernel)])
    cst = host_consts()
    w_in = np.ascontiguousarray(w_in, np.float32)
    w_pw = np.ascontiguousarray(w_pw, np.float32)
    w_out = np.ascontiguousarray(w_out, np.float32)
    nc = build(1, pair=True)
    in_maps = []
    for core in range(8):
        b, half = core // 2, core % 2
        ci = np.zeros((128, 4), np.float32)
        ci[:, 0] = half * TS
        ci[:, 1] = 0.0 if half == 1 else -30000.0
        ci[:, 2] = float(half)
        in_maps.append({"x": np.ascontiguousarray(x[b, half * TS:(half + 1) * TS]), "w_in": w_in, "w_pw": w_pw, "w_out": w_out,
                        "pv": pv, "cst": cst, "cinfo": ci})
    res = run_bass_kernel_spmd(nc, in_maps, core_ids=list(range(8)))
    out = np.zeros((B, SEQ, D), np.float32)
    for core in range(8):
        b, half = core // 2, core % 2
        out[b, half * TS:(half + 1) * TS] = np.asarray(res.results[core]["out"]).reshape(TS, D)
    return out
```
